# Optimizing a Trainium2 kernel written in Bass

```python
import math
import jax
import jax.numpy as jnp
from jax import lax
import numpy as np

D_MODEL = 4096
BATCH = 1
SEQ = 16384
DEPTH = 2

HEAD_DIM = 128
BLOCK = 128
SUPER_BLOCK = 1024
EPS = 1e-6
A_HEADS = D_MODEL // 512
A_KV_HEADS = max(A_HEADS // 4, 1)
A_GROUP = A_HEADS // A_KV_HEADS
A_WIDTH = A_HEADS * HEAD_DIM
A_KV_WIDTH = A_KV_HEADS * HEAD_DIM
WINDOW = 128
NUM_BUCKETS = 32
MAX_DISTANCE = 128
B_HEADS = D_MODEL // 1024
B_Q_LORA = D_MODEL // 4
B_KV_LORA = 512
B_NOPE_DIM = 128
B_ROPE_DIM = 64
B_V_DIM = 128
B_QK_DIM = B_NOPE_DIM + B_ROPE_DIM
B_WIDTH = B_HEADS * B_V_DIM
ROPE_THETA = 10000.0
C_HEADS = D_MODEL // 1024
C_WIDTH = C_HEADS * HEAD_DIM
IN_SIZES = (A_WIDTH, A_KV_WIDTH, A_KV_WIDTH, A_WIDTH,
            B_Q_LORA, B_KV_LORA, B_ROPE_DIM, B_WIDTH,
            C_WIDTH, C_WIDTH, C_WIDTH, C_WIDTH,
            D_MODEL, D_MODEL, D_MODEL)
IN_COLS = sum(IN_SIZES)

kernel_name = "hybrid_swa_mla_stickbreaking_block"


def rms_norm(x, g):
    xf = x.astype(jnp.float32)
    y = xf * lax.rsqrt(jnp.mean(xf * xf, axis=-1, keepdims=True) + EPS)
    return (y * g.astype(jnp.float32)).astype(x.dtype)


def t5_bucket(rel):
    n = jnp.maximum(rel, 0)
    max_exact = NUM_BUCKETS // 2
    logn = jnp.log(jnp.maximum(n, 1).astype(jnp.float32) / max_exact)
    large = max_exact + (logn / math.log(MAX_DISTANCE / max_exact) * (NUM_BUCKETS - max_exact)).astype(jnp.int32)
    large = jnp.minimum(large, NUM_BUCKETS - 1)
    return jnp.where(n < max_exact, n, large)


def swa_bias_and_mask(rel_bias, n_blocks):
    t = jnp.arange(BLOCK)[:, None]
    s = jnp.arange(2 * BLOCK)[None, :]
    rel = BLOCK + t - s
    bias = rel_bias[t5_bucket(rel)]
    bias = bias.reshape(BLOCK, 2 * BLOCK, A_KV_HEADS, A_GROUP).transpose(2, 3, 0, 1)
    key_abs = jnp.arange(n_blocks)[:, None, None] * BLOCK + s[None] - BLOCK
    mask = (rel >= 0)[None] & (rel < WINDOW)[None] & (key_abs >= 0)
    return bias, mask


def swa_sink_attention(q, k, v, sinks, bias, mask):
    B, S, HQ, D = q.shape
    nb = S // BLOCK
    qb = q.reshape(B, nb, BLOCK, A_KV_HEADS, A_GROUP, D)

    def band(t):
        tb = t.reshape(B, nb, BLOCK, A_KV_HEADS, D)
        prev = jnp.concatenate([jnp.zeros_like(tb[:, :1]), tb[:, :-1]], axis=1)
        return jnp.concatenate([prev, tb], axis=2)

    kk, vv = band(k), band(v)
    s = jnp.einsum('bnqhgd,bnkhd->bnhgqk', qb, kk).astype(jnp.float32) * (D ** -0.5)
    s = s + bias.astype(jnp.float32)
    s = jnp.where(mask[None, :, None, None], s, -jnp.inf)
    sink = sinks.astype(jnp.float32).reshape(A_KV_HEADS, A_GROUP)[:, :, None, None]
    sink = jnp.broadcast_to(sink, s.shape[:-1] + (1,))
    p = jax.nn.softmax(jnp.concatenate([s, sink], axis=-1), axis=-1)[..., :-1]
    out = jnp.einsum('bnhgqk,bnkhd->bnqhgd', p.astype(v.dtype), vv)
    return out.reshape(B, S, HQ, D)


def rope_tables(seq):
    pos = jnp.arange(seq, dtype=jnp.float32)
    inv = ROPE_THETA ** (-jnp.arange(0, B_ROPE_DIM, 2, dtype=jnp.float32) / B_ROPE_DIM)
    ang = pos[:, None] * inv[None, :]
    return jnp.cos(ang), jnp.sin(ang)


def apply_rope(x, cos, sin):
    half = x.shape[-1] // 2
    x1, x2 = x[..., :half], x[..., half:]
    c, s = cos.astype(x.dtype), sin.astype(x.dtype)
    return jnp.concatenate([x1 * c - x2 * s, x2 * c + x1 * s], axis=-1)


def to_blocks(t, start, end):
    B = t.shape[0]
    n = (end - start) // BLOCK
    return t[:, start:end].reshape(B, n, BLOCK, *t.shape[2:]).transpose(1, 0, 2, 3, 4)


def from_blocks(t):
    n, B = t.shape[0], t.shape[1]
    return t.transpose(1, 0, 2, 3, 4).reshape(B, n * BLOCK, *t.shape[3:])


def mla_attention(q_nope, q_rope, k_nope, k_rope, v):
    S = q_nope.shape[1]
    scale = B_QK_DIM ** -0.5
    outs = []
    for start in range(0, S, SUPER_BLOCK):
        end = min(start + SUPER_BLOCK, S)
        kn, kr, vv = k_nope[:, :end], k_rope[:, :end], v[:, :end]
        key_pos = jnp.arange(end)
        n = (end - start) // BLOCK

        def one_block(args, kn=kn, kr=kr, vv=vv, key_pos=key_pos, start=start):
            i, qn_b, qr_b = args
            s = (jnp.einsum('bqhd,bkhd->bhqk', qn_b, kn)
                 + jnp.einsum('bqhd,bkd->bhqk', qr_b, kr)).astype(jnp.float32) * scale
            q_pos = start + i * BLOCK + jnp.arange(BLOCK)
            s = jnp.where(key_pos[None, :] <= q_pos[:, None], s, -jnp.inf)
            p = jax.nn.softmax(s, axis=-1)
            return jnp.einsum('bhqk,bkhd->bqhd', p.astype(vv.dtype), vv)

        out = lax.map(one_block, (jnp.arange(n), to_blocks(q_nope, start, end), to_blocks(q_rope, start, end)))
        outs.append(from_blocks(out))
    return jnp.concatenate(outs, axis=1)


def stick_breaking_attention(q, k, v):
    B, S, H, D = q.shape
    scale = D ** -0.5
    outs = []
    for start in range(0, S, SUPER_BLOCK):
        end = min(start + SUPER_BLOCK, S)
        kk, vv = k[:, :end], v[:, :end]
        key_pos = jnp.arange(end)
        n = (end - start) // BLOCK
        n_kb = end // BLOCK

        def one_block(args, kk=kk, vv=vv, key_pos=key_pos, start=start, n_kb=n_kb):
            i, q_blk = args
            z = jnp.einsum('bqhd,bkhd->bhqk', q_blk, kk).astype(jnp.float32) * scale
            q_pos = start + i * BLOCK + jnp.arange(BLOCK)
            past = key_pos[None, :] < q_pos[:, None]
            log_keep = jnp.where(past, jax.nn.log_sigmoid(-z), 0.0)
            lk = log_keep.reshape(z.shape[:-1] + (n_kb, BLOCK))
            within = lax.cumsum(lk, axis=lk.ndim - 1, reverse=True) - lk
            tot = jnp.sum(lk, axis=-1)
            offset = lax.cumsum(tot, axis=tot.ndim - 1, reverse=True) - tot
            log_between = (within + offset[..., None]).reshape(z.shape)
            w = jnp.where(past, jnp.exp(jax.nn.log_sigmoid(z) + log_between), 0.0)
            return jnp.einsum('bhqk,bkhd->bqhd', w.astype(vv.dtype), vv)

        out = lax.map(one_block, (jnp.arange(n), to_blocks(q, start, end)))
        outs.append(from_blocks(out))
    return jnp.concatenate(outs, axis=1)


def setup_inputs(seed: int = 0) -> dict:
    key = jax.random.key(seed)
    ks = jax.random.split(key, 16)
    f32 = jnp.float32
    nrm = lambda k, shape, scale: jax.random.normal(k, shape, f32) * scale
    gain = lambda k, shape: 1.0 + 0.02 * jax.random.normal(k, shape, f32)
    return {
        "x": nrm(ks[0], (BATCH, SEQ, D_MODEL), 1.0),
        "norm_g": gain(ks[1], (DEPTH, D_MODEL)),
        "w_in": nrm(ks[2], (DEPTH, D_MODEL, IN_COLS), D_MODEL ** -0.5),
        "attn_sinks": nrm(ks[3], (DEPTH, A_HEADS), 0.5),
        "rel_bias": nrm(ks[4], (NUM_BUCKETS, A_HEADS), 0.5),
        "g_q_lora": gain(ks[5], (DEPTH, B_Q_LORA)),
        "w_q_up": nrm(ks[6], (DEPTH, B_Q_LORA, B_HEADS * B_QK_DIM), B_Q_LORA ** -0.5),
        "g_kv_lora": gain(ks[7], (DEPTH, B_KV_LORA)),
        "w_kv_up": nrm(ks[8], (DEPTH, B_KV_LORA, B_HEADS * (B_NOPE_DIM + B_V_DIM)), B_KV_LORA ** -0.5),
        "w_proj_a": nrm(ks[9], (DEPTH, A_WIDTH, D_MODEL), A_WIDTH ** -0.5),
        "w_proj_b": nrm(ks[10], (DEPTH, B_WIDTH, D_MODEL), B_WIDTH ** -0.5),
        "w_proj_c": nrm(ks[11], (DEPTH, C_WIDTH, D_MODEL), C_WIDTH ** -0.5),
        "w_out": nrm(ks[12], (DEPTH, D_MODEL, D_MODEL), D_MODEL ** -0.5),
        "final_g": gain(ks[13], (D_MODEL,)),
    }


def reference(x, norm_g, w_in, attn_sinks, rel_bias, g_q_lora, w_q_up, g_kv_lora, w_kv_up,
              w_proj_a, w_proj_b, w_proj_c, w_out, final_g):
    B, S, _ = x.shape
    nb = S // BLOCK
    split_points = np.cumsum(IN_SIZES)[:-1].tolist()
    swa_bias, swa_mask = swa_bias_and_mask(rel_bias, nb)
    cos, sin = rope_tables(S)

    for l in range(DEPTH):
        h = rms_norm(x, norm_g[l])
        p = jnp.einsum('bsd,de->bse', h, w_in[l])
        (qa, ka, va, za, cq, ckv, kr, zb, qc, kc, vc, zc, ga, gb, gc) = jnp.split(p, split_points, axis=-1)

        ya = swa_sink_attention(qa.reshape(B, S, A_HEADS, HEAD_DIM),
                                ka.reshape(B, S, A_KV_HEADS, HEAD_DIM),
                                va.reshape(B, S, A_KV_HEADS, HEAD_DIM),
                                attn_sinks[l], swa_bias, swa_mask)
        ya = ya.reshape(B, S, A_WIDTH) * jax.nn.silu(za)

        q = (rms_norm(cq, g_q_lora[l]) @ w_q_up[l]).reshape(B, S, B_HEADS, B_QK_DIM)
        q_nope = q[..., :B_NOPE_DIM]
        q_rope = apply_rope(q[..., B_NOPE_DIM:], cos[:, None, :], sin[:, None, :])
        kv = (rms_norm(ckv, g_kv_lora[l]) @ w_kv_up[l]).reshape(B, S, B_HEADS, B_NOPE_DIM + B_V_DIM)
        k_nope, vb = kv[..., :B_NOPE_DIM], kv[..., B_NOPE_DIM:]
        k_rope = apply_rope(kr, cos, sin)
        yb = mla_attention(q_nope, q_rope, k_nope, k_rope, vb)
        yb = yb.reshape(B, S, B_WIDTH) * jax.nn.silu(zb)

        yc = stick_breaking_attention(qc.reshape(B, S, C_HEADS, HEAD_DIM),
                                      kc.reshape(B, S, C_HEADS, HEAD_DIM),
                                      vc.reshape(B, S, C_HEADS, HEAD_DIM))
        yc = yc.reshape(B, S, C_WIDTH) * jax.nn.silu(zc)

        merged = (jax.nn.sigmoid(ga) * (ya @ w_proj_a[l])
                  + jax.nn.sigmoid(gb) * (yb @ w_proj_b[l])
                  + jax.nn.sigmoid(gc) * (yc @ w_proj_c[l]))
        x = x + merged @ w_out[l]

    return rms_norm(x, final_g)
```

```python
import numpy as np
import ml_dtypes
from contextlib import ExitStack
import concourse.bass as bass
import concourse.mybir as mybir
from concourse.bass_utils import run_bass_kernel_spmd

F32 = mybir.dt.float32
BF16 = mybir.dt.bfloat16
AF = mybir.ActivationFunctionType
ALU = mybir.AluOpType
AX = mybir.AxisListType
NEG = -30000.0
EPS = 1e-6
NCORES = 8
PAD = 7


class Buf:
    __slots__ = ("w", "r")

    def __init__(self):
        self.w = None
        self.r = {}


class T:
    def __init__(self, t, b=None, is_ps=False):
        self.t = t
        self.b = b or Buf()
        self.is_ps = is_ps


class Prog:
    KD = 6

    def __init__(self, nc, es):
        self.nc = nc
        self.es = es
        self.eng = {"pe": nc.tensor, "act": nc.scalar, "dve": nc.vector, "pool": nc.gpsimd, "sp": nc.sync}
        self.ops = {k: [] for k in self.eng}
        self.sems = {}
        self.cnt = {}
        self.seen = {k: {} for k in self.eng}
        for k in ("pe", "act", "dve", "pool"):
            self.sems[k] = es.enter_context(nc.semaphore("c_" + k))
            self.cnt[k] = 0
        self.ndma = {k: 0 for k in self.eng}
        for q in ("sp", "pool", "act"):
            for s in range(self.KD):
                key = ("d", q, s)
                self.sems[key] = es.enter_context(nc.semaphore("d_%s_%d" % (q, s)))
                self.cnt[key] = 0
        self.nt = 0
        self.nops = 0
        self.limit = None

    def _skip(self):
        self.nops += 1
        return self.limit is not None and self.nops > self.limit

    def sb(self, shape, dt, name=None):
        self.nt += 1
        t = self.es.enter_context(self.nc.sbuf_tensor("%s_%d" % (name or "t", self.nt), list(shape), dt))
        return T(t)

    def psum(self):
        self.nt += 1
        t = self.es.enter_context(self.nc.psum_tensor("ps_%d" % self.nt, [128, 512], F32))
        return T(t, is_ps=True)

    def dram(self, name, shape, dt, kind):
        return T(self.nc.dram_tensor(name, list(shape), dt, kind=kind).ap())

    def _deps(self, e, reads, writes, skip_self=False):
        need = {}

        def add(k, v):
            if skip_self and k == e:
                return
            if need.get(k, 0) < v:
                need[k] = v

        for b in reads:
            if b.w is not None:
                add(*b.w)
        for b in writes:
            if b.w is not None:
                add(*b.w)
            for k, v in b.r.items():
                add(k, v)
        out = []
        seen = self.seen[e]
        for k, v in need.items():
            if seen.get(k, 0) < v:
                seen[k] = v
                out.append((self.sems[k], v))
        return out

    def op(self, e, fn, reads=(), writes=(), inc=True):
        if self._skip():
            return
        writes = [x.b for x in writes] + [x.b for x in reads if x.is_ps and e != "pe"]
        reads = [x.b for x in reads if not (x.is_ps and e != "pe")]
        waits = self._deps(e, reads, writes, skip_self=(e == "pe"))
        if inc:
            self.cnt[e] += 1
            tok = (e, self.cnt[e])
        else:
            tok = (e, self.cnt[e] + 1)
        sem = self.sems[e]

        engobj = self.eng[e]
        for s, v in waits:
            engobj.wait_ge(s, v)
        ins = fn(engobj)
        if inc:
            ins.then_inc(sem, 1)
        for b in writes:
            b.w = tok
            b.r = {}
        for b in reads:
            if b.r.get(tok[0], 0) < tok[1]:
                b.r[tok[0]] = tok[1]

    def dma(self, q, out, in_, reads=(), writes=()):
        if self._skip():
            return
        reads = [x.b for x in reads]
        writes = [x.b for x in writes]
        n = self.ndma[q]
        self.ndma[q] += 1
        key = ("d", q, n % self.KD)
        waits = self._deps(q, reads, writes)
        prev = self.cnt[key]
        seen = self.seen[q]
        if prev > 0 and seen.get(key, 0) < prev:
            seen[key] = prev
            waits.append((self.sems[key], prev))
        self.cnt[key] += 16
        tok = (key, self.cnt[key])
        sem = self.sems[key]

        engobj = self.eng[q]
        for s, v in waits:
            engobj.wait_ge(s, v)
        engobj.dma_start(out=out, in_=in_).then_inc(sem, 16)
        for b in writes:
            b.w = tok
            b.r = {}
        for b in reads:
            if b.r.get(key, 0) < tok[1]:
                b.r[key] = tok[1]

    def alias(self, new, old):
        toks = {}
        for x in old:
            b = x.b
            if b.w is not None and toks.get(b.w[0], 0) < b.w[1]:
                toks[b.w[0]] = b.w[1]
            for k, v in b.r.items():
                if toks.get(k, 0) < v:
                    toks[k] = v
        for x in new:
            x.b.w = None
            x.b.r = dict(toks)

    def finish(self):
        fin = []
        for key, v in self.cnt.items():
            if isinstance(key, tuple) and v > 0:
                fin.append((self.sems[key], v))
        for k in ("pe", "act", "dve", "pool"):
            if self.cnt[k] > 0:
                fin.append((self.sems[k], self.cnt[k]))

        for s, v in fin:
            self.nc.sync.wait_ge(s, v)


class Ctx:
    pass


def mm_group(p, ps, pairs, n0=0, n1=512, m=128, reads=(), fp32=False):
    n = len(pairs)
    for k, (l, r) in enumerate(pairs):
        p.op("pe", (lambda e, l=l, r=r, k=k: e.matmul(ps.t[0:m, n0:n1], l, r, start=(k == 0), stop=(k == n - 1))),
             reads=reads, writes=[ps], inc=(k == n - 1))


def emit_rms_hT(p, c, x_dram, row0, gcol, hT, col0):
    D, KC = c.D, c.KC
    p.dma("sp", c.xin.t[:], x_dram.t[row0:row0 + 128, :], reads=[x_dram], writes=[c.xin])
    p.op("act", lambda e: e.activation(out=c.xs.t[:], in_=c.xin.t[:], func=AF.Square, accum_out=c.ss.t[:, 0:1]),
         reads=[c.xin], writes=[c.xs, c.ss])
    p.op("act", lambda e: e.activation(out=c.rs.t[:, 0:1], in_=c.ss.t[:, 0:1], func=AF.Sqrt, scale=1.0 / D, bias=c.epsc.t[:, 0:1]),
         reads=[c.ss, c.epsc], writes=[c.rs])
    p.op("dve", lambda e: e.reciprocal(out=c.rs.t[:, 0:1], in_=c.rs.t[:, 0:1]), reads=[c.rs], writes=[c.rs])
    p.op("act", lambda e: e.activation(out=c.xs.t[:], in_=c.xin.t[:], func=AF.Copy, scale=c.rs.t[:, 0:1]),
         reads=[c.xin, c.rs], writes=[c.xs])
    for g4 in range(0, KC, 4):
        ps = c.ps_tr[(g4 // 4) % 2]
        nk = min(4, KC - g4)
        for k in range(nk):
            ch = g4 + k
            p.op("pe", (lambda e, ch=ch, k=k, ps=ps: e.transpose(ps.t[:, k * 128:(k + 1) * 128],
                                                                c.xs.t[:, ch * 128:(ch + 1) * 128], c.ident.t[:])),
                 reads=[c.xs, c.ident], writes=[ps], inc=(k == nk - 1))
        for k in range(nk):
            ch = g4 + k
            if k % 2 == 0:
                p.op("dve", (lambda e, ch=ch, k=k, ps=ps: e.tensor_scalar(
                    out=hT.t[:, ch, col0:col0 + 128], in0=ps.t[:, k * 128:(k + 1) * 128],
                    scalar1=gcol.t[:, ch:ch + 1], scalar2=None, op0=ALU.mult)), reads=[ps, gcol], writes=[hT])
            else:
                p.op("act", (lambda e, ch=ch, k=k, ps=ps: e.activation(
                    out=hT.t[:, ch, col0:col0 + 128], in_=ps.t[:, k * 128:(k + 1) * 128],
                    func=AF.Copy, scale=gcol.t[:, ch:ch + 1])), reads=[ps, gcol], writes=[hT])


def load_const(p, t, src, q="sp"):
    p.dma(q, t.t[:], src.t, reads=[src], writes=[t])


def build_A(D, TPC, limit=None):
    KC = D // 128
    NT = TPC * 512
    nc = bass.Bass("TRN2", target_bir_lowering=False)
    es = ExitStack()
    p = Prog(nc, es)
    p.limit = limit
    c = Ctx()
    c.D, c.KC = D, KC
    I = lambda n, s, dt=F32: p.dram(n, s, dt, "ExternalInput")
    O = lambda n, s, dt=BF16: p.dram(n, s, dt, "ExternalOutput")
    x = I("x", [NT, D])
    gcol_d = I("gcol", [128, KC])
    wF_d = I("wF", [11, 128, KC * 128])
    wva_d = I("wva", [128, KC * 256])
    wvc_d = I("wvc", [128, KC * 512])
    gkv_d = I("gkv", [128, 4])
    wkvF_d = I("wkvF", [128, 4 * 4 * 128])
    wkvT_d = I("wkvT", [128, 4 * 512])
    cos_d = I("cosT", [64, NT])
    sin_d = I("sinT", [64, NT])
    ident_d = I("ident", [128, 128])
    kaT_o = O("kaT", [2 * 128, NT])
    kcT_o = O("kcT", [4 * 128, NT])
    knT_o = O("knT", [4 * 128, NT])
    krT_o = O("krT", [64, NT])
    va_o = O("va", [NT, 256])
    vc_o = O("vc", [NT, 512])
    vb_o = O("vb", [NT, 512])

    ps = [p.psum() for _ in range(8)]
    c.ps_tr = [ps[5], ps[6]]
    c.xin = p.sb([128, D], F32, "xin")
    c.xs = p.sb([128, D], F32, "xs")
    c.ss = p.sb([128, 1], F32, "ss")
    c.rs = p.sb([128, 1], F32, "rs")
    c.ident = p.sb([128, 128], F32, "ident")
    c.epsc = p.sb([128, 1], F32, "epsc")
    p.op("pool", lambda e: e.memset(c.epsc.t[:], EPS), writes=[c.epsc])
    gcol = p.sb([128, KC], F32, "gcol")
    gkv = p.sb([128, 4], F32, "gkv")
    ones = p.sb([128, 128], F32, "ones")
    wva = p.sb([128, KC, 256], BF16, "wva")
    wvc = p.sb([128, KC, 512], BF16, "wvc")
    wkvF = p.sb([128, 4, 4, 128], BF16, "wkvF")
    wkvT = p.sb([128, 4, 512], BF16, "wkvT")
    hT = p.sb([128, KC, 512], BF16, "hT")
    wFs = [p.sb([128, KC, 128], BF16, "wFs") for _ in range(2)]
    ckvT = p.sb([128, 4, 512], F32, "ckvT")
    sq = [p.sb([128, 512], F32, "sq") for _ in range(2)]
    rstdb = p.sb([128, 512], F32, "rstdb")
    ckvn = p.sb([128, 4, 512], BF16, "ckvn")
    cosS = p.sb([64, 512], F32, "cos")
    sinS = p.sb([64, 512], F32, "sin")
    t1 = p.sb([64, 512], F32, "t1")
    t2 = p.sb([64, 512], F32, "t2")
    outK = [p.sb([128, 512], BF16, "outK") for _ in range(2)]
    outV = [p.sb([128, 512], BF16, "outV") for _ in range(2)]
    nK = [0]
    nV = [0]

    load_const(p, c.ident, ident_d)
    load_const(p, gcol, gcol_d)
    load_const(p, gkv, gkv_d)
    p.op("pool", lambda e: e.memset(ones.t[:], 1.0), writes=[ones])
    p.dma("pool", wva.t[:], wva_d.t.rearrange("p (c n) -> p c n", c=KC), reads=[wva_d], writes=[wva])
    p.dma("pool", wvc.t[:], wvc_d.t.rearrange("p (c n) -> p c n", c=KC), reads=[wvc_d], writes=[wvc])
    p.dma("pool", wkvF.t[:], wkvF_d.t.rearrange("p (h c n) -> p h c n", h=4, c=4), reads=[wkvF_d], writes=[wkvF])
    p.dma("pool", wkvT.t[:], wkvT_d.t.rearrange("p (c n) -> p c n", c=4), reads=[wkvT_d], writes=[wkvT])

    def evac_store(src_ps, m, out_ap, eng):
        o = outK[nK[0] % 2]
        nK[0] += 1
        if eng == "act":
            p.op("act", lambda e: e.activation(out=o.t[0:m, :], in_=src_ps.t[0:m, :], func=AF.Copy),
                 reads=[src_ps], writes=[o])
        else:
            p.op("dve", lambda e: e.tensor_copy(out=o.t[0:m, :], in_=src_ps.t[0:m, :]), reads=[src_ps], writes=[o])
        p.dma("sp", out_ap, o.t[0:m, :], reads=[o], writes=[])

    for j in range(TPC):
        t0 = j * 512
        for i in range(4):
            emit_rms_hT(p, c, x, t0 + i * 128, gcol, hT, i * 128)
        p.dma("sp", cosS.t[:], cos_d.t[:, t0:t0 + 512], reads=[cos_d], writes=[cosS])
        p.dma("sp", sinS.t[:], sin_d.t[:, t0:t0 + 512], reads=[sin_d], writes=[sinS])
        for b in range(11):
            w = wFs[b % 2]
            p.dma("pool", w.t[:], wF_d.t[b].rearrange("p (c n) -> p c n", c=KC), reads=[wF_d], writes=[w])
            if b < 10:
                bank = ps[b % 2]
                mm_group(p, bank, [(w.t[:, ch, :], hT.t[:, ch, :]) for ch in range(KC)], reads=[w, hT])
                if b < 2:
                    evac_store(bank, 128, kaT_o.t[b * 128:(b + 1) * 128, t0:t0 + 512], "act")
                elif b < 6:
                    evac_store(bank, 128, kcT_o.t[(b - 2) * 128:(b - 1) * 128, t0:t0 + 512], "dve")
                else:
                    c4 = b - 6
                    s = sq[c4 % 2]
                    p.op("dve", lambda e, c4=c4, bank=bank: e.tensor_copy(out=ckvT.t[:, c4, :], in_=bank.t[:, :]),
                         reads=[bank], writes=[ckvT])
                    p.op("act", lambda e, s=s, bank=bank: e.activation(out=s.t[:], in_=bank.t[:, :], func=AF.Square),
                         reads=[bank], writes=[s])
                    p.op("pe", lambda e, s=s, c4=c4: e.matmul(ps[2].t[:, :], ones.t[:], s.t[:], start=(c4 == 0),
                                                              stop=(c4 == 3)),
                         reads=[ones, s], writes=[ps[2]], inc=True)
            else:
                mm_group(p, ps[3], [(w.t[:, ch, 0:64], hT.t[:, ch, :]) for ch in range(KC)], m=64, reads=[w, hT])
                mm_group(p, ps[4], [(w.t[:, ch, 64:128], hT.t[:, ch, :]) for ch in range(KC)], m=64, reads=[w, hT])
                p.op("dve", lambda e: e.tensor_tensor(out=t1.t[:], in0=ps[3].t[0:64, :], in1=cosS.t[:], op=ALU.mult),
                     reads=[ps[3], cosS], writes=[t1])
                p.op("dve", lambda e: e.tensor_tensor(out=t2.t[:], in0=ps[4].t[0:64, :], in1=sinS.t[:], op=ALU.mult),
                     reads=[ps[4], sinS], writes=[t2])
                o = outK[nK[0] % 2]
                nK[0] += 1
                p.op("pool", lambda e, o=o: e.tensor_tensor(out=o.t[0:64, :], in0=t1.t[:], in1=t2.t[:], op=ALU.add),
                     reads=[t1, t2], writes=[o])
                p.dma("sp", krT_o.t[:, t0:t0 + 512], o.t[0:64, :], reads=[o], writes=[])
        p.op("act", lambda e: e.activation(out=rstdb.t[:], in_=ps[2].t[:, :], func=AF.Sqrt, scale=1.0 / 512, bias=c.epsc.t[:, 0:1]),
             reads=[ps[2], c.epsc], writes=[rstdb])
        p.op("dve", lambda e: e.reciprocal(out=rstdb.t[:], in_=rstdb.t[:]), reads=[rstdb], writes=[rstdb])
        for c4 in range(4):
            p.op("dve", lambda e, c4=c4: e.scalar_tensor_tensor(out=ckvn.t[:, c4, :], in0=ckvT.t[:, c4, :],
                                                                 scalar=gkv.t[:, c4:c4 + 1], in1=rstdb.t[:],
                                                                 op0=ALU.mult, op1=ALU.mult),
                 reads=[ckvT, gkv, rstdb], writes=[ckvn])
        for h in range(4):
            bank = ps[h % 2]
            mm_group(p, bank, [(wkvF.t[:, h, ch, :], ckvn.t[:, ch, :]) for ch in range(4)], reads=[wkvF, ckvn])
            evac_store(bank, 128, knT_o.t[h * 128:(h + 1) * 128, t0:t0 + 512], "act" if h % 2 else "dve")
        for i in range(4):
            r0 = t0 + i * 128
            for (wt, n, dst, kc_, src) in ((wva, 256, va_o, KC, hT), (wvc, 512, vc_o, KC, hT), (wkvT, 512, vb_o, 4, ckvn)):
                mm_group(p, ps[7], [(src.t[:, ch, i * 128:(i + 1) * 128], wt.t[:, ch, :]) for ch in range(kc_)],
                         n1=n, reads=[wt, src])
                o = outV[nV[0] % 2]
                nV[0] += 1
                p.op("act" if nV[0] % 2 else "dve",
                     (lambda e, o=o, n=n: e.activation(out=o.t[:, 0:n], in_=ps[7].t[:, 0:n], func=AF.Copy))
                     if nV[0] % 2 else
                     (lambda e, o=o, n=n: e.tensor_copy(out=o.t[:, 0:n], in_=ps[7].t[:, 0:n])),
                     reads=[ps[7]], writes=[o])
                p.dma("sp", dst.t[r0:r0 + 128, :], o.t[:, 0:n], reads=[o], writes=[])
    p.finish()
    es.close()
    nc._nops = p.nops
    return nc


def f_layout(w):
    Din, N = w.shape
    KC, nb = Din // 128, N // 128
    return np.ascontiguousarray(w.reshape(KC, 128, nb, 128).transpose(2, 1, 0, 3).reshape(nb, 128, KC * 128))


def t_layout(w):
    Din, N = w.shape
    KC = Din // 128
    return np.ascontiguousarray(w.reshape(KC, 128, N).transpose(1, 0, 2).reshape(128, KC * N))


def col_layout(g):
    return np.ascontiguousarray(g.reshape(-1, 128).T)


def in_offsets(D):
    sizes = (1024, 256, 256, 1024, 1024, 512, 64, 512, 512, 512, 512, 512, D, D, D)
    names = ("qa", "ka", "va", "za", "cq", "ckv", "kr", "zb", "qc", "kc", "vc", "zc", "ga", "gb", "gc")
    off = {}
    o = 0
    for n, s in zip(names, sizes):
        off[n] = (o, o + s)
        o += s
    return off


def own_blocks(core, TPC):
    return [32 * j + 8 * i + core for j in range(TPC) for i in range(4)]


def own_rows(core, TPC):
    return np.concatenate([np.arange(gb * 128, gb * 128 + 128) for gb in own_blocks(core, TPC)])


def rope_tables(rows):
    pos = rows.astype(np.float32)
    inv = (np.float32(10000.0) ** (-np.arange(0, 64, 2, dtype=np.float32) / np.float32(64))).astype(np.float32)
    ang = (pos[:, None] * inv[None, :]).astype(np.float32)
    cs, sn = np.cos(ang).astype(np.float32), np.sin(ang).astype(np.float32)
    C = np.concatenate([cs, cs], axis=1).T
    S = np.concatenate([-sn, sn], axis=1).T
    return np.ascontiguousarray(C), np.ascontiguousarray(S)


SWAP64 = np.concatenate([np.arange(32, 64), np.arange(0, 32)])


def prep_A(D, TPC, x_rows, lw, core):
    off = in_offsets(D)
    W = lw["w_in"]
    sl = lambda n: W[:, off[n][0]:off[n][1]]
    kr = sl("kr")
    wF = f_layout(np.concatenate([sl("ka"), sl("kc"), sl("ckv"), kr, kr[:, SWAP64]], axis=1))
    Wkv = lw["w_kv_up"].reshape(512, 4, 2, 128)
    wkvF = f_layout(np.ascontiguousarray(Wkv[:, :, 0, :]).reshape(512, 512))
    wkvF = np.ascontiguousarray(wkvF.transpose(1, 0, 2).reshape(128, 4 * 4 * 128))
    wkvT = t_layout(np.ascontiguousarray(Wkv[:, :, 1, :]).reshape(512, 512))
    C, S = rope_tables(own_rows(core, TPC))
    return {
        "x": x_rows, "gcol": col_layout(lw["norm_g"]), "wF": wF, "wva": t_layout(sl("va")), "wvc": t_layout(sl("vc")),
        "gkv": col_layout(lw["g_kv_lora"]), "wkvF": wkvF, "wkvT": wkvT, "cosT": C, "sinT": S,
        "ident": np.eye(128, dtype=np.float32),
    }


QB = 2
KCH = 4


def build_B(D, NB, last, limit=None):
    KC = D // 128
    TW = QB * 128
    TPC = NB // (8 * QB)
    NT = TPC * TW
    NKB = ((NB + PAD + KCH - 1) // KCH) * KCH
    NK = NKB * 128
    nc = bass.Bass("TRN2", target_bir_lowering=False)
    es = ExitStack()
    p = Prog(nc, es)
    p.limit = limit
    c = Ctx()
    c.D, c.KC = D, KC
    I = lambda n, s, dt=F32: p.dram(n, s, dt, "ExternalInput")
    x = I("x", [NT, D])
    gcol_d = I("gcol", [128, KC])
    cos_d = I("cosT", [64, NT])
    sin_d = I("sinT", [64, NT])
    ident_d = I("ident", [128, 128])
    kaT_d = I("kaTp", [256, NK], BF16)
    va_d = I("vap", [NK, 256], BF16)
    knT_d = I("knTp", [512, NK], BF16)
    krT_d = I("krTp", [64, NK], BF16)
    vb_d = I("vbp", [NK, 512], BF16)
    kcT_d = I("kcTp", [512, NK], BF16)
    vc_d = I("vcp", [NK, 512], BF16)
    bcol_d = I("biascols", [128, 8])
    biasT_d = I("swabias", [128, 8 * 2 * 128])
    sinks_d = I("sinks", [128, 8])
    msk_d = I("masks", [128, 6 * 128])
    wF_d = I("wF", [36 + 3 * KC, 128, KC * 128])
    gq_d = I("gq", [128, 8])
    wq_d = I("wq", [4, 128, 8 * 256])
    wpa_d = I("wpa", [KC, 128, 8 * 128])
    wpb_d = I("wpb", [KC, 128, 4 * 128])
    wpc_d = I("wpc", [KC, 128, 4 * 128])
    wout_d = I("wout", [KC, 128, KC * 128])
    fg_d = I("fgb", [128, D])
    out_o = p.dram("xout", [NT, D], F32, "ExternalOutput")
    NWF = 36 + 3 * KC
    wF_s = p.dram("wF_s", [NWF, 128, KC * 128], BF16, "Internal")
    wF_sb = [T(wF_s.t) for _ in range(NWF)]
    wq_s = p.dram("wq_s", [4, 128, 8 * 256], BF16, "Internal")
    wq_sb = [T(wq_s.t) for _ in range(4)]
    wp_s = p.dram("wp_s", [KC, 128, 16 * 128], BF16, "Internal")
    wp_sb = [T(wp_s.t) for _ in range(KC)]
    wo_s = p.dram("wo_s", [KC, 128, KC * 128], BF16, "Internal")
    wo_sb = [T(wo_s.t) for _ in range(KC)]
    cur = {"j": 0}

    def wload(w, shp, src_d, src_ap, s_d, s_sb, blk):
        sap = s_d.t[blk].rearrange(shp[0], **shp[1])
        if cur["j"] == 0:
            p.dma("pool", w.t[:], src_ap, reads=[src_d], writes=[w])
            p.dma("pool", sap, w.t[:], reads=[w], writes=[s_sb[blk]])
        else:
            p.dma("sp", w.t[:], sap, reads=[s_sb[blk]], writes=[w])

    ps = [p.psum() for _ in range(8)]
    c.ps_tr = [ps[6], ps[7]]
    c.xin = p.sb([128, D], F32, "xin")
    c.xs = p.sb([128, D], F32, "xs")
    c.ss = p.sb([128, 1], F32, "ss")
    c.rs = p.sb([128, 1], F32, "rs")
    c.ident = p.sb([128, 128], F32, "ident")
    c.epsc = p.sb([128, 1], F32, "epsc")
    p.op("pool", lambda e: e.memset(c.epsc.t[:], EPS), writes=[c.epsc])
    gcol = p.sb([128, KC], F32, "gcol")
    gq = p.sb([128, 8], F32, "gq")
    ones = p.sb([128, 128], F32, "ones")
    negones = p.sb([128, 128], F32, "negones")
    onesb = p.sb([128, 128], BF16, "onesb")
    bcol = p.sb([128, 8], F32, "bcol")
    biasT = p.sb([128, 8, 2, 128], F32, "biasT")
    sinks = p.sb([128, 8], F32, "sinks")
    msk = p.sb([128, 6, 128], F32, "msk")
    mskb = p.sb([128, 2, 128], BF16, "mskb")
    maskA = p.sb([128, 2, 4, 128], BF16, "maskA")
    esink = p.sb([128, 2, 4, 128], F32, "esink")
    hT = p.sb([128, KC, TW], BF16, "hT")
    wFs = [p.sb([128, KC, 128], BF16, "wFs") for _ in range(3)]
    wqs = [p.sb([128, 8, 256], BF16, "wqs") for _ in range(1)]
    wps = [p.sb([128, 16, 128], BF16, "wps") for _ in range(2)]
    qaT = p.sb([128, 8, TW], BF16, "qaT")
    cqT = p.sb([128, 8, TW], F32, "cqT")
    cqn = p.sb([128, 8, TW], BF16, "cqn")
    qcT = p.sb([128, 4, TW], BF16, "qcT")
    qnT = p.sb([128, 4, TW], BF16, "qnT")
    qrT = p.sb([64, 4, TW], BF16, "qrT")
    yT = p.sb([128, 16, TW], BF16, "yT")
    mT = p.sb([128, KC, TW], BF16, "mT")
    sq = [p.sb([128, TW], F32, "sq") for _ in range(2)]
    rstdq = p.sb([128, TW], F32, "rstdq")
    cosS = p.sb([64, TW], F32, "cos")
    sinS = p.sb([64, TW], F32, "sin")
    t1 = p.sb([64, TW], F32, "t1")
    t2 = p.sb([64, TW], F32, "t2")
    assert QB == 2
    kch = [p.sb([128, KCH * 128], BF16, "kch") for _ in range(8)]
    krch = [p.sb([64, KCH * 128], BF16, "krch") for _ in range(2)]
    vch = [p.sb([128, KCH, 128], BF16, "vch") for _ in range(8)]
    PtP = [p.sb([128, 2 * TW], BF16, "PtP") for _ in range(4)]
    eeP = [p.sb([128, 2 * TW], F32, "eeP") for _ in range(2)]
    spP = [p.sb([128, 2 * TW], BF16, "spP") for _ in range(2)]
    LaP = [p.sb([128, 2 * TW], F32, "LaP") for _ in range(2)]
    LbP = [p.sb([128, 2 * TW], BF16, "LbP") for _ in range(2)]
    rdp = [p.sb([128, 2 * TW], F32, "rdp") for _ in range(2)]
    mskb2 = p.sb([128, 2, 2, 128], BF16, "mskb2")
    Pacc = [p.sb([128, 2 * TW], F32, "Pacc") for _ in range(2)]
    wfP = Pacc
    Pt, ef, rden = PtP, eeP, rdp[0]
    negIUb = p.sb([128, 128], BF16, "negIUb")
    negonesb = p.sb([128, 128], BF16, "negonesb")

    sg = [p.sb([128, TW], F32, "sg") for _ in range(2)]
    acc = [p.sb([128, TW], F32, "acc") for _ in range(2)]
    fgp = [p.sb([128, 256], F32, "fgp") for _ in range(2)]
    cnt = {"ch": 0, "P": 0, "e": 0, "s": 0, "wf": 0, "sg": 0, "kr": 0}

    load_const(p, c.ident, ident_d)
    load_const(p, gcol, gcol_d)
    load_const(p, gq, gq_d)
    load_const(p, bcol, bcol_d)
    load_const(p, sinks, sinks_d)
    p.dma("sp", biasT.t[:], biasT_d.t.rearrange("p (h t q) -> p h t q", h=8, t=2), reads=[biasT_d], writes=[biasT])
    p.dma("sp", msk.t[:], msk_d.t.rearrange("p (m q) -> p m q", m=6), reads=[msk_d], writes=[msk])
    p.op("pool", lambda e: e.memset(ones.t[:], 1.0), writes=[ones])
    p.op("pool", lambda e: e.memset(negones.t[:], -1.0), writes=[negones])
    p.op("pool", lambda e: e.memset(onesb.t[:], 1.0), writes=[onesb])
    p.op("dve", lambda e: e.tensor_copy(out=mskb.t[:], in_=msk.t[:, 0:2, :]), reads=[msk], writes=[mskb])
    for t_ in range(2):
        for hh in range(2):
            p.op("dve", lambda e, t_=t_, hh=hh: e.tensor_copy(out=mskb2.t[:, t_, hh, :], in_=msk.t[:, t_, :]), reads=[msk], writes=[mskb2])
    for t_ in range(2):
        for hh in range(4):
            p.op("dve", lambda e, t_=t_, hh=hh: e.tensor_copy(out=maskA.t[:, t_, hh, :], in_=msk.t[:, 2 + t_, :]),
                 reads=[msk], writes=[maskA])
    p.op("act", lambda e: e.activation(out=sinks.t[:], in_=sinks.t[:], func=AF.Exp), reads=[sinks], writes=[sinks])
    for h in range(8):
        p.op("dve", lambda e, h=h: e.tensor_scalar(out=esink.t[:, h // 4, h % 4, :], in0=ones.t[:], scalar1=sinks.t[:, h:h + 1],
                                                   scalar2=None, op0=ALU.mult), reads=[ones, sinks], writes=[esink])
    p.op("dve", lambda e: e.tensor_copy(out=negIUb.t[:], in_=msk.t[:, 4, :]), reads=[msk], writes=[negIUb])
    p.op("pool", lambda e: e.memset(negonesb.t[:], -1.0), writes=[negonesb])

    def fproj(blk, evac, m=128):
        w = wFs[cnt["wf"] % 3]
        bank = ps[cnt["wf"] % 2]
        cnt["wf"] += 1
        wload(w, ("p (c n) -> p c n", dict(c=KC)), wF_d, wF_d.t[blk].rearrange("p (c n) -> p c n", c=KC), wF_s, wF_sb, blk)
        mm_group(p, bank, [(w.t[:, ch, :], hT.t[:, ch, :]) for ch in range(KC)], n1=TW, reads=[w, hT])
        evac(bank)

    def load_chunk(src_kT, row0, src_v, col0, ck):
        k = cnt["ch"] % 8
        cnt["ch"] += 1
        kt, vt = kch[k], vch[k]
        k0 = ck * KCH * 128
        p.dma("sp", kt.t[:], src_kT.t[row0:row0 + 128, k0:k0 + KCH * 128], reads=[src_kT], writes=[kt])
        p.dma("sp", vt.t[:], src_v.t[k0:k0 + KCH * 128, col0:col0 + 128].rearrange("(b p) d -> p b d", p=128),
              reads=[src_v], writes=[vt])
        return kt, vt

    def bias_ap(kbp):
        return bcol.t[:, (kbp + 7):(kbp + 8)] if kbp < 0 else bcol.t[:, 7:8]

    for j in range(TPC):
        cur["j"] = j
        t0 = j * TW
        qk = [8 * QB * j + 8 * i for i in range(QB)]
        idx_max = qk[-1] + PAD

        def col0_of(kbp):
            for i in range(QB):
                if qk[i] >= kbp:
                    return i
            raise AssertionError

        for i in range(QB):
            emit_rms_hT(p, c, x, t0 + i * 128, gcol, hT, i * 128)
        p.dma("sp", cosS.t[:], cos_d.t[:, t0:t0 + TW], reads=[cos_d], writes=[cosS])
        p.dma("sp", sinS.t[:], sin_d.t[:, t0:t0 + TW], reads=[sin_d], writes=[sinS])

        for b in range(8):
            fproj(b, lambda bank, b=b: p.op("act", lambda e: e.activation(out=qaT.t[:, b, :], in_=bank.t[:, 0:TW], func=AF.Copy),
                                            reads=[bank], writes=[qaT]))
        for b in range(8):
            def ev(bank, b=b):
                s = sq[b % 2]
                p.op("dve", lambda e: e.tensor_copy(out=cqT.t[:, b, :], in_=bank.t[:, 0:TW]), reads=[bank], writes=[cqT])
                p.op("act", lambda e: e.activation(out=s.t[:], in_=cqT.t[:, b, :], func=AF.Square), reads=[cqT], writes=[s])
                p.op("pe", lambda e: e.matmul(ps[2].t[:, 0:TW], ones.t[:], s.t[:], start=(b == 0), stop=(b == 7)),
                     reads=[ones, s], writes=[ps[2]], inc=True)
            fproj(8 + b, ev)
        p.op("act", lambda e: e.activation(out=rstdq.t[:], in_=ps[2].t[:, 0:TW], func=AF.Sqrt, scale=1.0 / 1024, bias=c.epsc.t[:, 0:1]),
             reads=[ps[2], c.epsc], writes=[rstdq])
        p.op("dve", lambda e: e.reciprocal(out=rstdq.t[:], in_=rstdq.t[:]), reads=[rstdq], writes=[rstdq])
        for b in range(8):
            p.op("dve", lambda e, b=b: e.scalar_tensor_tensor(out=cqn.t[:, b, :], in0=cqT.t[:, b, :], scalar=gq.t[:, b:b + 1],
                                                              in1=rstdq.t[:], op0=ALU.mult, op1=ALU.mult),
                 reads=[cqT, gq, rstdq], writes=[cqn])
        for b in range(4):
            fproj(16 + b, lambda bank, b=b: p.op("act", lambda e: e.activation(out=qcT.t[:, b, :], in_=bank.t[:, 0:TW], func=AF.Copy,
                                                                               scale=128.0 ** -0.5), reads=[bank], writes=[qcT]))
        for h in range(4):
            w = wqs[0]
            wload(w, ("p (c n) -> p c n", dict(c=8)), wq_d, wq_d.t[h].rearrange("p (c n) -> p c n", c=8), wq_s, wq_sb, h)
            mm_group(p, ps[3], [(w.t[:, ch, 0:128], cqn.t[:, ch, :]) for ch in range(8)], n1=TW, reads=[w, cqn])
            mm_group(p, ps[4], [(w.t[:, ch, 128:192], cqn.t[:, ch, :]) for ch in range(8)], n1=TW, m=64, reads=[w, cqn])
            mm_group(p, ps[5], [(w.t[:, ch, 192:256], cqn.t[:, ch, :]) for ch in range(8)], n1=TW, m=64, reads=[w, cqn])
            p.op("act", lambda e, h=h: e.activation(out=qnT.t[:, h, :], in_=ps[3].t[:, 0:TW], func=AF.Copy), reads=[ps[3]], writes=[qnT])
            p.op("dve", lambda e: e.tensor_tensor(out=t1.t[:], in0=ps[4].t[0:64, 0:TW], in1=cosS.t[:], op=ALU.mult),
                 reads=[ps[4], cosS], writes=[t1])
            p.op("dve", lambda e: e.tensor_tensor(out=t2.t[:], in0=ps[5].t[0:64, 0:TW], in1=sinS.t[:], op=ALU.mult),
                 reads=[ps[5], sinS], writes=[t2])
            p.op("pool", lambda e, h=h: e.tensor_tensor(out=qrT.t[:, h, :], in0=t1.t[:], in1=t2.t[:], op=ALU.add),
                 reads=[t1, t2], writes=[qrT])

        for i in range(QB):
            for g in range(2):
                k = cnt["ch"] % 8
                cnt["ch"] += 1
                kt, vt = kch[k], vch[k]
                k0 = (qk[i] + PAD - 1) * 128
                p.dma("sp", kt.t[:, 0:256], kaT_d.t[g * 128:(g + 1) * 128, k0:k0 + 256], reads=[kaT_d], writes=[kt])
                p.dma("sp", vt.t[:, 0:2, :], va_d.t[k0:k0 + 256, g * 128:(g + 1) * 128].rearrange("(b p) d -> p b d", p=128),
                      reads=[va_d], writes=[vt])
                q4 = qaT.t[:, 4 * g:4 * g + 4, i * 128:(i + 1) * 128]
                for t_ in range(2):
                    sb_ = ps[t_]
                    s3 = sb_.t[:, 0:512].rearrange("p (h q) -> p h q", h=4)
                    p.op("pe", lambda e, t_=t_, s3=s3: e.matmul(s3, kt.t[:, t_ * 128:(t_ + 1) * 128], q4, start=True, stop=True),
                         reads=[kt, qaT], writes=[sb_])
                    ee = ef[cnt["e"] % 2]
                    cnt["e"] += 1
                    e3 = ee.t[:, 0:512].rearrange("p (h q) -> p h q", h=4)
                    p.op("dve", lambda e, s3=s3, e3=e3, t_=t_: e.scalar_tensor_tensor(
                        out=e3, in0=s3, scalar=128.0 ** -0.5, in1=biasT.t[:, 4 * g:4 * g + 4, t_, :], op0=ALU.mult, op1=ALU.add),
                         reads=[sb_, biasT], writes=[ee])
                    P = Pt[cnt["P"] % 2]
                    cnt["P"] += 1
                    bap = bias_ap(qk[i] - 1) if t_ == 0 else bcol.t[:, 7:8]
                    p.op("act", lambda e, P=P, ee=ee, bap=bap: e.activation(out=P.t[:, 0:512], in_=ee.t[:, 0:512], func=AF.Exp, bias=bap),
                         reads=[ee, bcol], writes=[P])
                    P3 = P.t[:, 0:512].rearrange("p (h q) -> p h q", h=4)
                    p.op("dve", lambda e, P3=P3, t_=t_: e.tensor_tensor(out=P3, in0=P3, in1=maskA.t[:, t_, :, :], op=ALU.mult),
                         reads=[P, maskA], writes=[P])
                    p.op("pe", lambda e, P=P, t_=t_: e.matmul(ps[2].t[:, 0:512], vt.t[:, t_, :], P.t[:, 0:512], start=(t_ == 0), stop=(t_ == 1)),
                         reads=[vt, P], writes=[ps[2]], inc=(t_ == 1))
                    p.op("pe", lambda e, P=P, t_=t_: e.matmul(ps[3].t[:, 0:512], onesb.t[:], P.t[:, 0:512], start=(t_ == 0), stop=(t_ == 1)),
                         reads=[onesb, P], writes=[ps[3]], inc=(t_ == 1))
                p.op("dve", lambda e, g=g: e.tensor_tensor(out=rden.t[:, 0:512], in0=ps[3].t[:, 0:512],
                                                           in1=esink.t[:, g, :, :].rearrange("p h q -> p (h q)"), op=ALU.add),
                     reads=[ps[3], esink], writes=[rden])
                p.op("dve", lambda e: e.reciprocal(out=rden.t[:, 0:512], in_=rden.t[:, 0:512]), reads=[rden], writes=[rden])
                p.op("dve", lambda e, g=g, i=i: e.tensor_tensor(
                    out=yT.t[:, 4 * g:4 * g + 4, i * 128:(i + 1) * 128], in0=ps[2].t[:, 0:512].rearrange("p (h q) -> p h q", h=4),
                    in1=rden.t[:, 0:512].rearrange("p (h q) -> p h q", h=4), op=ALU.mult),
                     reads=[ps[2], rden], writes=[yT])

        sclB = 192.0 ** -0.5
        for bk in (ps[4], ps[5]):
            p.op("dve", lambda e: e.memset(bk.t[:, :], 0.0), writes=[bk])
        for pr in range(2):
            p.op("dve", lambda e: e.memset(Pacc[pr].t[:], 0.0), writes=[Pacc[pr]])
        cacheB = {}

        def chunksB(ck):
            if ck not in cacheB:
                k0 = ck * KCH * 128
                krt = krch[cnt["kr"] % 2]
                cnt["kr"] += 1
                p.dma("sp", krt.t[:], krT_d.t[:, k0:k0 + KCH * 128], reads=[krT_d], writes=[krt])
                cacheB[ck] = (krt, [load_chunk(knT_d, h * 128, vb_d, h * 128, ck) for h in range(4)])
            return cacheB[ck]

        def binfo(idx):
            kbp = idx - PAD
            i0 = col0_of(kbp)
            return kbp, i0 * 128, (kbp == qk[i0])

        def v3(t, n0):
            return t.t[:, 0:2 * TW].rearrange("p (h q) -> p h q", h=2)[:, :, n0:TW]

        def scoreB(idx, pr):
            kbp, n0, diag = binfo(idx)
            krt, chunks = chunksB(idx // KCH)
            bl = idx % KCH
            sb_ = ps[2 * pr + idx % 2]
            for hh in range(2):
                h = 2 * pr + hh
                kt, vt = chunks[h]
                oc = hh * TW
                p.op("pe", lambda e: e.matmul(sb_.t[:, oc + n0:oc + TW], kt.t[:, bl * 128:(bl + 1) * 128], qnT.t[:, h, n0:TW],
                                              start=True, stop=False), reads=[kt, qnT], writes=[sb_], inc=False)
                p.op("pe", lambda e: e.matmul(sb_.t[:, oc + n0:oc + TW], krt.t[0:64, bl * 128:(bl + 1) * 128], qrT.t[0:64, h, n0:TW],
                                              start=False, stop=True), reads=[krt, qrT], writes=[sb_], inc=(hh == 1))

        for pr in range(2):
            scoreB(0, pr)
        for idx in range(idx_max + 1):
            kbp, n0, diag = binfo(idx)
            lastb = (idx == idx_max)
            krt, chunks = chunksB(idx // KCH)
            bl = idx % KCH
            for pr in range(2):
                if not lastb:
                    scoreB(idx + 1, pr)
                sb_ = ps[2 * pr + idx % 2]
                P = PtP[2 * pr + idx % 2]
                if kbp < 0:
                    bap = bias_ap(kbp)
                    p.op("act", lambda e: e.activation(out=v3(P, n0), in_=v3(sb_, n0), func=AF.Exp, scale=sclB, bias=bap),
                         reads=[sb_, bcol], writes=[P])
                else:
                    p.op("act", lambda e: e.activation(out=v3(P, n0), in_=v3(sb_, n0), func=AF.Exp, scale=sclB),
                         reads=[sb_], writes=[P])
                if diag:
                    pd = P.t[:, 0:2 * TW].rearrange("p (h q) -> p h q", h=2)[:, :, n0:n0 + 128]
                    p.op("dve", lambda e: e.tensor_tensor(out=pd, in0=pd, in1=mskb2.t[:, 0, :, :], op=ALU.mult), reads=[P, mskb2], writes=[P])
                ob, db = ps[4 + pr], ps[6 + pr]
                for hh in range(2):
                    h = 2 * pr + hh
                    kt, vt = chunks[h]
                    oc = hh * TW
                    p.op("pe", lambda e: e.matmul(ob.t[:, oc + n0:oc + TW], vt.t[:, bl, :], P.t[:, oc + n0:oc + TW], start=False, stop=lastb,
                                                  skip_group_check=True), reads=[vt, P], writes=[ob], inc=(hh == 1))
                pa = Pacc[pr]
                p.op("dve", lambda e: e.tensor_tensor(out=v3(pa, n0), in0=v3(pa, n0), in1=v3(P, n0), op=ALU.add), reads=[pa, P], writes=[pa])
        for pr in range(2):
            ob, db = ps[4 + pr], ps[6 + pr]
            rd = rdp[pr]
            p.op("pe", lambda e: e.matmul(db.t[:, 0:2 * TW], ones.t[:], Pacc[pr].t[:], start=True, stop=True), reads=[ones, Pacc[pr]], writes=[db])
            p.op("dve", lambda e: e.reciprocal(out=rd.t[:, :], in_=db.t[:, 0:2 * TW]), reads=[db], writes=[rd])
            p.op("dve", lambda e: e.tensor_tensor(out=yT.t[:, 8 + 2 * pr:10 + 2 * pr, :], in0=ob.t[:, 0:2 * TW].rearrange("p (h q) -> p h q", h=2),
                                                  in1=rd.t[:, :].rearrange("p (h q) -> p h q", h=2), op=ALU.mult),
                 reads=[ob, rd], writes=[yT])

        for pr in range(2):
            p.op("dve", lambda e: e.memset(LaP[pr].t[:], 0.0), writes=[LaP[pr]])
            p.op("dve", lambda e: e.memset(LbP[pr].t[:], 0.0), writes=[LbP[pr]])
        for bk in (ps[4], ps[5]):
            p.op("dve", lambda e: e.memset(bk.t[:, :], 0.0), writes=[bk])
        cacheC = {}

        def chunksC(ck):
            if ck not in cacheC:
                cacheC[ck] = [load_chunk(kcT_d, h * 128, vc_d, h * 128, ck) for h in range(4)]
            return cacheC[ck]

        def zC(idx):
            kbp, n0, diag = binfo(idx)
            chunks = chunksC(idx // KCH)
            bl = idx % KCH
            for h in range(4):
                kt, vt = chunks[h]
                zb = ps[h // 2]
                oc = (h % 2) * TW
                p.op("pe", lambda e: e.matmul(zb.t[:, oc + n0:oc + TW], kt.t[:, bl * 128:(bl + 1) * 128], qcT.t[:, h, n0:TW], start=True, stop=True),
                     reads=[kt, qcT], writes=[zb], inc=(h % 2 == 1))

        def spC(idx):
            kbp, n0, diag = binfo(idx)
            bap = bias_ap(kbp) if kbp < 0 else 0.0
            for pr in range(2):
                zb, ee, sp_ = ps[pr], eeP[pr], spP[pr]
                p.op("act", lambda e: e.activation(out=v3(ee, n0), in_=v3(zb, n0), func=AF.Exp, bias=bap), reads=[zb, bcol], writes=[ee])
                p.op("act", lambda e: e.activation(out=v3(sp_, n0), in_=v3(ee, n0), func=AF.Ln, bias=1.0), reads=[ee], writes=[sp_])
                if diag:
                    sd = sp_.t[:, 0:2 * TW].rearrange("p (h q) -> p h q", h=2)[:, :, n0:n0 + 128]
                    p.op("dve", lambda e: e.tensor_tensor(out=sd, in0=sd, in1=mskb2.t[:, 1, :, :], op=ALU.mult), reads=[sp_, mskb2], writes=[sp_])

        zC(idx_max)
        spC(idx_max)
        for idx in range(idx_max, -1, -1):
            kbp, n0, diag = binfo(idx)
            lastb = (idx == 0)
            chunks = chunksC(idx // KCH)
            bl = idx % KCH
            bap = bias_ap(kbp) if kbp < 0 else 0.0
            for h in range(4):
                kt, vt = chunks[h]
                ab, sp_, Lb = ps[2 + h // 2], spP[h // 2], LbP[h // 2]
                oc = (h % 2) * TW
                kap = kt.t[:, bl * 128:(bl + 1) * 128]
                p.op("pe", lambda e: e.matmul(ab.t[:, oc + n0:oc + TW], negIUb.t[:], sp_.t[:, oc + n0:oc + TW], start=True, stop=False),
                     reads=[negIUb, sp_], writes=[ab], inc=False)
                p.op("pe", lambda e: e.matmul(ab.t[:, oc + n0:oc + TW], negonesb.t[:], Lb.t[:, oc + n0:oc + TW], start=False, stop=True),
                     reads=[negonesb, Lb], writes=[ab], inc=(h % 2 == 1))
            if not lastb:
                zC(idx - 1)
            for pr in range(2):
                ab, sp_, La, Lb = ps[2 + pr], spP[pr], LaP[pr], LbP[pr]
                P = PtP[2 * pr + idx % 2]
                ee = eeP[pr]
                wf = wfP[pr]
                p.op("act", lambda e: e.activation(out=v3(wf, n0), in_=v3(ab, n0), func=AF.Exp), reads=[ab], writes=[wf])
                p.op("dve", lambda e: e.tensor_tensor(out=v3(P, n0), in0=v3(wf, n0), in1=v3(ee, n0), op=ALU.mult), reads=[wf, ee], writes=[P])
                if diag:
                    pd = P.t[:, 0:2 * TW].rearrange("p (h q) -> p h q", h=2)[:, :, n0:n0 + 128]
                    p.op("dve", lambda e: e.tensor_tensor(out=pd, in0=pd, in1=mskb2.t[:, 1, :, :], op=ALU.mult), reads=[P, mskb2], writes=[P])
                p.op("dve", lambda e: e.tensor_tensor(out=v3(La, n0), in0=v3(La, n0), in1=v3(sp_, n0), op=ALU.add), reads=[La, sp_], writes=[La])
                p.op("dve", lambda e: e.tensor_copy(out=v3(Lb, n0), in_=v3(La, n0)), reads=[La], writes=[Lb])
            if not lastb:
                spC(idx - 1)
            for h in range(4):
                kt, vt = chunks[h]
                ob = ps[4 + h // 2]
                oc = (h % 2) * TW
                P = PtP[2 * (h // 2) + idx % 2]
                p.op("pe", lambda e: e.matmul(ob.t[:, oc + n0:oc + TW], vt.t[:, bl, :], P.t[:, oc + n0:oc + TW], start=False, stop=lastb,
                                              skip_group_check=True), reads=[vt, P], writes=[ob], inc=(h % 2 == 1))
        for pr in range(2):
            ob = ps[4 + pr]
            p.op("act", lambda e: e.activation(out=yT.t[:, 12 + 2 * pr:14 + 2 * pr, :], in_=ob.t[:, 0:2 * TW].rearrange("p (h q) -> p h q", h=2),
                                               func=AF.Copy), reads=[ob], writes=[yT])

        for k in range(16):
            def ev(bank, k=k):
                s = sg[k % 2]
                p.op("act", lambda e: e.activation(out=s.t[:], in_=bank.t[:, 0:TW], func=AF.Silu), reads=[bank], writes=[s])
                p.op("dve", lambda e: e.tensor_tensor(out=yT.t[:, k, :], in0=yT.t[:, k, :], in1=s.t[:], op=ALU.mult),
                     reads=[yT, s], writes=[yT])
            fproj(20 + k, ev)

        for jb in range(KC):
            wp = wps[jb % 2]
            if j == 0:
                p.dma("pool", wp.t[:, 0:8, :], wpa_d.t[jb].rearrange("p (c n) -> p c n", c=8), reads=[wpa_d], writes=[wp])
                p.dma("pool", wp.t[:, 8:12, :], wpb_d.t[jb].rearrange("p (c n) -> p c n", c=4), reads=[wpb_d], writes=[wp])
                p.dma("pool", wp.t[:, 12:16, :], wpc_d.t[jb].rearrange("p (c n) -> p c n", c=4), reads=[wpc_d], writes=[wp])
                p.dma("pool", wp_s.t[jb].rearrange("p (c n) -> p c n", c=16), wp.t[:], reads=[wp], writes=[wp_sb[jb]])
            else:
                p.dma("sp", wp.t[:], wp_s.t[jb].rearrange("p (c n) -> p c n", c=16), reads=[wp_sb[jb]], writes=[wp])
            a_ = acc[jb % 2]
            for br, (k0, k1) in enumerate(((0, 8), (8, 12), (12, 16))):
                pb = ps[2 + br]
                mm_group(p, pb, [(wp.t[:, k, :], yT.t[:, k, :]) for k in range(k0, k1)], n1=TW, reads=[wp, yT])

                def ev(bank, br=br, pb=pb, a_=a_, jb=jb):
                    s = sg[cnt["sg"] % 2]
                    cnt["sg"] += 1
                    p.op("act", lambda e: e.activation(out=s.t[:], in_=bank.t[:, 0:TW], func=AF.Sigmoid), reads=[bank], writes=[s])
                    if br == 0:
                        p.op("dve", lambda e: e.tensor_tensor(out=a_.t[:], in0=pb.t[:, 0:TW], in1=s.t[:], op=ALU.mult),
                             reads=[pb, s], writes=[a_])
                    else:
                        p.op("dve", lambda e: e.tensor_tensor(out=s.t[:], in0=pb.t[:, 0:TW], in1=s.t[:], op=ALU.mult),
                             reads=[pb, s], writes=[s])
                        if br == 1:
                            p.op("pool", lambda e: e.tensor_tensor(out=a_.t[:], in0=a_.t[:], in1=s.t[:], op=ALU.add),
                                 reads=[a_, s], writes=[a_])
                        else:
                            p.op("pool", lambda e: e.tensor_tensor(out=mT.t[:, jb, :], in0=a_.t[:], in1=s.t[:], op=ALU.add),
                                 reads=[a_, s], writes=[mT])
                fproj(36 + br * KC + jb, ev)

        xn = [c.xin, c.xs]
        for i in range(QB):
            p.dma("sp", xn[i].t[:], x.t[t0 + i * 128:t0 + (i + 1) * 128, :], reads=[x], writes=[xn[i]])
        for n in range(KC):
            w = wFs[cnt["wf"] % 3]
            cnt["wf"] += 1
            wload(w, ("p (c n) -> p c n", dict(c=KC)), wout_d, wout_d.t[n].rearrange("p (c n) -> p c n", c=KC), wo_s, wo_sb, n)
            for i in range(QB):
                bank = ps[(n * QB + i) % 2]
                mm_group(p, bank, [(mT.t[:, ch, i * 128:(i + 1) * 128], w.t[:, ch, :]) for ch in range(KC)], n1=128, reads=[w, mT])
                p.op("dve", lambda e, bank=bank, i=i, n=n: e.tensor_tensor(out=xn[i].t[:, n * 128:(n + 1) * 128], in0=bank.t[:, 0:128],
                                                                          in1=xn[i].t[:, n * 128:(n + 1) * 128], op=ALU.add),
                     reads=[bank, xn[i]], writes=[xn[i]])
        for i in range(QB):
            if last:
                junk = hT.t[:, :, :].rearrange("p c t -> p (c t)")[:, 0:D]
                p.op("act", lambda e, i=i, junk=junk: e.activation(out=junk, in_=xn[i].t[:], func=AF.Square, accum_out=c.ss.t[:, 0:1]),
                     reads=[xn[i]], writes=[hT, c.ss])
                p.op("act", lambda e: e.activation(out=c.rs.t[:, 0:1], in_=c.ss.t[:, 0:1], func=AF.Sqrt, scale=1.0 / D, bias=c.epsc.t[:, 0:1]),
                     reads=[c.ss, c.epsc], writes=[c.rs])
                p.op("dve", lambda e: e.reciprocal(out=c.rs.t[:, 0:1], in_=c.rs.t[:, 0:1]), reads=[c.rs], writes=[c.rs])
                for k in range(0, D, 256):
                    f = fgp[(k // 256) % 2]
                    p.dma("sp", f.t[:], fg_d.t[:, k:k + 256], reads=[fg_d], writes=[f])
                    p.op("dve", lambda e, i=i, k=k, f=f: e.scalar_tensor_tensor(out=xn[i].t[:, k:k + 256], in0=xn[i].t[:, k:k + 256],
                                                                               scalar=c.rs.t[:, 0:1], in1=f.t[:], op0=ALU.mult, op1=ALU.mult),
                         reads=[xn[i], c.rs, f], writes=[xn[i]])
            p.dma("sp", out_o.t[t0 + i * 128:t0 + (i + 1) * 128, :], xn[i].t[:], reads=[xn[i]], writes=[])
    p.finish()
    es.close()
    nc._nops = p.nops
    return nc


def t5_bucket_np(rel):
    n = np.maximum(rel, 0)
    logn = np.log(np.maximum(n, 1).astype(np.float32) / np.float32(16))
    large = 16 + (logn / np.float32(np.log(128 / 16)) * np.float32(16)).astype(np.int32)
    large = np.minimum(large, 31)
    return np.where(n < 16, n, large)


def make_masks():
    r = np.arange(128)[:, None]
    cc = np.arange(128)[None, :]
    m = np.zeros((128, 6, 128), np.float32)
    m[:, 0] = (r <= cc)
    m[:, 1] = (r < cc)
    m[:, 2] = (r > cc)
    m[:, 3] = (r <= cc)
    m[:, 4] = -(r >= cc).astype(np.float32)
    return np.ascontiguousarray(m.reshape(128, 6 * 128))


def shifted(G, core, NB, axis):
    NKB = ((NB + PAD + KCH - 1) // KCH) * KCH
    tot = NKB + 8
    shp = list(G.shape)
    shp[axis] = tot * 128
    Gp = np.zeros(shp, G.dtype)
    sl = [slice(None)] * G.ndim
    sl[axis] = slice(PAD * 128, (PAD + NB) * 128)
    Gp[tuple(sl)] = G
    sl[axis] = slice(core * 128, (core + NKB) * 128)
    return np.ascontiguousarray(Gp[tuple(sl)])


def prep_B_shared(D, lw, rel_bias, final_g):
    off = in_offsets(D)
    W = lw["w_in"]
    sl = lambda n: W[:, off[n][0]:off[n][1]]
    wF = f_layout(np.concatenate([sl(n) for n in ("qa", "cq", "qc", "za", "zb", "zc", "ga", "gb", "gc")], axis=1))
    Wq = lw["w_q_up"].reshape(1024, 4, 192)
    wq = np.stack([t_layout(np.concatenate([Wq[:, h, :128], Wq[:, h, 128:], Wq[:, h, 128:][:, SWAP64]], axis=1)) for h in range(4)])
    s_ = np.arange(128)[:, None]
    q_ = np.arange(128)[None, :]
    sw = np.zeros((128, 8, 2, 128), np.float32)
    for t in range(2):
        rel = 128 + q_ - (t * 128 + s_)
        sw[:, :, t, :] = rel_bias[t5_bucket_np(rel)].transpose(0, 2, 1)
    return {
        "gcol": col_layout(lw["norm_g"]), "ident": np.eye(128, dtype=np.float32),
        "swabias": np.ascontiguousarray(sw.reshape(128, 8 * 2 * 128)),
        "sinks": np.ascontiguousarray(np.broadcast_to(lw["attn_sinks"][None, :], (128, 8))),
        "masks": make_masks(), "wF": wF, "gq": col_layout(lw["g_q_lora"]), "wq": wq,
        "wpa": f_layout(lw["w_proj_a"]), "wpb": f_layout(lw["w_proj_b"]), "wpc": f_layout(lw["w_proj_c"]),
        "wout": f_layout(lw["w_out"]), "fgb": np.ascontiguousarray(np.broadcast_to(final_g[None, :], (128, D))),
    }


def prep_B_core(D, NB, x_rows, glob, core, shared):
    TPC = NB // 32
    C, S = rope_tables(own_rows(core, TPC))
    bc = np.zeros((128, 8), np.float32)
    for t in range(7):
        if (t - 7) + core < 0:
            bc[:, t] = NEG
    m = dict(shared)
    m.update({
        "x": x_rows, "cosT": C, "sinT": S, "biascols": bc,
        "kaTp": shifted(glob["kaT"], core, NB, 1), "vap": shifted(glob["va"], core, NB, 0),
        "knTp": shifted(glob["knT"], core, NB, 1), "krTp": shifted(glob["krT"], core, NB, 1),
        "vbp": shifted(glob["vb"], core, NB, 0), "kcTp": shifted(glob["kcT"], core, NB, 1),
        "vcp": shifted(glob["vc"], core, NB, 0),
    })
    return m


def gather_kv(resA, NB):
    TPC = NB // 32
    glob = {}
    for k in ("kaT", "kcT", "knT", "krT", "va", "vc", "vb"):
        tm = not k.endswith("T")
        a0 = np.asarray(resA[0][k])
        shp = (NB * 128, a0.shape[1]) if tm else (a0.shape[0], NB * 128)
        G = np.zeros(shp, a0.dtype)
        for core in range(NCORES):
            a = np.asarray(resA[core][k])
            for m_, gb in enumerate(own_blocks(core, TPC)):
                if tm:
                    G[gb * 128:(gb + 1) * 128] = a[m_ * 128:(m_ + 1) * 128]
                else:
                    G[:, gb * 128:(gb + 1) * 128] = a[:, m_ * 128:(m_ + 1) * 128]
        glob[k] = G
    return glob


_CACHE = {}


def run_model(x2d, layers, rel_bias, final_g, D, SEQ):
    NB = SEQ // 128
    TPC = NB // 32
    rows = [own_rows(c, TPC) for c in range(NCORES)]
    xs = [np.ascontiguousarray(x2d[r]) for r in rows]
    cores = list(range(NCORES))
    for li, lw in enumerate(layers):
        last = (li == len(layers) - 1)
        if ("A", D, TPC) not in _CACHE:
            _CACHE[("A", D, TPC)] = build_A(D, TPC)
        if ("B", D, NB, last) not in _CACHE:
            _CACHE[("B", D, NB, last)] = build_B(D, NB, last)
        resA = run_bass_kernel_spmd(_CACHE[("A", D, TPC)], [prep_A(D, TPC, xs[c], lw, c) for c in cores], core_ids=cores).results
        glob = gather_kv(resA, NB)
        shared = prep_B_shared(D, lw, rel_bias, final_g)
        resB = run_bass_kernel_spmd(_CACHE[("B", D, NB, last)], [prep_B_core(D, NB, xs[c], glob, c, shared) for c in cores],
                                    core_ids=cores).results
        xs = [np.asarray(resB[c]["xout"]) for c in cores]
    out = np.zeros((SEQ, D), np.float32)
    for c in cores:
        out[rows[c]] = xs[c]
    return out


def kernel(x, norm_g, w_in, attn_sinks, rel_bias, g_q_lora, w_q_up, g_kv_lora, w_kv_up,
           w_proj_a, w_proj_b, w_proj_c, w_out, final_g):
    x = np.asarray(x, np.float32)
    B_, S_, D_ = x.shape
    names = ("norm_g", "w_in", "attn_sinks", "g_q_lora", "w_q_up", "g_kv_lora", "w_kv_up", "w_proj_a", "w_proj_b", "w_proj_c", "w_out")
    vals = (norm_g, w_in, attn_sinks, g_q_lora, w_q_up, g_kv_lora, w_kv_up, w_proj_a, w_proj_b, w_proj_c, w_out)
    depth = np.asarray(norm_g).shape[0]
    layers = [{n: np.asarray(v, np.float32)[l] for n, v in zip(names, vals)} for l in range(depth)]
    out = run_model(x[0], layers, np.asarray(rel_bias, np.float32), np.asarray(final_g, np.float32), D_, S_)
    return out[None].astype(np.float32)
```

```python
import numpy as np
import ml_dtypes
from contextlib import ExitStack
import concourse.bass as bass
import concourse.mybir as mybir
from concourse.bass_utils import run_bass_kernel_spmd

F32 = mybir.dt.float32
BF16 = mybir.dt.bfloat16
AF = mybir.ActivationFunctionType
ALU = mybir.AluOpType
AX = mybir.AxisListType
NEG = -30000.0
EPS = 1e-6
NCORES = 8
PAD = 7


class Buf:
    __slots__ = ("w", "r")

    def __init__(self):
        self.w = None
        self.r = {}


class T:
    def __init__(self, t, b=None, is_ps=False):
        self.t = t
        self.b = b or Buf()
        self.is_ps = is_ps


class Prog:
    KD = 6

    def __init__(self, nc, es):
        self.nc = nc
        self.es = es
        self.eng = {"pe": nc.tensor, "act": nc.scalar, "dve": nc.vector, "pool": nc.gpsimd, "sp": nc.sync}
        self.ops = {k: [] for k in self.eng}
        self.sems = {}
        self.cnt = {}
        self.seen = {k: {} for k in self.eng}
        for k in ("pe", "act", "dve", "pool"):
            self.sems[k] = es.enter_context(nc.semaphore("c_" + k))
            self.cnt[k] = 0
        self.ndma = {k: 0 for k in self.eng}
        for q in ("sp", "pool", "act"):
            for s in range(self.KD):
                key = ("d", q, s)
                self.sems[key] = es.enter_context(nc.semaphore("d_%s_%d" % (q, s)))
                self.cnt[key] = 0
        self.nt = 0
        self.nops = 0
        self.limit = None

    def _skip(self):
        self.nops += 1
        return self.limit is not None and self.nops > self.limit

    def sb(self, shape, dt, name=None):
        self.nt += 1
        t = self.es.enter_context(self.nc.sbuf_tensor("%s_%d" % (name or "t", self.nt), list(shape), dt))
        return T(t)

    def psum(self):
        self.nt += 1
        t = self.es.enter_context(self.nc.psum_tensor("ps_%d" % self.nt, [128, 512], F32))
        return T(t, is_ps=True)

    def dram(self, name, shape, dt, kind):
        return T(self.nc.dram_tensor(name, list(shape), dt, kind=kind).ap())

    def _deps(self, e, reads, writes, skip_self=False):
        need = {}

        def add(k, v):
            if skip_self and k == e:
                return
            if need.get(k, 0) < v:
                need[k] = v

        for b in reads:
            if b.w is not None:
                add(*b.w)
        for b in writes:
            if b.w is not None:
                add(*b.w)
            for k, v in b.r.items():
                add(k, v)
        out = []
        seen = self.seen[e]
        for k, v in need.items():
            if seen.get(k, 0) < v:
                seen[k] = v
                out.append((self.sems[k], v))
        return out

    def op(self, e, fn, reads=(), writes=(), inc=True):
        if self._skip():
            return
        writes = [x.b for x in writes] + [x.b for x in reads if x.is_ps and e != "pe"]
        reads = [x.b for x in reads if not (x.is_ps and e != "pe")]
        waits = self._deps(e, reads, writes, skip_self=(e == "pe"))
        if inc:
            self.cnt[e] += 1
            tok = (e, self.cnt[e])
        else:
            tok = (e, self.cnt[e] + 1)
        sem = self.sems[e]

        engobj = self.eng[e]
        for s, v in waits:
            engobj.wait_ge(s, v)
        ins = fn(engobj)
        if inc:
            ins.then_inc(sem, 1)
        for b in writes:
            b.w = tok
            b.r = {}
        for b in reads:
            if b.r.get(tok[0], 0) < tok[1]:
                b.r[tok[0]] = tok[1]

    def dma(self, q, out, in_, reads=(), writes=()):
        if self._skip():
            return
        reads = [x.b for x in reads]
        writes = [x.b for x in writes]
        n = self.ndma[q]
        self.ndma[q] += 1
        key = ("d", q, n % self.KD)
        waits = self._deps(q, reads, writes)
        prev = self.cnt[key]
        seen = self.seen[q]
        if prev > 0 and seen.get(key, 0) < prev:
            seen[key] = prev
            waits.append((self.sems[key], prev))
        self.cnt[key] += 16
        tok = (key, self.cnt[key])
        sem = self.sems[key]

        engobj = self.eng[q]
        for s, v in waits:
            engobj.wait_ge(s, v)
        engobj.dma_start(out=out, in_=in_).then_inc(sem, 16)
        for b in writes:
            b.w = tok
            b.r = {}
        for b in reads:
            if b.r.get(key, 0) < tok[1]:
                b.r[key] = tok[1]

    def alias(self, new, old):
        toks = {}
        for x in old:
            b = x.b
            if b.w is not None and toks.get(b.w[0], 0) < b.w[1]:
                toks[b.w[0]] = b.w[1]
            for k, v in b.r.items():
                if toks.get(k, 0) < v:
                    toks[k] = v
        for x in new:
            x.b.w = None
            x.b.r = dict(toks)

    def finish(self):
        fin = []
        for key, v in self.cnt.items():
            if isinstance(key, tuple) and v > 0:
                fin.append((self.sems[key], v))
        for k in ("pe", "act", "dve", "pool"):
            if self.cnt[k] > 0:
                fin.append((self.sems[k], self.cnt[k]))

        for s, v in fin:
            self.nc.sync.wait_ge(s, v)


class Ctx:
    pass


def mm_group(p, ps, pairs, n0=0, n1=512, m=128, reads=(), fp32=False):
    n = len(pairs)
    for k, (l, r) in enumerate(pairs):
        p.op("pe", (lambda e, l=l, r=r, k=k: e.matmul(ps.t[0:m, n0:n1], l, r, start=(k == 0), stop=(k == n - 1))),
             reads=reads, writes=[ps], inc=(k == n - 1))


def emit_rms_hT(p, c, x_dram, row0, gcol, hT, col0, bufs=None):
    D, KC = c.D, c.KC
    if bufs is not None:
        c = bufs
    p.dma("sp", c.xin.t[:], x_dram.t[row0:row0 + 128, :], reads=[x_dram], writes=[c.xin])
    p.op("act", lambda e: e.activation(out=c.xs.t[:], in_=c.xin.t[:], func=AF.Square, accum_out=c.ss.t[:, 0:1]),
         reads=[c.xin], writes=[c.xs, c.ss])
    p.op("act", lambda e: e.activation(out=c.rs.t[:, 0:1], in_=c.ss.t[:, 0:1], func=AF.Sqrt, scale=1.0 / D, bias=c.epsc.t[:, 0:1]),
         reads=[c.ss, c.epsc], writes=[c.rs])
    p.op("dve", lambda e: e.reciprocal(out=c.rs.t[:, 0:1], in_=c.rs.t[:, 0:1]), reads=[c.rs], writes=[c.rs])
    p.op("act", lambda e: e.activation(out=c.xs.t[:], in_=c.xin.t[:], func=AF.Copy, scale=c.rs.t[:, 0:1]),
         reads=[c.xin, c.rs], writes=[c.xs])
    for g4 in range(0, KC, 4):
        ps = c.ps_tr[(g4 // 4) % 2]
        nk = min(4, KC - g4)
        for k in range(nk):
            ch = g4 + k
            p.op("pe", (lambda e, ch=ch, k=k, ps=ps: e.transpose(ps.t[:, k * 128:(k + 1) * 128],
                                                                c.xs.t[:, ch * 128:(ch + 1) * 128], c.ident.t[:])),
                 reads=[c.xs, c.ident], writes=[ps], inc=(k == nk - 1))
        for k in range(nk):
            ch = g4 + k
            if k % 2 == 0:
                p.op("dve", (lambda e, ch=ch, k=k, ps=ps: e.tensor_scalar(
                    out=hT.t[:, ch, col0:col0 + 128], in0=ps.t[:, k * 128:(k + 1) * 128],
                    scalar1=gcol.t[:, ch:ch + 1], scalar2=None, op0=ALU.mult)), reads=[ps, gcol], writes=[hT])
            else:
                p.op("act", (lambda e, ch=ch, k=k, ps=ps: e.activation(
                    out=hT.t[:, ch, col0:col0 + 128], in_=ps.t[:, k * 128:(k + 1) * 128],
                    func=AF.Copy, scale=gcol.t[:, ch:ch + 1])), reads=[ps, gcol], writes=[hT])


def load_const(p, t, src, q="sp"):
    p.dma(q, t.t[:], src.t, reads=[src], writes=[t])


def build_A(D, TPC, limit=None):
    KC = D // 128
    NT = TPC * 512
    nc = bass.Bass("TRN2", target_bir_lowering=False)
    es = ExitStack()
    p = Prog(nc, es)
    p.limit = limit
    c = Ctx()
    c.D, c.KC = D, KC
    I = lambda n, s, dt=F32: p.dram(n, s, dt, "ExternalInput")
    O = lambda n, s, dt=BF16: p.dram(n, s, dt, "ExternalOutput")
    x = I("x", [NT, D])
    gcol_d = I("gcol", [128, KC])
    wF_d = I("wF", [11, 128, KC * 128])
    wva_d = I("wva", [128, KC * 256])
    wvc_d = I("wvc", [128, KC * 512])
    gkv_d = I("gkv", [128, 4])
    wkvF_d = I("wkvF", [128, 4 * 4 * 128])
    wkvT_d = I("wkvT", [128, 4 * 512])
    cos_d = I("cosT", [64, NT])
    sin_d = I("sinT", [64, NT])
    ident_d = I("ident", [128, 128])
    kaT_o = O("kaT", [2 * 128, NT])
    kcT_o = O("kcT", [4 * 128, NT])
    knT_o = O("knT", [4 * 128, NT])
    krT_o = O("krT", [64, NT])
    va_o = O("va", [NT, 256])
    vc_o = O("vc", [NT, 512])
    vb_o = O("vb", [NT, 512])

    ps = [p.psum() for _ in range(8)]
    c.ps_tr = [ps[5], ps[6]]
    c.xin = p.sb([128, D], F32, "xin")
    c.xs = p.sb([128, D], F32, "xs")
    c.ss = p.sb([128, 1], F32, "ss")
    c.rs = p.sb([128, 1], F32, "rs")
    c2 = Ctx()
    c2.D, c2.KC = D, KC
    c2.xin = p.sb([128, D], F32, "xin2")
    c2.xs = p.sb([128, D], F32, "xs2")
    c2.ss = p.sb([128, 1], F32, "ss2")
    c2.rs = p.sb([128, 1], F32, "rs2")
    c.ident = p.sb([128, 128], F32, "ident")
    c.epsc = p.sb([128, 1], F32, "epsc")
    p.op("pool", lambda e: e.memset(c.epsc.t[:], EPS), writes=[c.epsc])
    gcol = p.sb([128, KC], F32, "gcol")
    gkv = p.sb([128, 4], F32, "gkv")
    ones = p.sb([128, 128], F32, "ones")
    wva = p.sb([128, KC, 256], BF16, "wva")
    wvc = p.sb([128, KC, 512], BF16, "wvc")
    wkvF = p.sb([128, 4, 4, 128], BF16, "wkvF")
    wkvT = p.sb([128, 4, 512], BF16, "wkvT")
    hT = p.sb([128, KC, 512], BF16, "hT")
    wFs = [p.sb([128, KC, 128], BF16, "wFs") for _ in range(2)]
    ckvT = p.sb([128, 4, 512], F32, "ckvT")
    sq = [p.sb([128, 512], F32, "sq") for _ in range(2)]
    rstdb = p.sb([128, 512], F32, "rstdb")
    ckvn = p.sb([128, 4, 512], BF16, "ckvn")
    cosS = p.sb([64, 512], F32, "cos")
    sinS = p.sb([64, 512], F32, "sin")
    t1 = p.sb([64, 512], F32, "t1")
    t2 = p.sb([64, 512], F32, "t2")
    outK = [p.sb([128, 512], BF16, "outK") for _ in range(2)]
    outV = [p.sb([128, 512], BF16, "outV") for _ in range(2)]
    nK = [0]
    nV = [0]

    load_const(p, c.ident, ident_d)
    c2.ident, c2.epsc, c2.ps_tr = c.ident, c.epsc, c.ps_tr
    wA_s = p.dram("wA_s", [11, 128, KC * 128], BF16, "Internal")
    wA_sb = [T(wA_s.t) for _ in range(11)]
    load_const(p, gcol, gcol_d)
    load_const(p, gkv, gkv_d)
    p.op("pool", lambda e: e.memset(ones.t[:], 1.0), writes=[ones])
    p.dma("pool", wva.t[:], wva_d.t.rearrange("p (c n) -> p c n", c=KC), reads=[wva_d], writes=[wva])
    p.dma("pool", wvc.t[:], wvc_d.t.rearrange("p (c n) -> p c n", c=KC), reads=[wvc_d], writes=[wvc])
    p.dma("pool", wkvF.t[:], wkvF_d.t.rearrange("p (h c n) -> p h c n", h=4, c=4), reads=[wkvF_d], writes=[wkvF])
    p.dma("pool", wkvT.t[:], wkvT_d.t.rearrange("p (c n) -> p c n", c=4), reads=[wkvT_d], writes=[wkvT])

    def evac_store(src_ps, m, out_ap, eng):
        o = outK[nK[0] % 2]
        nK[0] += 1
        if eng == "act":
            p.op("act", lambda e: e.activation(out=o.t[0:m, :], in_=src_ps.t[0:m, :], func=AF.Copy),
                 reads=[src_ps], writes=[o])
        else:
            p.op("dve", lambda e: e.tensor_copy(out=o.t[0:m, :], in_=src_ps.t[0:m, :]), reads=[src_ps], writes=[o])
        p.dma("sp", out_ap, o.t[0:m, :], reads=[o], writes=[])

    for j in range(TPC):
        t0 = j * 512
        for i in range(4):
            emit_rms_hT(p, c, x, t0 + i * 128, gcol, hT, i * 128, bufs=(c if i % 2 == 0 else c2))
        p.dma("sp", cosS.t[:], cos_d.t[:, t0:t0 + 512], reads=[cos_d], writes=[cosS])
        p.dma("sp", sinS.t[:], sin_d.t[:, t0:t0 + 512], reads=[sin_d], writes=[sinS])
        for b in range(11):
            w = wFs[b % 2]
            if j == 0:
                p.dma("pool", w.t[:], wF_d.t[b].rearrange("p (c n) -> p c n", c=KC), reads=[wF_d], writes=[w])
                p.dma("pool", wA_s.t[b].rearrange("p (c n) -> p c n", c=KC), w.t[:], reads=[w], writes=[wA_sb[b]])
            else:
                p.dma("sp", w.t[:], wA_s.t[b].rearrange("p (c n) -> p c n", c=KC), reads=[wA_sb[b]], writes=[w])
            if b < 10:
                bank = ps[b % 2]
                mm_group(p, bank, [(w.t[:, ch, :], hT.t[:, ch, :]) for ch in range(KC)], reads=[w, hT])
                if b < 2:
                    evac_store(bank, 128, kaT_o.t[b * 128:(b + 1) * 128, t0:t0 + 512], "act")
                elif b < 6:
                    evac_store(bank, 128, kcT_o.t[(b - 2) * 128:(b - 1) * 128, t0:t0 + 512], "dve")
                else:
                    c4 = b - 6
                    s = sq[c4 % 2]
                    p.op("dve", lambda e, c4=c4, bank=bank: e.tensor_copy(out=ckvT.t[:, c4, :], in_=bank.t[:, :]),
                         reads=[bank], writes=[ckvT])
                    p.op("act", lambda e, s=s, bank=bank: e.activation(out=s.t[:], in_=bank.t[:, :], func=AF.Square),
                         reads=[bank], writes=[s])
                    p.op("pe", lambda e, s=s, c4=c4: e.matmul(ps[2].t[:, :], ones.t[:], s.t[:], start=(c4 == 0),
                                                              stop=(c4 == 3)),
                         reads=[ones, s], writes=[ps[2]], inc=True)
            else:
                mm_group(p, ps[3], [(w.t[:, ch, 0:64], hT.t[:, ch, :]) for ch in range(KC)], m=64, reads=[w, hT])
                mm_group(p, ps[4], [(w.t[:, ch, 64:128], hT.t[:, ch, :]) for ch in range(KC)], m=64, reads=[w, hT])
                p.op("dve", lambda e: e.tensor_tensor(out=t1.t[:], in0=ps[3].t[0:64, :], in1=cosS.t[:], op=ALU.mult),
                     reads=[ps[3], cosS], writes=[t1])
                p.op("dve", lambda e: e.tensor_tensor(out=t2.t[:], in0=ps[4].t[0:64, :], in1=sinS.t[:], op=ALU.mult),
                     reads=[ps[4], sinS], writes=[t2])
                o = outK[nK[0] % 2]
                nK[0] += 1
                p.op("pool", lambda e, o=o: e.tensor_tensor(out=o.t[0:64, :], in0=t1.t[:], in1=t2.t[:], op=ALU.add),
                     reads=[t1, t2], writes=[o])
                p.dma("sp", krT_o.t[:, t0:t0 + 512], o.t[0:64, :], reads=[o], writes=[])
        p.op("act", lambda e: e.activation(out=rstdb.t[:], in_=ps[2].t[:, :], func=AF.Sqrt, scale=1.0 / 512, bias=c.epsc.t[:, 0:1]),
             reads=[ps[2], c.epsc], writes=[rstdb])
        p.op("dve", lambda e: e.reciprocal(out=rstdb.t[:], in_=rstdb.t[:]), reads=[rstdb], writes=[rstdb])
        for c4 in range(4):
            p.op("dve", lambda e, c4=c4: e.scalar_tensor_tensor(out=ckvn.t[:, c4, :], in0=ckvT.t[:, c4, :],
                                                                 scalar=gkv.t[:, c4:c4 + 1], in1=rstdb.t[:],
                                                                 op0=ALU.mult, op1=ALU.mult),
                 reads=[ckvT, gkv, rstdb], writes=[ckvn])
        for h in range(4):
            bank = ps[h % 2]
            mm_group(p, bank, [(wkvF.t[:, h, ch, :], ckvn.t[:, ch, :]) for ch in range(4)], reads=[wkvF, ckvn])
            evac_store(bank, 128, knT_o.t[h * 128:(h + 1) * 128, t0:t0 + 512], "act" if h % 2 else "dve")
        for i in range(4):
            r0 = t0 + i * 128
            for (wt, n, dst, kc_, src) in ((wva, 256, va_o, KC, hT), (wvc, 512, vc_o, KC, hT), (wkvT, 512, vb_o, 4, ckvn)):
                mm_group(p, ps[7], [(src.t[:, ch, i * 128:(i + 1) * 128], wt.t[:, ch, :]) for ch in range(kc_)],
                         n1=n, reads=[wt, src])
                o = outV[nV[0] % 2]
                nV[0] += 1
                p.op("act" if nV[0] % 2 else "dve",
                     (lambda e, o=o, n=n: e.activation(out=o.t[:, 0:n], in_=ps[7].t[:, 0:n], func=AF.Copy))
                     if nV[0] % 2 else
                     (lambda e, o=o, n=n: e.tensor_copy(out=o.t[:, 0:n], in_=ps[7].t[:, 0:n])),
                     reads=[ps[7]], writes=[o])
                p.dma("sp", dst.t[r0:r0 + 128, :], o.t[:, 0:n], reads=[o], writes=[])
    p.finish()
    es.close()
    nc._nops = p.nops
    return nc


def f_layout(w):
    Din, N = w.shape
    KC, nb = Din // 128, N // 128
    return np.ascontiguousarray(w.reshape(KC, 128, nb, 128).transpose(2, 1, 0, 3).reshape(nb, 128, KC * 128))


def t_layout(w):
    Din, N = w.shape
    KC = Din // 128
    return np.ascontiguousarray(w.reshape(KC, 128, N).transpose(1, 0, 2).reshape(128, KC * N))


def col_layout(g):
    return np.ascontiguousarray(g.reshape(-1, 128).T)


def in_offsets(D):
    sizes = (1024, 256, 256, 1024, 1024, 512, 64, 512, 512, 512, 512, 512, D, D, D)
    names = ("qa", "ka", "va", "za", "cq", "ckv", "kr", "zb", "qc", "kc", "vc", "zc", "ga", "gb", "gc")
    off = {}
    o = 0
    for n, s in zip(names, sizes):
        off[n] = (o, o + s)
        o += s
    return off


def own_blocks(core, TPC):
    return [32 * j + 8 * i + core for j in range(TPC) for i in range(4)]


def own_rows(core, TPC):
    return np.concatenate([np.arange(gb * 128, gb * 128 + 128) for gb in own_blocks(core, TPC)])


def rope_tables(rows):
    pos = rows.astype(np.float32)
    inv = (np.float32(10000.0) ** (-np.arange(0, 64, 2, dtype=np.float32) / np.float32(64))).astype(np.float32)
    ang = (pos[:, None] * inv[None, :]).astype(np.float32)
    cs, sn = np.cos(ang).astype(np.float32), np.sin(ang).astype(np.float32)
    C = np.concatenate([cs, cs], axis=1).T
    S = np.concatenate([-sn, sn], axis=1).T
    return np.ascontiguousarray(C), np.ascontiguousarray(S)


SWAP64 = np.concatenate([np.arange(32, 64), np.arange(0, 32)])


def prep_A(D, TPC, x_rows, lw, core):
    off = in_offsets(D)
    W = lw["w_in"]
    sl = lambda n: W[:, off[n][0]:off[n][1]]
    kr = sl("kr")
    wF = f_layout(np.concatenate([sl("ka"), sl("kc"), sl("ckv"), kr, kr[:, SWAP64]], axis=1))
    Wkv = lw["w_kv_up"].reshape(512, 4, 2, 128)
    wkvF = f_layout(np.ascontiguousarray(Wkv[:, :, 0, :]).reshape(512, 512))
    wkvF = np.ascontiguousarray(wkvF.transpose(1, 0, 2).reshape(128, 4 * 4 * 128))
    wkvT = t_layout(np.ascontiguousarray(Wkv[:, :, 1, :]).reshape(512, 512))
    C, S = rope_tables(own_rows(core, TPC))
    return {
        "x": x_rows, "gcol": col_layout(lw["norm_g"]), "wF": wF, "wva": t_layout(sl("va")), "wvc": t_layout(sl("vc")),
        "gkv": col_layout(lw["g_kv_lora"]), "wkvF": wkvF, "wkvT": wkvT, "cosT": C, "sinT": S,
        "ident": np.eye(128, dtype=np.float32),
    }


QB = 2
KCH = 4


def build_B(D, NB, last, limit=None):
    KC = D // 128
    TW = QB * 128
    TPC = NB // (8 * QB)
    NT = TPC * TW
    NKB = ((NB + PAD + KCH - 1) // KCH) * KCH
    NK = NKB * 128
    nc = bass.Bass("TRN2", target_bir_lowering=False)
    es = ExitStack()
    p = Prog(nc, es)
    p.limit = limit
    c = Ctx()
    c.D, c.KC = D, KC
    I = lambda n, s, dt=F32: p.dram(n, s, dt, "ExternalInput")
    x = I("x", [NT, D])
    gcol_d = I("gcol", [128, KC])
    cos_d = I("cosT", [64, NT])
    sin_d = I("sinT", [64, NT])
    ident_d = I("ident", [128, 128])
    kaT_d = I("kaTp", [256, NK], BF16)
    va_d = I("vap", [NK, 256], BF16)
    knT_d = I("knTp", [512, NK], BF16)
    krT_d = I("krTp", [64, NK], BF16)
    vb_d = I("vbp", [NK, 512], BF16)
    kcT_d = I("kcTp", [512, NK], BF16)
    vc_d = I("vcp", [NK, 512], BF16)
    bcol_d = I("biascols", [128, 8])
    biasT_d = I("swabias", [128, 8 * 2 * 128])
    sinks_d = I("sinks", [128, 8])
    msk_d = I("masks", [128, 6 * 128])
    wF_d = I("wF", [36 + 3 * KC, 128, KC * 128])
    gq_d = I("gq", [128, 8])
    wq_d = I("wq", [4, 128, 8 * 256])
    wpa_d = I("wpa", [KC, 128, 8 * 128])
    wpb_d = I("wpb", [KC, 128, 4 * 128])
    wpc_d = I("wpc", [KC, 128, 4 * 128])
    wout_d = I("wout", [KC, 128, KC * 128])
    fg_d = I("fgb", [128, D])
    out_o = p.dram("xout", [NT, D], F32, "ExternalOutput")
    NWF = 36 + 3 * KC
    wF_s = p.dram("wF_s", [NWF, 128, KC * 128], BF16, "Internal")
    wF_sb = [T(wF_s.t) for _ in range(NWF)]
    wq_s = p.dram("wq_s", [4, 128, 8 * 256], BF16, "Internal")
    wq_sb = [T(wq_s.t) for _ in range(4)]
    wp_s = p.dram("wp_s", [KC, 128, 16 * 128], BF16, "Internal")
    wp_sb = [T(wp_s.t) for _ in range(KC)]
    wo_s = p.dram("wo_s", [KC, 128, KC * 128], BF16, "Internal")
    wo_sb = [T(wo_s.t) for _ in range(KC)]
    cur = {"j": 0}

    def wload(w, shp, src_d, src_ap, s_d, s_sb, blk):
        sap = s_d.t[blk].rearrange(shp[0], **shp[1])
        if cur["j"] == 0:
            p.dma("pool", w.t[:], src_ap, reads=[src_d], writes=[w])
            p.dma("pool", sap, w.t[:], reads=[w], writes=[s_sb[blk]])
        else:
            p.dma("sp", w.t[:], sap, reads=[s_sb[blk]], writes=[w])

    ps = [p.psum() for _ in range(8)]
    c.ps_tr = [ps[6], ps[7]]
    c.xin = p.sb([128, D], F32, "xin")
    c.xs = p.sb([128, D], F32, "xs")
    c.ss = p.sb([128, 1], F32, "ss")
    c.rs = p.sb([128, 1], F32, "rs")
    c.ident = p.sb([128, 128], F32, "ident")
    c.epsc = p.sb([128, 1], F32, "epsc")
    p.op("pool", lambda e: e.memset(c.epsc.t[:], EPS), writes=[c.epsc])
    gcol = p.sb([128, KC], F32, "gcol")
    gq = p.sb([128, 8], F32, "gq")
    ones = p.sb([128, 128], F32, "ones")
    negones = p.sb([128, 128], F32, "negones")
    onesb = p.sb([128, 128], BF16, "onesb")
    bcol = p.sb([128, 8], F32, "bcol")
    biasT = p.sb([128, 8, 2, 128], F32, "biasT")
    sinks = p.sb([128, 8], F32, "sinks")
    msk = p.sb([128, 6, 128], F32, "msk")
    mskb = p.sb([128, 2, 128], BF16, "mskb")
    maskA = p.sb([128, 2, 4, 128], BF16, "maskA")
    esink = p.sb([128, 2, 4, 128], F32, "esink")
    hT = p.sb([128, KC, TW], BF16, "hT")
    wFs = [p.sb([128, KC, 128], BF16, "wFs") for _ in range(3)]
    wqs = [p.sb([128, 8, 256], BF16, "wqs") for _ in range(1)]
    wps = [p.sb([128, 16, 128], BF16, "wps") for _ in range(2)]
    qaT = p.sb([128, 8, TW], BF16, "qaT")
    cqT = p.sb([128, 8, TW], F32, "cqT")
    cqn = p.sb([128, 8, TW], BF16, "cqn")
    qcT = p.sb([128, 4, TW], BF16, "qcT")
    qnT = p.sb([128, 4, TW], BF16, "qnT")
    qrT = p.sb([64, 4, TW], BF16, "qrT")
    yT = p.sb([128, 16, TW], BF16, "yT")
    mT = p.sb([128, KC, TW], BF16, "mT")
    sq = [p.sb([128, TW], F32, "sq") for _ in range(2)]
    rstdq = p.sb([128, TW], F32, "rstdq")
    cosS = p.sb([64, TW], F32, "cos")
    sinS = p.sb([64, TW], F32, "sin")
    t1 = p.sb([64, TW], F32, "t1")
    t2 = p.sb([64, TW], F32, "t2")
    assert QB == 2
    kch = [p.sb([128, KCH * 128], BF16, "kch") for _ in range(8)]
    krch = [p.sb([64, KCH * 128], BF16, "krch") for _ in range(2)]
    vch = [p.sb([128, KCH, 128], BF16, "vch") for _ in range(8)]
    PtP = [p.sb([128, 2 * TW], BF16, "PtP") for _ in range(4)]
    eeP = [p.sb([128, 2 * TW], F32, "eeP") for _ in range(2)]
    spP = [p.sb([128, 2 * TW], BF16, "spP") for _ in range(2)]
    LaP = [p.sb([128, 2 * TW], F32, "LaP") for _ in range(2)]
    LbP = [p.sb([128, 2 * TW], BF16, "LbP") for _ in range(2)]
    rdp = [p.sb([128, 2 * TW], F32, "rdp") for _ in range(2)]
    mskb2 = p.sb([128, 2, 2, 128], BF16, "mskb2")
    Pt, ef, rden = PtP, eeP, rdp[0]
    negIUb = p.sb([128, 128], BF16, "negIUb")
    negonesb = p.sb([128, 128], BF16, "negonesb")

    sg = [p.sb([128, TW], F32, "sg") for _ in range(2)]
    acc = [p.sb([128, TW], F32, "acc") for _ in range(2)]
    fgp = [p.sb([128, 256], F32, "fgp") for _ in range(2)]
    cnt = {"ch": 0, "P": 0, "e": 0, "s": 0, "wf": 0, "sg": 0, "kr": 0}

    load_const(p, c.ident, ident_d)
    load_const(p, gcol, gcol_d)
    load_const(p, gq, gq_d)
    load_const(p, bcol, bcol_d)
    load_const(p, sinks, sinks_d)
    p.dma("sp", biasT.t[:], biasT_d.t.rearrange("p (h t q) -> p h t q", h=8, t=2), reads=[biasT_d], writes=[biasT])
    p.dma("sp", msk.t[:], msk_d.t.rearrange("p (m q) -> p m q", m=6), reads=[msk_d], writes=[msk])
    p.op("pool", lambda e: e.memset(ones.t[:], 1.0), writes=[ones])
    p.op("pool", lambda e: e.memset(negones.t[:], -1.0), writes=[negones])
    p.op("pool", lambda e: e.memset(onesb.t[:], 1.0), writes=[onesb])
    p.op("dve", lambda e: e.tensor_copy(out=mskb.t[:], in_=msk.t[:, 0:2, :]), reads=[msk], writes=[mskb])
    for t_ in range(2):
        for hh in range(2):
            p.op("dve", lambda e, t_=t_, hh=hh: e.tensor_copy(out=mskb2.t[:, t_, hh, :], in_=msk.t[:, t_, :]), reads=[msk], writes=[mskb2])
    for t_ in range(2):
        for hh in range(4):
            p.op("dve", lambda e, t_=t_, hh=hh: e.tensor_copy(out=maskA.t[:, t_, hh, :], in_=msk.t[:, 2 + t_, :]),
                 reads=[msk], writes=[maskA])
    p.op("act", lambda e: e.activation(out=sinks.t[:], in_=sinks.t[:], func=AF.Exp), reads=[sinks], writes=[sinks])
    for h in range(8):
        p.op("dve", lambda e, h=h: e.tensor_scalar(out=esink.t[:, h // 4, h % 4, :], in0=ones.t[:], scalar1=sinks.t[:, h:h + 1],
                                                   scalar2=None, op0=ALU.mult), reads=[ones, sinks], writes=[esink])
    p.op("dve", lambda e: e.tensor_copy(out=negIUb.t[:], in_=msk.t[:, 4, :]), reads=[msk], writes=[negIUb])
    p.op("pool", lambda e: e.memset(negonesb.t[:], -1.0), writes=[negonesb])

    def fproj(blk, evac, m=128):
        w = wFs[cnt["wf"] % 3]
        bank = ps[cnt["wf"] % 2]
        cnt["wf"] += 1
        wload(w, ("p (c n) -> p c n", dict(c=KC)), wF_d, wF_d.t[blk].rearrange("p (c n) -> p c n", c=KC), wF_s, wF_sb, blk)
        mm_group(p, bank, [(w.t[:, ch, :], hT.t[:, ch, :]) for ch in range(KC)], n1=TW, reads=[w, hT])
        evac(bank)

    def load_chunk(src_kT, row0, src_v, col0, ck):
        k = cnt["ch"] % 8
        cnt["ch"] += 1
        kt, vt = kch[k], vch[k]
        k0 = ck * KCH * 128
        p.dma("sp", kt.t[:], src_kT.t[row0:row0 + 128, k0:k0 + KCH * 128], reads=[src_kT], writes=[kt])
        p.dma("sp", vt.t[:], src_v.t[k0:k0 + KCH * 128, col0:col0 + 128].rearrange("(b p) d -> p b d", p=128),
              reads=[src_v], writes=[vt])
        return kt, vt

    def bias_ap(kbp):
        return bcol.t[:, (kbp + 7):(kbp + 8)] if kbp < 0 else bcol.t[:, 7:8]

    for j in range(TPC):
        cur["j"] = j
        t0 = j * TW
        qk = [8 * QB * j + 8 * i for i in range(QB)]
        idx_max = qk[-1] + PAD

        def col0_of(kbp):
            for i in range(QB):
                if qk[i] >= kbp:
                    return i
            raise AssertionError

        for i in range(QB):
            emit_rms_hT(p, c, x, t0 + i * 128, gcol, hT, i * 128)
        p.dma("sp", cosS.t[:], cos_d.t[:, t0:t0 + TW], reads=[cos_d], writes=[cosS])
        p.dma("sp", sinS.t[:], sin_d.t[:, t0:t0 + TW], reads=[sin_d], writes=[sinS])

        for b in range(8):
            fproj(b, lambda bank, b=b: p.op("act", lambda e: e.activation(out=qaT.t[:, b, :], in_=bank.t[:, 0:TW], func=AF.Copy),
                                            reads=[bank], writes=[qaT]))
        for b in range(8):
            def ev(bank, b=b):
                s = sq[b % 2]
                p.op("dve", lambda e: e.tensor_copy(out=cqT.t[:, b, :], in_=bank.t[:, 0:TW]), reads=[bank], writes=[cqT])
                p.op("act", lambda e: e.activation(out=s.t[:], in_=cqT.t[:, b, :], func=AF.Square), reads=[cqT], writes=[s])
                p.op("pe", lambda e: e.matmul(ps[2].t[:, 0:TW], ones.t[:], s.t[:], start=(b == 0), stop=(b == 7)),
                     reads=[ones, s], writes=[ps[2]], inc=True)
            fproj(8 + b, ev)
        p.op("act", lambda e: e.activation(out=rstdq.t[:], in_=ps[2].t[:, 0:TW], func=AF.Sqrt, scale=1.0 / 1024, bias=c.epsc.t[:, 0:1]),
             reads=[ps[2], c.epsc], writes=[rstdq])
        p.op("dve", lambda e: e.reciprocal(out=rstdq.t[:], in_=rstdq.t[:]), reads=[rstdq], writes=[rstdq])
        for b in range(8):
            p.op("dve", lambda e, b=b: e.scalar_tensor_tensor(out=cqn.t[:, b, :], in0=cqT.t[:, b, :], scalar=gq.t[:, b:b + 1],
                                                              in1=rstdq.t[:], op0=ALU.mult, op1=ALU.mult),
                 reads=[cqT, gq, rstdq], writes=[cqn])
        for b in range(4):
            fproj(16 + b, lambda bank, b=b: p.op("act", lambda e: e.activation(out=qcT.t[:, b, :], in_=bank.t[:, 0:TW], func=AF.Copy,
                                                                               scale=128.0 ** -0.5), reads=[bank], writes=[qcT]))
        for h in range(4):
            w = wqs[0]
            wload(w, ("p (c n) -> p c n", dict(c=8)), wq_d, wq_d.t[h].rearrange("p (c n) -> p c n", c=8), wq_s, wq_sb, h)
            mm_group(p, ps[3], [(w.t[:, ch, 0:128], cqn.t[:, ch, :]) for ch in range(8)], n1=TW, reads=[w, cqn])
            mm_group(p, ps[4], [(w.t[:, ch, 128:192], cqn.t[:, ch, :]) for ch in range(8)], n1=TW, m=64, reads=[w, cqn])
            mm_group(p, ps[5], [(w.t[:, ch, 192:256], cqn.t[:, ch, :]) for ch in range(8)], n1=TW, m=64, reads=[w, cqn])
            p.op("act", lambda e, h=h: e.activation(out=qnT.t[:, h, :], in_=ps[3].t[:, 0:TW], func=AF.Copy), reads=[ps[3]], writes=[qnT])
            p.op("dve", lambda e: e.tensor_tensor(out=t1.t[:], in0=ps[4].t[0:64, 0:TW], in1=cosS.t[:], op=ALU.mult),
                 reads=[ps[4], cosS], writes=[t1])
            p.op("dve", lambda e: e.tensor_tensor(out=t2.t[:], in0=ps[5].t[0:64, 0:TW], in1=sinS.t[:], op=ALU.mult),
                 reads=[ps[5], sinS], writes=[t2])
            p.op("pool", lambda e, h=h: e.tensor_tensor(out=qrT.t[:, h, :], in0=t1.t[:], in1=t2.t[:], op=ALU.add),
                 reads=[t1, t2], writes=[qrT])

        for i in range(QB):
            for g in range(2):
                k = cnt["ch"] % 8
                cnt["ch"] += 1
                kt, vt = kch[k], vch[k]
                k0 = (qk[i] + PAD - 1) * 128
                p.dma("sp", kt.t[:, 0:256], kaT_d.t[g * 128:(g + 1) * 128, k0:k0 + 256], reads=[kaT_d], writes=[kt])
                p.dma("sp", vt.t[:, 0:2, :], va_d.t[k0:k0 + 256, g * 128:(g + 1) * 128].rearrange("(b p) d -> p b d", p=128),
                      reads=[va_d], writes=[vt])
                q4 = qaT.t[:, 4 * g:4 * g + 4, i * 128:(i + 1) * 128]
                for t_ in range(2):
                    sb_ = ps[t_]
                    s3 = sb_.t[:, 0:512].rearrange("p (h q) -> p h q", h=4)
                    p.op("pe", lambda e, t_=t_, s3=s3: e.matmul(s3, kt.t[:, t_ * 128:(t_ + 1) * 128], q4, start=True, stop=True),
                         reads=[kt, qaT], writes=[sb_])
                    ee = ef[cnt["e"] % 2]
                    cnt["e"] += 1
                    e3 = ee.t[:, 0:512].rearrange("p (h q) -> p h q", h=4)
                    p.op("dve", lambda e, s3=s3, e3=e3, t_=t_: e.scalar_tensor_tensor(
                        out=e3, in0=s3, scalar=128.0 ** -0.5, in1=biasT.t[:, 4 * g:4 * g + 4, t_, :], op0=ALU.mult, op1=ALU.add),
                         reads=[sb_, biasT], writes=[ee])
                    P = Pt[cnt["P"] % 2]
                    cnt["P"] += 1
                    bap = bias_ap(qk[i] - 1) if t_ == 0 else bcol.t[:, 7:8]
                    p.op("act", lambda e, P=P, ee=ee, bap=bap: e.activation(out=P.t[:, 0:512], in_=ee.t[:, 0:512], func=AF.Exp, bias=bap),
                         reads=[ee, bcol], writes=[P])
                    P3 = P.t[:, 0:512].rearrange("p (h q) -> p h q", h=4)
                    p.op("dve", lambda e, P3=P3, t_=t_: e.tensor_tensor(out=P3, in0=P3, in1=maskA.t[:, t_, :, :], op=ALU.mult),
                         reads=[P, maskA], writes=[P])
                    p.op("pe", lambda e, P=P, t_=t_: e.matmul(ps[2].t[:, 0:512], vt.t[:, t_, :], P.t[:, 0:512], start=(t_ == 0), stop=(t_ == 1)),
                         reads=[vt, P], writes=[ps[2]], inc=(t_ == 1))
                    p.op("pe", lambda e, P=P, t_=t_: e.matmul(ps[3].t[:, 0:512], onesb.t[:], P.t[:, 0:512], start=(t_ == 0), stop=(t_ == 1)),
                         reads=[onesb, P], writes=[ps[3]], inc=(t_ == 1))
                p.op("dve", lambda e, g=g: e.tensor_tensor(out=rden.t[:, 0:512], in0=ps[3].t[:, 0:512],
                                                           in1=esink.t[:, g, :, :].rearrange("p h q -> p (h q)"), op=ALU.add),
                     reads=[ps[3], esink], writes=[rden])
                p.op("dve", lambda e: e.reciprocal(out=rden.t[:, 0:512], in_=rden.t[:, 0:512]), reads=[rden], writes=[rden])
                p.op("dve", lambda e, g=g, i=i: e.tensor_tensor(
                    out=yT.t[:, 4 * g:4 * g + 4, i * 128:(i + 1) * 128], in0=ps[2].t[:, 0:512].rearrange("p (h q) -> p h q", h=4),
                    in1=rden.t[:, 0:512].rearrange("p (h q) -> p h q", h=4), op=ALU.mult),
                     reads=[ps[2], rden], writes=[yT])

        sclB = 192.0 ** -0.5
        for bk in (ps[4], ps[5], ps[6], ps[7]):
            p.op("dve", lambda e: e.memset(bk.t[:, :], 0.0), writes=[bk])
        cacheB = {}

        def chunksB(ck):
            if ck not in cacheB:
                k0 = ck * KCH * 128
                krt = krch[cnt["kr"] % 2]
                cnt["kr"] += 1
                p.dma("sp", krt.t[:], krT_d.t[:, k0:k0 + KCH * 128], reads=[krT_d], writes=[krt])
                cacheB[ck] = (krt, [load_chunk(knT_d, h * 128, vb_d, h * 128, ck) for h in range(4)])
            return cacheB[ck]

        def binfo(idx):
            kbp = idx - PAD
            i0 = col0_of(kbp)
            return kbp, i0 * 128, (kbp == qk[i0])

        def v3(t, n0):
            return t.t[:, 0:2 * TW].rearrange("p (h q) -> p h q", h=2)[:, :, n0:TW]

        def scoreB(idx, pr):
            kbp, n0, diag = binfo(idx)
            krt, chunks = chunksB(idx // KCH)
            bl = idx % KCH
            sb_ = ps[2 * pr + idx % 2]
            for hh in range(2):
                h = 2 * pr + hh
                kt, vt = chunks[h]
                oc = hh * TW
                p.op("pe", lambda e: e.matmul(sb_.t[:, oc + n0:oc + TW], kt.t[:, bl * 128:(bl + 1) * 128], qnT.t[:, h, n0:TW],
                                              start=True, stop=False), reads=[kt, qnT], writes=[sb_], inc=False)
                p.op("pe", lambda e: e.matmul(sb_.t[:, oc + n0:oc + TW], krt.t[0:64, bl * 128:(bl + 1) * 128], qrT.t[0:64, h, n0:TW],
                                              start=False, stop=True), reads=[krt, qrT], writes=[sb_], inc=(hh == 1))

        for pr in range(2):
            scoreB(0, pr)
        for idx in range(idx_max + 1):
            kbp, n0, diag = binfo(idx)
            lastb = (idx == idx_max)
            krt, chunks = chunksB(idx // KCH)
            bl = idx % KCH
            for pr in range(2):
                if not lastb:
                    scoreB(idx + 1, pr)
                sb_ = ps[2 * pr + idx % 2]
                P = PtP[2 * pr + idx % 2]
                if kbp < 0:
                    bap = bias_ap(kbp)
                    p.op("act", lambda e: e.activation(out=v3(P, n0), in_=v3(sb_, n0), func=AF.Exp, scale=sclB, bias=bap),
                         reads=[sb_, bcol], writes=[P])
                else:
                    p.op("act", lambda e: e.activation(out=v3(P, n0), in_=v3(sb_, n0), func=AF.Exp, scale=sclB),
                         reads=[sb_], writes=[P])
                if diag:
                    pd = P.t[:, 0:2 * TW].rearrange("p (h q) -> p h q", h=2)[:, :, n0:n0 + 128]
                    p.op("dve", lambda e: e.tensor_tensor(out=pd, in0=pd, in1=mskb2.t[:, 0, :, :], op=ALU.mult), reads=[P, mskb2], writes=[P])
                ob, db = ps[4 + pr], ps[6 + pr]
                for hh in range(2):
                    h = 2 * pr + hh
                    kt, vt = chunks[h]
                    oc = hh * TW
                    p.op("pe", lambda e: e.matmul(ob.t[:, oc + n0:oc + TW], vt.t[:, bl, :], P.t[:, oc + n0:oc + TW], start=False, stop=lastb,
                                                  skip_group_check=True), reads=[vt, P], writes=[ob], inc=False)
                    p.op("pe", lambda e: e.matmul(db.t[:, oc + n0:oc + TW], onesb.t[:], P.t[:, oc + n0:oc + TW], start=False, stop=lastb,
                                                  skip_group_check=True), reads=[onesb, P], writes=[db], inc=(hh == 1))
        for pr in range(2):
            ob, db = ps[4 + pr], ps[6 + pr]
            rd = rdp[pr]
            p.op("dve", lambda e: e.reciprocal(out=rd.t[:, :], in_=db.t[:, 0:2 * TW]), reads=[db], writes=[rd])
            p.op("dve", lambda e: e.tensor_tensor(out=yT.t[:, 8 + 2 * pr:10 + 2 * pr, :], in0=ob.t[:, 0:2 * TW].rearrange("p (h q) -> p h q", h=2),
                                                  in1=rd.t[:, :].rearrange("p (h q) -> p h q", h=2), op=ALU.mult),
                 reads=[ob, rd], writes=[yT])

        for pr in range(2):
            p.op("dve", lambda e: e.memset(LaP[pr].t[:], 0.0), writes=[LaP[pr]])
            p.op("dve", lambda e: e.memset(LbP[pr].t[:], 0.0), writes=[LbP[pr]])
        for bk in (ps[4], ps[5]):
            p.op("dve", lambda e: e.memset(bk.t[:, :], 0.0), writes=[bk])
        cacheC = {}

        def chunksC(ck):
            if ck not in cacheC:
                cacheC[ck] = [load_chunk(kcT_d, h * 128, vc_d, h * 128, ck) for h in range(4)]
            return cacheC[ck]

        def zC(idx):
            kbp, n0, diag = binfo(idx)
            chunks = chunksC(idx // KCH)
            bl = idx % KCH
            for h in range(4):
                kt, vt = chunks[h]
                zb = ps[h // 2]
                oc = (h % 2) * TW
                p.op("pe", lambda e: e.matmul(zb.t[:, oc + n0:oc + TW], kt.t[:, bl * 128:(bl + 1) * 128], qcT.t[:, h, n0:TW], start=True, stop=True),
                     reads=[kt, qcT], writes=[zb], inc=(h % 2 == 1))

        def spC(idx):
            kbp, n0, diag = binfo(idx)
            bap = bias_ap(kbp) if kbp < 0 else 0.0
            for pr in range(2):
                zb, ee, sp_ = ps[pr], eeP[pr], spP[pr]
                p.op("act", lambda e: e.activation(out=v3(ee, n0), in_=v3(zb, n0), func=AF.Exp, bias=bap), reads=[zb, bcol], writes=[ee])
                p.op("act", lambda e: e.activation(out=v3(sp_, n0), in_=v3(ee, n0), func=AF.Ln, bias=1.0), reads=[ee], writes=[sp_])
                if diag:
                    sd = sp_.t[:, 0:2 * TW].rearrange("p (h q) -> p h q", h=2)[:, :, n0:n0 + 128]
                    p.op("dve", lambda e: e.tensor_tensor(out=sd, in0=sd, in1=mskb2.t[:, 1, :, :], op=ALU.mult), reads=[sp_, mskb2], writes=[sp_])

        zC(idx_max)
        spC(idx_max)
        for idx in range(idx_max, -1, -1):
            kbp, n0, diag = binfo(idx)
            lastb = (idx == 0)
            chunks = chunksC(idx // KCH)
            bl = idx % KCH
            bap = bias_ap(kbp) if kbp < 0 else 0.0
            for h in range(4):
                kt, vt = chunks[h]
                ab, sp_, Lb = ps[2 + h // 2], spP[h // 2], LbP[h // 2]
                oc = (h % 2) * TW
                kap = kt.t[:, bl * 128:(bl + 1) * 128]
                p.op("pe", lambda e: e.matmul(ab.t[:, oc + n0:oc + TW], kap, qcT.t[:, h, n0:TW], start=True, stop=False),
                     reads=[kt, qcT], writes=[ab], inc=False)
                p.op("pe", lambda e: e.matmul(ab.t[:, oc + n0:oc + TW], negIUb.t[:], sp_.t[:, oc + n0:oc + TW], start=False, stop=False),
                     reads=[negIUb, sp_], writes=[ab], inc=False)
                p.op("pe", lambda e: e.matmul(ab.t[:, oc + n0:oc + TW], negonesb.t[:], Lb.t[:, oc + n0:oc + TW], start=False, stop=True),
                     reads=[negonesb, Lb], writes=[ab], inc=(h % 2 == 1))
            if not lastb:
                zC(idx - 1)
            for pr in range(2):
                ab, sp_, La, Lb = ps[2 + pr], spP[pr], LaP[pr], LbP[pr]
                P = PtP[2 * pr + idx % 2]
                p.op("act", lambda e: e.activation(out=v3(P, n0), in_=v3(ab, n0), func=AF.Exp, bias=bap), reads=[ab, bcol], writes=[P])
                if diag:
                    pd = P.t[:, 0:2 * TW].rearrange("p (h q) -> p h q", h=2)[:, :, n0:n0 + 128]
                    p.op("dve", lambda e: e.tensor_tensor(out=pd, in0=pd, in1=mskb2.t[:, 1, :, :], op=ALU.mult), reads=[P, mskb2], writes=[P])
                p.op("dve", lambda e: e.tensor_tensor(out=v3(La, n0), in0=v3(La, n0), in1=v3(sp_, n0), op=ALU.add), reads=[La, sp_], writes=[La])
                p.op("dve", lambda e: e.tensor_copy(out=v3(Lb, n0), in_=v3(La, n0)), reads=[La], writes=[Lb])
            if not lastb:
                spC(idx - 1)
            for h in range(4):
                kt, vt = chunks[h]
                ob = ps[4 + h // 2]
                oc = (h % 2) * TW
                P = PtP[2 * (h // 2) + idx % 2]
                p.op("pe", lambda e: e.matmul(ob.t[:, oc + n0:oc + TW], vt.t[:, bl, :], P.t[:, oc + n0:oc + TW], start=False, stop=lastb,
                                              skip_group_check=True), reads=[vt, P], writes=[ob], inc=(h % 2 == 1))
        for pr in range(2):
            ob = ps[4 + pr]
            p.op("act", lambda e: e.activation(out=yT.t[:, 12 + 2 * pr:14 + 2 * pr, :], in_=ob.t[:, 0:2 * TW].rearrange("p (h q) -> p h q", h=2),
                                               func=AF.Copy), reads=[ob], writes=[yT])

        for k in range(16):
            def ev(bank, k=k):
                s = sg[k % 2]
                p.op("act", lambda e: e.activation(out=s.t[:], in_=bank.t[:, 0:TW], func=AF.Silu), reads=[bank], writes=[s])
                p.op("dve", lambda e: e.tensor_tensor(out=yT.t[:, k, :], in0=yT.t[:, k, :], in1=s.t[:], op=ALU.mult),
                     reads=[yT, s], writes=[yT])
            fproj(20 + k, ev)

        for jb in range(KC):
            wp = wps[jb % 2]
            if j == 0:
                p.dma("pool", wp.t[:, 0:8, :], wpa_d.t[jb].rearrange("p (c n) -> p c n", c=8), reads=[wpa_d], writes=[wp])
                p.dma("pool", wp.t[:, 8:12, :], wpb_d.t[jb].rearrange("p (c n) -> p c n", c=4), reads=[wpb_d], writes=[wp])
                p.dma("pool", wp.t[:, 12:16, :], wpc_d.t[jb].rearrange("p (c n) -> p c n", c=4), reads=[wpc_d], writes=[wp])
                p.dma("pool", wp_s.t[jb].rearrange("p (c n) -> p c n", c=16), wp.t[:], reads=[wp], writes=[wp_sb[jb]])
            else:
                p.dma("sp", wp.t[:], wp_s.t[jb].rearrange("p (c n) -> p c n", c=16), reads=[wp_sb[jb]], writes=[wp])
            a_ = acc[jb % 2]
            for br, (k0, k1) in enumerate(((0, 8), (8, 12), (12, 16))):
                pb = ps[2 + br]
                mm_group(p, pb, [(wp.t[:, k, :], yT.t[:, k, :]) for k in range(k0, k1)], n1=TW, reads=[wp, yT])

                def ev(bank, br=br, pb=pb, a_=a_, jb=jb):
                    s = sg[cnt["sg"] % 2]
                    cnt["sg"] += 1
                    p.op("act", lambda e: e.activation(out=s.t[:], in_=bank.t[:, 0:TW], func=AF.Sigmoid), reads=[bank], writes=[s])
                    if br == 0:
                        p.op("dve", lambda e: e.tensor_tensor(out=a_.t[:], in0=pb.t[:, 0:TW], in1=s.t[:], op=ALU.mult),
                             reads=[pb, s], writes=[a_])
                    else:
                        p.op("dve", lambda e: e.tensor_tensor(out=s.t[:], in0=pb.t[:, 0:TW], in1=s.t[:], op=ALU.mult),
                             reads=[pb, s], writes=[s])
                        if br == 1:
                            p.op("pool", lambda e: e.tensor_tensor(out=a_.t[:], in0=a_.t[:], in1=s.t[:], op=ALU.add),
                                 reads=[a_, s], writes=[a_])
                        else:
                            p.op("pool", lambda e: e.tensor_tensor(out=mT.t[:, jb, :], in0=a_.t[:], in1=s.t[:], op=ALU.add),
                                 reads=[a_, s], writes=[mT])
                fproj(36 + br * KC + jb, ev)

        xn = [c.xin, c.xs]
        for i in range(QB):
            p.dma("sp", xn[i].t[:], x.t[t0 + i * 128:t0 + (i + 1) * 128, :], reads=[x], writes=[xn[i]])
        for n in range(KC):
            w = wFs[cnt["wf"] % 3]
            cnt["wf"] += 1
            wload(w, ("p (c n) -> p c n", dict(c=KC)), wout_d, wout_d.t[n].rearrange("p (c n) -> p c n", c=KC), wo_s, wo_sb, n)
            for i in range(QB):
                bank = ps[(n * QB + i) % 2]
                mm_group(p, bank, [(mT.t[:, ch, i * 128:(i + 1) * 128], w.t[:, ch, :]) for ch in range(KC)], n1=128, reads=[w, mT])
                p.op("dve", lambda e, bank=bank, i=i, n=n: e.tensor_tensor(out=xn[i].t[:, n * 128:(n + 1) * 128], in0=bank.t[:, 0:128],
                                                                          in1=xn[i].t[:, n * 128:(n + 1) * 128], op=ALU.add),
                     reads=[bank, xn[i]], writes=[xn[i]])
        for i in range(QB):
            if last:
                junk = hT.t[:, :, :].rearrange("p c t -> p (c t)")[:, 0:D]
                p.op("act", lambda e, i=i, junk=junk: e.activation(out=junk, in_=xn[i].t[:], func=AF.Square, accum_out=c.ss.t[:, 0:1]),
                     reads=[xn[i]], writes=[hT, c.ss])
                p.op("act", lambda e: e.activation(out=c.rs.t[:, 0:1], in_=c.ss.t[:, 0:1], func=AF.Sqrt, scale=1.0 / D, bias=c.epsc.t[:, 0:1]),
                     reads=[c.ss, c.epsc], writes=[c.rs])
                p.op("dve", lambda e: e.reciprocal(out=c.rs.t[:, 0:1], in_=c.rs.t[:, 0:1]), reads=[c.rs], writes=[c.rs])
                for k in range(0, D, 256):
                    f = fgp[(k // 256) % 2]
                    p.dma("sp", f.t[:], fg_d.t[:, k:k + 256], reads=[fg_d], writes=[f])
                    p.op("dve", lambda e, i=i, k=k, f=f: e.scalar_tensor_tensor(out=xn[i].t[:, k:k + 256], in0=xn[i].t[:, k:k + 256],
                                                                               scalar=c.rs.t[:, 0:1], in1=f.t[:], op0=ALU.mult, op1=ALU.mult),
                         reads=[xn[i], c.rs, f], writes=[xn[i]])
            p.dma("sp", out_o.t[t0 + i * 128:t0 + (i + 1) * 128, :], xn[i].t[:], reads=[xn[i]], writes=[])
    p.finish()
    es.close()
    nc._nops = p.nops
    return nc


def t5_bucket_np(rel):
    n = np.maximum(rel, 0)
    logn = np.log(np.maximum(n, 1).astype(np.float32) / np.float32(16))
    large = 16 + (logn / np.float32(np.log(128 / 16)) * np.float32(16)).astype(np.int32)
    large = np.minimum(large, 31)
    return np.where(n < 16, n, large)


def make_masks():
    r = np.arange(128)[:, None]
    cc = np.arange(128)[None, :]
    m = np.zeros((128, 6, 128), np.float32)
    m[:, 0] = (r <= cc)
    m[:, 1] = (r < cc)
    m[:, 2] = (r > cc)
    m[:, 3] = (r <= cc)
    m[:, 4] = -(r >= cc).astype(np.float32)
    return np.ascontiguousarray(m.reshape(128, 6 * 128))


def shifted(G, core, NB, axis):
    NKB = ((NB + PAD + KCH - 1) // KCH) * KCH
    tot = NKB + 8
    shp = list(G.shape)
    shp[axis] = tot * 128
    Gp = np.zeros(shp, G.dtype)
    sl = [slice(None)] * G.ndim
    sl[axis] = slice(PAD * 128, (PAD + NB) * 128)
    Gp[tuple(sl)] = G
    sl[axis] = slice(core * 128, (core + NKB) * 128)
    return np.ascontiguousarray(Gp[tuple(sl)])


def prep_B_shared(D, lw, rel_bias, final_g):
    off = in_offsets(D)
    W = lw["w_in"]
    sl = lambda n: W[:, off[n][0]:off[n][1]]
    wF = f_layout(np.concatenate([sl(n) for n in ("qa", "cq", "qc", "za", "zb", "zc", "ga", "gb", "gc")], axis=1))
    Wq = lw["w_q_up"].reshape(1024, 4, 192)
    wq = np.stack([t_layout(np.concatenate([Wq[:, h, :128], Wq[:, h, 128:], Wq[:, h, 128:][:, SWAP64]], axis=1)) for h in range(4)])
    s_ = np.arange(128)[:, None]
    q_ = np.arange(128)[None, :]
    sw = np.zeros((128, 8, 2, 128), np.float32)
    for t in range(2):
        rel = 128 + q_ - (t * 128 + s_)
        sw[:, :, t, :] = rel_bias[t5_bucket_np(rel)].transpose(0, 2, 1)
    return {
        "gcol": col_layout(lw["norm_g"]), "ident": np.eye(128, dtype=np.float32),
        "swabias": np.ascontiguousarray(sw.reshape(128, 8 * 2 * 128)),
        "sinks": np.ascontiguousarray(np.broadcast_to(lw["attn_sinks"][None, :], (128, 8))),
        "masks": make_masks(), "wF": wF, "gq": col_layout(lw["g_q_lora"]), "wq": wq,
        "wpa": f_layout(lw["w_proj_a"]), "wpb": f_layout(lw["w_proj_b"]), "wpc": f_layout(lw["w_proj_c"]),
        "wout": f_layout(lw["w_out"]), "fgb": np.ascontiguousarray(np.broadcast_to(final_g[None, :], (128, D))),
    }


def prep_B_core(D, NB, x_rows, glob, core, shared):
    TPC = NB // 32
    C, S = rope_tables(own_rows(core, TPC))
    bc = np.zeros((128, 8), np.float32)
    for t in range(7):
        if (t - 7) + core < 0:
            bc[:, t] = NEG
    m = dict(shared)
    m.update({
        "x": x_rows, "cosT": C, "sinT": S, "biascols": bc,
        "kaTp": shifted(glob["kaT"], core, NB, 1), "vap": shifted(glob["va"], core, NB, 0),
        "knTp": shifted(glob["knT"], core, NB, 1), "krTp": shifted(glob["krT"], core, NB, 1),
        "vbp": shifted(glob["vb"], core, NB, 0), "kcTp": shifted(glob["kcT"], core, NB, 1),
        "vcp": shifted(glob["vc"], core, NB, 0),
    })
    return m


def gather_kv(resA, NB):
    TPC = NB // 32
    glob = {}
    for k in ("kaT", "kcT", "knT", "krT", "va", "vc", "vb"):
        tm = not k.endswith("T")
        a0 = np.asarray(resA[0][k])
        shp = (NB * 128, a0.shape[1]) if tm else (a0.shape[0], NB * 128)
        G = np.zeros(shp, a0.dtype)
        for core in range(NCORES):
            a = np.asarray(resA[core][k])
            for m_, gb in enumerate(own_blocks(core, TPC)):
                if tm:
                    G[gb * 128:(gb + 1) * 128] = a[m_ * 128:(m_ + 1) * 128]
                else:
                    G[:, gb * 128:(gb + 1) * 128] = a[:, m_ * 128:(m_ + 1) * 128]
        glob[k] = G
    return glob


_CACHE = {}


def run_model(x2d, layers, rel_bias, final_g, D, SEQ):
    NB = SEQ // 128
    TPC = NB // 32
    rows = [own_rows(c, TPC) for c in range(NCORES)]
    xs = [np.ascontiguousarray(x2d[r]) for r in rows]
    cores = list(range(NCORES))
    for li, lw in enumerate(layers):
        last = (li == len(layers) - 1)
        if ("A", D, TPC) not in _CACHE:
            _CACHE[("A", D, TPC)] = build_A(D, TPC)
        if ("B", D, NB, last) not in _CACHE:
            _CACHE[("B", D, NB, last)] = build_B(D, NB, last)
        resA = run_bass_kernel_spmd(_CACHE[("A", D, TPC)], [prep_A(D, TPC, xs[c], lw, c) for c in cores], core_ids=cores).results
        glob = gather_kv(resA, NB)
        shared = prep_B_shared(D, lw, rel_bias, final_g)
        resB = run_bass_kernel_spmd(_CACHE[("B", D, NB, last)], [prep_B_core(D, NB, xs[c], glob, c, shared) for c in cores],
                                    core_ids=cores).results
        xs = [np.asarray(resB[c]["xout"]) for c in cores]
    out = np.zeros((SEQ, D), np.float32)
    for c in cores:
        out[rows[c]] = xs[c]
    return out


def kernel(x, norm_g, w_in, attn_sinks, rel_bias, g_q_lora, w_q_up, g_kv_lora, w_kv_up,
           w_proj_a, w_proj_b, w_proj_c, w_out, final_g):
    x = np.asarray(x, np.float32)
    B_, S_, D_ = x.shape
    names = ("norm_g", "w_in", "attn_sinks", "g_q_lora", "w_q_up", "g_kv_lora", "w_kv_up", "w_proj_a", "w_proj_b", "w_proj_c", "w_out")
    vals = (norm_g, w_in, attn_sinks, g_q_lora, w_q_up, g_kv_lora, w_kv_up, w_proj_a, w_proj_b, w_proj_c, w_out)
    depth = np.asarray(norm_g).shape[0]
    layers = [{n: np.asarray(v, np.float32)[l] for n, v in zip(names, vals)} for l in range(depth)]
    out = run_model(x[0], layers, np.asarray(rel_bias, np.float32), np.asarray(final_g, np.float32), D_, S_)
    return out[None].astype(np.float32)
```

```python
import numpy as np
import ml_dtypes
from contextlib import ExitStack
import concourse.bass as bass
import concourse.mybir as mybir
from concourse.bass_utils import run_bass_kernel_spmd

F32 = mybir.dt.float32
BF16 = mybir.dt.bfloat16
AF = mybir.ActivationFunctionType
ALU = mybir.AluOpType
AX = mybir.AxisListType
NEG = -30000.0
EPS = 1e-6
NCORES = 8
PAD = 7


class Buf:
    __slots__ = ("w", "r")

    def __init__(self):
        self.w = None
        self.r = {}


class T:
    def __init__(self, t, b=None, is_ps=False):
        self.t = t
        self.b = b or Buf()
        self.is_ps = is_ps


class Prog:
    KD = 6

    def __init__(self, nc, es):
        self.nc = nc
        self.es = es
        self.eng = {"pe": nc.tensor, "act": nc.scalar, "dve": nc.vector, "pool": nc.gpsimd, "sp": nc.sync}
        self.ops = {k: [] for k in self.eng}
        self.sems = {}
        self.cnt = {}
        self.seen = {k: {} for k in self.eng}
        for k in ("pe", "act", "dve", "pool"):
            self.sems[k] = es.enter_context(nc.semaphore("c_" + k))
            self.cnt[k] = 0
        self.ndma = {k: 0 for k in self.eng}
        for q in ("sp", "pool", "act"):
            for s in range(self.KD):
                key = ("d", q, s)
                self.sems[key] = es.enter_context(nc.semaphore("d_%s_%d" % (q, s)))
                self.cnt[key] = 0
        self.nt = 0
        self.nops = 0
        self.limit = None

    def _skip(self):
        self.nops += 1
        return self.limit is not None and self.nops > self.limit

    def sb(self, shape, dt, name=None):
        self.nt += 1
        t = self.es.enter_context(self.nc.sbuf_tensor("%s_%d" % (name or "t", self.nt), list(shape), dt))
        return T(t)

    def psum(self):
        self.nt += 1
        t = self.es.enter_context(self.nc.psum_tensor("ps_%d" % self.nt, [128, 512], F32))
        return T(t, is_ps=True)

    def dram(self, name, shape, dt, kind):
        return T(self.nc.dram_tensor(name, list(shape), dt, kind=kind).ap())

    def _deps(self, e, reads, writes, skip_self=False):
        need = {}

        def add(k, v):
            if skip_self and k == e:
                return
            if need.get(k, 0) < v:
                need[k] = v

        for b in reads:
            if b.w is not None:
                add(*b.w)
        for b in writes:
            if b.w is not None:
                add(*b.w)
            for k, v in b.r.items():
                add(k, v)
        out = []
        seen = self.seen[e]
        for k, v in need.items():
            if seen.get(k, 0) < v:
                seen[k] = v
                out.append((self.sems[k], v))
        return out

    def op(self, e, fn, reads=(), writes=(), inc=True):
        if self._skip():
            return
        writes = [x.b for x in writes] + [x.b for x in reads if x.is_ps and e != "pe"]
        reads = [x.b for x in reads if not (x.is_ps and e != "pe")]
        waits = self._deps(e, reads, writes, skip_self=(e == "pe"))
        if inc:
            self.cnt[e] += 1
            tok = (e, self.cnt[e])
        else:
            tok = (e, self.cnt[e] + 1)
        sem = self.sems[e]

        engobj = self.eng[e]
        for s, v in waits:
            engobj.wait_ge(s, v)
        ins = fn(engobj)
        if inc:
            ins.then_inc(sem, 1)
        for b in writes:
            b.w = tok
            b.r = {}
        for b in reads:
            if b.r.get(tok[0], 0) < tok[1]:
                b.r[tok[0]] = tok[1]

    def dma(self, q, out, in_, reads=(), writes=()):
        if self._skip():
            return
        reads = [x.b for x in reads]
        writes = [x.b for x in writes]
        n = self.ndma[q]
        self.ndma[q] += 1
        key = ("d", q, n % self.KD)
        waits = self._deps(q, reads, writes)
        prev = self.cnt[key]
        seen = self.seen[q]
        if prev > 0 and seen.get(key, 0) < prev:
            seen[key] = prev
            waits.append((self.sems[key], prev))
        self.cnt[key] += 16
        tok = (key, self.cnt[key])
        sem = self.sems[key]

        engobj = self.eng[q]
        for s, v in waits:
            engobj.wait_ge(s, v)
        engobj.dma_start(out=out, in_=in_).then_inc(sem, 16)
        for b in writes:
            b.w = tok
            b.r = {}
        for b in reads:
            if b.r.get(key, 0) < tok[1]:
                b.r[key] = tok[1]

    def alias(self, new, old):
        toks = {}
        for x in old:
            b = x.b
            if b.w is not None and toks.get(b.w[0], 0) < b.w[1]:
                toks[b.w[0]] = b.w[1]
            for k, v in b.r.items():
                if toks.get(k, 0) < v:
                    toks[k] = v
        for x in new:
            x.b.w = None
            x.b.r = dict(toks)

    def finish(self):
        fin = []
        for key, v in self.cnt.items():
            if isinstance(key, tuple) and v > 0:
                fin.append((self.sems[key], v))
        for k in ("pe", "act", "dve", "pool"):
            if self.cnt[k] > 0:
                fin.append((self.sems[k], self.cnt[k]))

        for s, v in fin:
            self.nc.sync.wait_ge(s, v)


class Ctx:
    pass


def mm_group(p, ps, pairs, n0=0, n1=512, m=128, reads=(), fp32=False):
    n = len(pairs)
    for k, (l, r) in enumerate(pairs):
        p.op("pe", (lambda e, l=l, r=r, k=k: e.matmul(ps.t[0:m, n0:n1], l, r, start=(k == 0), stop=(k == n - 1))),
             reads=reads, writes=[ps], inc=(k == n - 1))


def emit_rms_hT(p, c, x_dram, row0, gcol, hT, col0, bufs=None):
    D, KC = c.D, c.KC
    if bufs is not None:
        c = bufs
    p.dma("sp", c.xin.t[:], x_dram.t[row0:row0 + 128, :], reads=[x_dram], writes=[c.xin])
    p.op("act", lambda e: e.activation(out=c.xs.t[:], in_=c.xin.t[:], func=AF.Square, accum_out=c.ss.t[:, 0:1]),
         reads=[c.xin], writes=[c.xs, c.ss])
    p.op("act", lambda e: e.activation(out=c.rs.t[:, 0:1], in_=c.ss.t[:, 0:1], func=AF.Sqrt, scale=1.0 / D, bias=c.epsc.t[:, 0:1]),
         reads=[c.ss, c.epsc], writes=[c.rs])
    p.op("dve", lambda e: e.reciprocal(out=c.rs.t[:, 0:1], in_=c.rs.t[:, 0:1]), reads=[c.rs], writes=[c.rs])
    p.op("act", lambda e: e.activation(out=c.xs.t[:], in_=c.xin.t[:], func=AF.Copy, scale=c.rs.t[:, 0:1]),
         reads=[c.xin, c.rs], writes=[c.xs])
    for g4 in range(0, KC, 4):
        ps = c.ps_tr[(g4 // 4) % 2]
        nk = min(4, KC - g4)
        for k in range(nk):
            ch = g4 + k
            p.op("pe", (lambda e, ch=ch, k=k, ps=ps: e.transpose(ps.t[:, k * 128:(k + 1) * 128],
                                                                c.xs.t[:, ch * 128:(ch + 1) * 128], c.ident.t[:])),
                 reads=[c.xs, c.ident], writes=[ps], inc=(k == nk - 1))
        for k in range(nk):
            ch = g4 + k
            if k % 2 == 0:
                p.op("dve", (lambda e, ch=ch, k=k, ps=ps: e.tensor_scalar(
                    out=hT.t[:, ch, col0:col0 + 128], in0=ps.t[:, k * 128:(k + 1) * 128],
                    scalar1=gcol.t[:, ch:ch + 1], scalar2=None, op0=ALU.mult)), reads=[ps, gcol], writes=[hT])
            else:
                p.op("act", (lambda e, ch=ch, k=k, ps=ps: e.activation(
                    out=hT.t[:, ch, col0:col0 + 128], in_=ps.t[:, k * 128:(k + 1) * 128],
                    func=AF.Copy, scale=gcol.t[:, ch:ch + 1])), reads=[ps, gcol], writes=[hT])


def load_const(p, t, src, q="sp"):
    p.dma(q, t.t[:], src.t, reads=[src], writes=[t])


def build_A(D, TPC, limit=None):
    KC = D // 128
    NT = TPC * 512
    nc = bass.Bass("TRN2", target_bir_lowering=False)
    es = ExitStack()
    p = Prog(nc, es)
    p.limit = limit
    c = Ctx()
    c.D, c.KC = D, KC
    I = lambda n, s, dt=F32: p.dram(n, s, dt, "ExternalInput")
    O = lambda n, s, dt=BF16: p.dram(n, s, dt, "ExternalOutput")
    x = I("x", [NT, D])
    gcol_d = I("gcol", [128, KC])
    wF_d = I("wF", [11, 128, KC * 128])
    wva_d = I("wva", [128, KC * 256])
    wvc_d = I("wvc", [128, KC * 512])
    gkv_d = I("gkv", [128, 4])
    wkvF_d = I("wkvF", [128, 4 * 4 * 128])
    wkvT_d = I("wkvT", [128, 4 * 512])
    cos_d = I("cosT", [64, NT])
    sin_d = I("sinT", [64, NT])
    ident_d = I("ident", [128, 128])
    kaT_o = O("kaT", [2 * 128, NT])
    kcT_o = O("kcT", [4 * 128, NT])
    knT_o = O("knT", [4 * 128, NT])
    krT_o = O("krT", [64, NT])
    va_o = O("va", [NT, 256])
    vc_o = O("vc", [NT, 512])
    vb_o = O("vb", [NT, 512])

    ps = [p.psum() for _ in range(8)]
    c.ps_tr = [ps[5], ps[6]]
    c.xin = p.sb([128, D], F32, "xin")
    c.xs = p.sb([128, D], F32, "xs")
    c.ss = p.sb([128, 1], F32, "ss")
    c.rs = p.sb([128, 1], F32, "rs")
    c2 = Ctx()
    c2.D, c2.KC = D, KC
    c2.xin = p.sb([128, D], F32, "xin2")
    c2.xs = p.sb([128, D], F32, "xs2")
    c2.ss = p.sb([128, 1], F32, "ss2")
    c2.rs = p.sb([128, 1], F32, "rs2")
    c.ident = p.sb([128, 128], F32, "ident")
    c.epsc = p.sb([128, 1], F32, "epsc")
    p.op("pool", lambda e: e.memset(c.epsc.t[:], EPS), writes=[c.epsc])
    gcol = p.sb([128, KC], F32, "gcol")
    gkv = p.sb([128, 4], F32, "gkv")
    ones = p.sb([128, 128], F32, "ones")
    wva = p.sb([128, KC, 256], BF16, "wva")
    wvc = p.sb([128, KC, 512], BF16, "wvc")
    wkvF = p.sb([128, 4, 4, 128], BF16, "wkvF")
    wkvT = p.sb([128, 4, 512], BF16, "wkvT")
    hT = p.sb([128, KC, 512], BF16, "hT")
    wFs = [p.sb([128, KC, 128], BF16, "wFs") for _ in range(2)]
    ckvT = p.sb([128, 4, 512], F32, "ckvT")
    sq = [p.sb([128, 512], F32, "sq") for _ in range(2)]
    rstdb = p.sb([128, 512], F32, "rstdb")
    ckvn = p.sb([128, 4, 512], BF16, "ckvn")
    cosS = p.sb([64, 512], F32, "cos")
    sinS = p.sb([64, 512], F32, "sin")
    t1 = p.sb([64, 512], F32, "t1")
    t2 = p.sb([64, 512], F32, "t2")
    outK = [p.sb([128, 512], BF16, "outK") for _ in range(2)]
    outV = [p.sb([128, 512], BF16, "outV") for _ in range(2)]
    nK = [0]
    nV = [0]

    load_const(p, c.ident, ident_d)
    c2.ident, c2.epsc, c2.ps_tr = c.ident, c.epsc, c.ps_tr
    wA_s = p.dram("wA_s", [11, 128, KC * 128], BF16, "Internal")
    wA_sb = [T(wA_s.t) for _ in range(11)]
    load_const(p, gcol, gcol_d)
    load_const(p, gkv, gkv_d)
    p.op("pool", lambda e: e.memset(ones.t[:], 1.0), writes=[ones])
    p.dma("pool", wva.t[:], wva_d.t.rearrange("p (c n) -> p c n", c=KC), reads=[wva_d], writes=[wva])
    p.dma("pool", wvc.t[:], wvc_d.t.rearrange("p (c n) -> p c n", c=KC), reads=[wvc_d], writes=[wvc])
    p.dma("pool", wkvF.t[:], wkvF_d.t.rearrange("p (h c n) -> p h c n", h=4, c=4), reads=[wkvF_d], writes=[wkvF])
    p.dma("pool", wkvT.t[:], wkvT_d.t.rearrange("p (c n) -> p c n", c=4), reads=[wkvT_d], writes=[wkvT])

    def evac_store(src_ps, m, out_ap, eng):
        o = outK[nK[0] % 2]
        nK[0] += 1
        if eng == "act":
            p.op("act", lambda e: e.activation(out=o.t[0:m, :], in_=src_ps.t[0:m, :], func=AF.Copy),
                 reads=[src_ps], writes=[o])
        else:
            p.op("dve", lambda e: e.tensor_copy(out=o.t[0:m, :], in_=src_ps.t[0:m, :]), reads=[src_ps], writes=[o])
        p.dma("sp", out_ap, o.t[0:m, :], reads=[o], writes=[])

    for j in range(TPC):
        t0 = j * 512
        for i in range(4):
            emit_rms_hT(p, c, x, t0 + i * 128, gcol, hT, i * 128, bufs=(c if i % 2 == 0 else c2))
        p.dma("sp", cosS.t[:], cos_d.t[:, t0:t0 + 512], reads=[cos_d], writes=[cosS])
        p.dma("sp", sinS.t[:], sin_d.t[:, t0:t0 + 512], reads=[sin_d], writes=[sinS])
        for b in range(11):
            w = wFs[b % 2]
            if j == 0:
                p.dma("pool", w.t[:], wF_d.t[b].rearrange("p (c n) -> p c n", c=KC), reads=[wF_d], writes=[w])
                p.dma("pool", wA_s.t[b].rearrange("p (c n) -> p c n", c=KC), w.t[:], reads=[w], writes=[wA_sb[b]])
            else:
                p.dma("sp", w.t[:], wA_s.t[b].rearrange("p (c n) -> p c n", c=KC), reads=[wA_sb[b]], writes=[w])
            if b < 10:
                bank = ps[b % 2]
                mm_group(p, bank, [(w.t[:, ch, :], hT.t[:, ch, :]) for ch in range(KC)], reads=[w, hT])
                if b < 2:
                    evac_store(bank, 128, kaT_o.t[b * 128:(b + 1) * 128, t0:t0 + 512], "act")
                elif b < 6:
                    evac_store(bank, 128, kcT_o.t[(b - 2) * 128:(b - 1) * 128, t0:t0 + 512], "dve")
                else:
                    c4 = b - 6
                    s = sq[c4 % 2]
                    p.op("dve", lambda e, c4=c4, bank=bank: e.tensor_copy(out=ckvT.t[:, c4, :], in_=bank.t[:, :]),
                         reads=[bank], writes=[ckvT])
                    p.op("act", lambda e, s=s, bank=bank: e.activation(out=s.t[:], in_=bank.t[:, :], func=AF.Square),
                         reads=[bank], writes=[s])
                    p.op("pe", lambda e, s=s, c4=c4: e.matmul(ps[2].t[:, :], ones.t[:], s.t[:], start=(c4 == 0),
                                                              stop=(c4 == 3)),
                         reads=[ones, s], writes=[ps[2]], inc=True)
            else:
                mm_group(p, ps[3], [(w.t[:, ch, 0:64], hT.t[:, ch, :]) for ch in range(KC)], m=64, reads=[w, hT])
                mm_group(p, ps[4], [(w.t[:, ch, 64:128], hT.t[:, ch, :]) for ch in range(KC)], m=64, reads=[w, hT])
                p.op("dve", lambda e: e.tensor_tensor(out=t1.t[:], in0=ps[3].t[0:64, :], in1=cosS.t[:], op=ALU.mult),
                     reads=[ps[3], cosS], writes=[t1])
                p.op("dve", lambda e: e.tensor_tensor(out=t2.t[:], in0=ps[4].t[0:64, :], in1=sinS.t[:], op=ALU.mult),
                     reads=[ps[4], sinS], writes=[t2])
                o = outK[nK[0] % 2]
                nK[0] += 1
                p.op("pool", lambda e, o=o: e.tensor_tensor(out=o.t[0:64, :], in0=t1.t[:], in1=t2.t[:], op=ALU.add),
                     reads=[t1, t2], writes=[o])
                p.dma("sp", krT_o.t[:, t0:t0 + 512], o.t[0:64, :], reads=[o], writes=[])
        p.op("act", lambda e: e.activation(out=rstdb.t[:], in_=ps[2].t[:, :], func=AF.Sqrt, scale=1.0 / 512, bias=c.epsc.t[:, 0:1]),
             reads=[ps[2], c.epsc], writes=[rstdb])
        p.op("dve", lambda e: e.reciprocal(out=rstdb.t[:], in_=rstdb.t[:]), reads=[rstdb], writes=[rstdb])
        for c4 in range(4):
            p.op("dve", lambda e, c4=c4: e.scalar_tensor_tensor(out=ckvn.t[:, c4, :], in0=ckvT.t[:, c4, :],
                                                                 scalar=gkv.t[:, c4:c4 + 1], in1=rstdb.t[:],
                                                                 op0=ALU.mult, op1=ALU.mult),
                 reads=[ckvT, gkv, rstdb], writes=[ckvn])
        for h in range(4):
            bank = ps[h % 2]
            mm_group(p, bank, [(wkvF.t[:, h, ch, :], ckvn.t[:, ch, :]) for ch in range(4)], reads=[wkvF, ckvn])
            evac_store(bank, 128, knT_o.t[h * 128:(h + 1) * 128, t0:t0 + 512], "act" if h % 2 else "dve")
        for i in range(4):
            r0 = t0 + i * 128
            for (wt, n, dst, kc_, src) in ((wva, 256, va_o, KC, hT), (wvc, 512, vc_o, KC, hT), (wkvT, 512, vb_o, 4, ckvn)):
                mm_group(p, ps[7], [(src.t[:, ch, i * 128:(i + 1) * 128], wt.t[:, ch, :]) for ch in range(kc_)],
                         n1=n, reads=[wt, src])
                o = outV[nV[0] % 2]
                nV[0] += 1
                p.op("act" if nV[0] % 2 else "dve",
                     (lambda e, o=o, n=n: e.activation(out=o.t[:, 0:n], in_=ps[7].t[:, 0:n], func=AF.Copy))
                     if nV[0] % 2 else
                     (lambda e, o=o, n=n: e.tensor_copy(out=o.t[:, 0:n], in_=ps[7].t[:, 0:n])),
                     reads=[ps[7]], writes=[o])
                p.dma("sp", dst.t[r0:r0 + 128, :], o.t[:, 0:n], reads=[o], writes=[])
    p.finish()
    es.close()
    nc._nops = p.nops
    return nc


def f_layout(w):
    Din, N = w.shape
    KC, nb = Din // 128, N // 128
    return np.ascontiguousarray(w.reshape(KC, 128, nb, 128).transpose(2, 1, 0, 3).reshape(nb, 128, KC * 128))


def t_layout(w):
    Din, N = w.shape
    KC = Din // 128
    return np.ascontiguousarray(w.reshape(KC, 128, N).transpose(1, 0, 2).reshape(128, KC * N))


def col_layout(g):
    return np.ascontiguousarray(g.reshape(-1, 128).T)


def in_offsets(D):
    sizes = (1024, 256, 256, 1024, 1024, 512, 64, 512, 512, 512, 512, 512, D, D, D)
    names = ("qa", "ka", "va", "za", "cq", "ckv", "kr", "zb", "qc", "kc", "vc", "zc", "ga", "gb", "gc")
    off = {}
    o = 0
    for n, s in zip(names, sizes):
        off[n] = (o, o + s)
        o += s
    return off


def own_blocks(core, TPC):
    return [32 * j + 8 * i + core for j in range(TPC) for i in range(4)]


def own_rows(core, TPC):
    return np.concatenate([np.arange(gb * 128, gb * 128 + 128) for gb in own_blocks(core, TPC)])


def rope_tables(rows):
    pos = rows.astype(np.float32)
    inv = (np.float32(10000.0) ** (-np.arange(0, 64, 2, dtype=np.float32) / np.float32(64))).astype(np.float32)
    ang = (pos[:, None] * inv[None, :]).astype(np.float32)
    cs, sn = np.cos(ang).astype(np.float32), np.sin(ang).astype(np.float32)
    C = np.concatenate([cs, cs], axis=1).T
    S = np.concatenate([-sn, sn], axis=1).T
    return np.ascontiguousarray(C), np.ascontiguousarray(S)


SWAP64 = np.concatenate([np.arange(32, 64), np.arange(0, 32)])


def prep_A(D, TPC, x_rows, lw, core):
    off = in_offsets(D)
    W = lw["w_in"]
    sl = lambda n: W[:, off[n][0]:off[n][1]]
    kr = sl("kr")
    wF = f_layout(np.concatenate([sl("ka"), sl("kc"), sl("ckv"), kr, kr[:, SWAP64]], axis=1))
    Wkv = lw["w_kv_up"].reshape(512, 4, 2, 128)
    wkvF = f_layout(np.ascontiguousarray(Wkv[:, :, 0, :]).reshape(512, 512))
    wkvF = np.ascontiguousarray(wkvF.transpose(1, 0, 2).reshape(128, 4 * 4 * 128))
    wkvT = t_layout(np.ascontiguousarray(Wkv[:, :, 1, :]).reshape(512, 512))
    C, S = rope_tables(own_rows(core, TPC))
    return {
        "x": x_rows, "gcol": col_layout(lw["norm_g"]), "wF": wF, "wva": t_layout(sl("va")), "wvc": t_layout(sl("vc")),
        "gkv": col_layout(lw["g_kv_lora"]), "wkvF": wkvF, "wkvT": wkvT, "cosT": C, "sinT": S,
        "ident": np.eye(128, dtype=np.float32),
    }


QB = 2
KCH = 4


def build_B(D, NB, last, limit=None):
    KC = D // 128
    TW = QB * 128
    TPC = NB // (8 * QB)
    NT = TPC * TW
    NKB = ((NB + PAD + KCH - 1) // KCH) * KCH
    NK = NKB * 128
    nc = bass.Bass("TRN2", target_bir_lowering=False)
    es = ExitStack()
    p = Prog(nc, es)
    p.limit = limit
    c = Ctx()
    c.D, c.KC = D, KC
    I = lambda n, s, dt=F32: p.dram(n, s, dt, "ExternalInput")
    x = I("x", [NT, D])
    gcol_d = I("gcol", [128, KC])
    cos_d = I("cosT", [64, NT])
    sin_d = I("sinT", [64, NT])
    ident_d = I("ident", [128, 128])
    kaT_d = I("kaTp", [256, NK], BF16)
    va_d = I("vap", [NK, 256], BF16)
    knT_d = I("knTp", [512, NK], BF16)
    krT_d = I("krTp", [64, NK], BF16)
    vb_d = I("vbp", [NK, 512], BF16)
    kcT_d = I("kcTp", [512, NK], BF16)
    vc_d = I("vcp", [NK, 512], BF16)
    bcol_d = I("biascols", [128, 8])
    biasT_d = I("swabias", [128, 8 * 2 * 128])
    sinks_d = I("sinks", [128, 8])
    msk_d = I("masks", [128, 6 * 128])
    wF_d = I("wF", [36 + 3 * KC, 128, KC * 128])
    gq_d = I("gq", [128, 8])
    wq_d = I("wq", [4, 128, 8 * 256])
    wpa_d = I("wpa", [KC, 128, 8 * 128])
    wpb_d = I("wpb", [KC, 128, 4 * 128])
    wpc_d = I("wpc", [KC, 128, 4 * 128])
    wout_d = I("wout", [KC, 128, KC * 128])
    fg_d = I("fgb", [128, D])
    out_o = p.dram("xout", [NT, D], F32, "ExternalOutput")
    NWF = 36 + 3 * KC
    wF_s = p.dram("wF_s", [NWF, 128, KC * 128], BF16, "Internal")
    wF_sb = [T(wF_s.t) for _ in range(NWF)]
    wq_s = p.dram("wq_s", [4, 128, 8 * 256], BF16, "Internal")
    wq_sb = [T(wq_s.t) for _ in range(4)]
    wp_s = p.dram("wp_s", [KC, 128, 16 * 128], BF16, "Internal")
    wp_sb = [T(wp_s.t) for _ in range(KC)]
    wo_s = p.dram("wo_s", [KC, 128, KC * 128], BF16, "Internal")
    wo_sb = [T(wo_s.t) for _ in range(KC)]
    cur = {"j": 0}

    def wload(w, shp, src_d, src_ap, s_d, s_sb, blk):
        sap = s_d.t[blk].rearrange(shp[0], **shp[1])
        if cur["j"] == 0:
            p.dma("pool", w.t[:], src_ap, reads=[src_d], writes=[w])
            p.dma("pool", sap, w.t[:], reads=[w], writes=[s_sb[blk]])
        else:
            p.dma("sp", w.t[:], sap, reads=[s_sb[blk]], writes=[w])

    ps = [p.psum() for _ in range(8)]
    c.ps_tr = [ps[6], ps[7]]
    c.xin = p.sb([128, D], F32, "xin")
    c.xs = p.sb([128, D], F32, "xs")
    c.ss = p.sb([128, 1], F32, "ss")
    c.rs = p.sb([128, 1], F32, "rs")
    c.ident = p.sb([128, 128], F32, "ident")
    c.epsc = p.sb([128, 1], F32, "epsc")
    p.op("pool", lambda e: e.memset(c.epsc.t[:], EPS), writes=[c.epsc])
    gcol = p.sb([128, KC], F32, "gcol")
    gq = p.sb([128, 8], F32, "gq")
    ones = p.sb([128, 128], F32, "ones")
    negones = p.sb([128, 128], F32, "negones")
    onesb = p.sb([128, 128], BF16, "onesb")
    bcol = p.sb([128, 8], F32, "bcol")
    biasT = p.sb([128, 8, 2, 128], F32, "biasT")
    sinks = p.sb([128, 8], F32, "sinks")
    msk = p.sb([128, 6, 128], F32, "msk")
    mskb = p.sb([128, 2, 128], BF16, "mskb")
    maskA = p.sb([128, 2, 4, 128], BF16, "maskA")
    esink = p.sb([128, 2, 4, 128], F32, "esink")
    hT = p.sb([128, KC, TW], BF16, "hT")
    wFs = [p.sb([128, KC, 128], BF16, "wFs") for _ in range(3)]
    wqs = [p.sb([128, 8, 256], BF16, "wqs") for _ in range(1)]
    wps = [p.sb([128, 16, 128], BF16, "wps") for _ in range(2)]
    qaT = p.sb([128, 8, TW], BF16, "qaT")
    cqT = p.sb([128, 8, TW], F32, "cqT")
    cqn = p.sb([128, 8, TW], BF16, "cqn")
    qcT = p.sb([128, 4, TW], BF16, "qcT")
    qnT = p.sb([128, 4, TW], BF16, "qnT")
    qrT = p.sb([128, 4, TW], BF16, "qrT")
    yT = p.sb([128, 16, TW], BF16, "yT")
    mT = p.sb([128, KC, TW], BF16, "mT")
    sq = [p.sb([128, TW], F32, "sq") for _ in range(2)]
    rstdq = p.sb([128, TW], F32, "rstdq")
    cosS = p.sb([64, TW], F32, "cos")
    sinS = p.sb([64, TW], F32, "sin")
    t1 = p.sb([64, TW], F32, "t1")
    t2 = p.sb([64, TW], F32, "t2")
    assert QB == 2
    kch = [p.sb([128, KCH * 128], BF16, "kch") for _ in range(8)]
    krch = [p.sb([128, KCH * 128], BF16, "krch") for _ in range(2)]
    vch = [p.sb([128, KCH, 128], BF16, "vch") for _ in range(8)]
    PtP = [p.sb([128, 2 * TW], BF16, "PtP") for _ in range(4)]
    eeP = [p.sb([128, 2 * TW], F32, "eeP") for _ in range(2)]
    spP = [p.sb([128, 2 * TW], BF16, "spP") for _ in range(2)]
    LaP = [p.sb([128, 2 * TW], F32, "LaP") for _ in range(2)]
    LbP = [p.sb([128, 2 * TW], BF16, "LbP") for _ in range(2)]
    rdp = [p.sb([128, 2 * TW], F32, "rdp") for _ in range(2)]
    mskb2 = p.sb([128, 2, 2, 128], BF16, "mskb2")
    Pt, ef, rden = PtP, eeP, rdp[0]
    negIUb = p.sb([128, 128], BF16, "negIUb")
    negonesb = p.sb([128, 128], BF16, "negonesb")

    sg = [p.sb([128, TW], F32, "sg") for _ in range(2)]
    acc = [p.sb([128, TW], F32, "acc") for _ in range(2)]
    fgp = [p.sb([128, 256], F32, "fgp") for _ in range(2)]
    cnt = {"ch": 0, "P": 0, "e": 0, "s": 0, "wf": 0, "sg": 0, "kr": 0}

    load_const(p, c.ident, ident_d)
    p.op("pool", lambda e: e.memset(qrT.t[:], 0.0), writes=[qrT])
    for kk in range(2):
        p.op("pool", lambda e, kk=kk: e.memset(krch[kk].t[:], 0.0), writes=[krch[kk]])
    load_const(p, gcol, gcol_d)
    load_const(p, gq, gq_d)
    load_const(p, bcol, bcol_d)
    load_const(p, sinks, sinks_d)
    p.dma("sp", biasT.t[:], biasT_d.t.rearrange("p (h t q) -> p h t q", h=8, t=2), reads=[biasT_d], writes=[biasT])
    p.dma("sp", msk.t[:], msk_d.t.rearrange("p (m q) -> p m q", m=6), reads=[msk_d], writes=[msk])
    p.op("pool", lambda e: e.memset(ones.t[:], 1.0), writes=[ones])
    p.op("pool", lambda e: e.memset(negones.t[:], -1.0), writes=[negones])
    p.op("pool", lambda e: e.memset(onesb.t[:], 1.0), writes=[onesb])
    p.op("dve", lambda e: e.tensor_copy(out=mskb.t[:], in_=msk.t[:, 0:2, :]), reads=[msk], writes=[mskb])
    for t_ in range(2):
        for hh in range(2):
            p.op("dve", lambda e, t_=t_, hh=hh: e.tensor_copy(out=mskb2.t[:, t_, hh, :], in_=msk.t[:, t_, :]), reads=[msk], writes=[mskb2])
    for t_ in range(2):
        for hh in range(4):
            p.op("dve", lambda e, t_=t_, hh=hh: e.tensor_copy(out=maskA.t[:, t_, hh, :], in_=msk.t[:, 2 + t_, :]),
                 reads=[msk], writes=[maskA])
    p.op("act", lambda e: e.activation(out=sinks.t[:], in_=sinks.t[:], func=AF.Exp), reads=[sinks], writes=[sinks])
    for h in range(8):
        p.op("dve", lambda e, h=h: e.tensor_scalar(out=esink.t[:, h // 4, h % 4, :], in0=ones.t[:], scalar1=sinks.t[:, h:h + 1],
                                                   scalar2=None, op0=ALU.mult), reads=[ones, sinks], writes=[esink])
    p.op("dve", lambda e: e.tensor_copy(out=negIUb.t[:], in_=msk.t[:, 4, :]), reads=[msk], writes=[negIUb])
    p.op("pool", lambda e: e.memset(negonesb.t[:], -1.0), writes=[negonesb])

    def fproj(blk, evac, m=128):
        w = wFs[cnt["wf"] % 3]
        bank = ps[cnt["wf"] % 2]
        cnt["wf"] += 1
        wload(w, ("p (c n) -> p c n", dict(c=KC)), wF_d, wF_d.t[blk].rearrange("p (c n) -> p c n", c=KC), wF_s, wF_sb, blk)
        mm_group(p, bank, [(w.t[:, ch, :], hT.t[:, ch, :]) for ch in range(KC)], n1=TW, reads=[w, hT])
        evac(bank)

    def load_chunk(src_kT, row0, src_v, col0, ck):
        k = cnt["ch"] % 8
        cnt["ch"] += 1
        kt, vt = kch[k], vch[k]
        k0 = ck * KCH * 128
        p.dma("sp", kt.t[:], src_kT.t[row0:row0 + 128, k0:k0 + KCH * 128], reads=[src_kT], writes=[kt])
        p.dma("sp", vt.t[:], src_v.t[k0:k0 + KCH * 128, col0:col0 + 128].rearrange("(b p) d -> p b d", p=128),
              reads=[src_v], writes=[vt])
        return kt, vt

    def bias_ap(kbp):
        return bcol.t[:, (kbp + 7):(kbp + 8)] if kbp < 0 else bcol.t[:, 7:8]

    for j in range(TPC):
        cur["j"] = j
        t0 = j * TW
        qk = [8 * QB * j + 8 * i for i in range(QB)]
        idx_max = qk[-1] + PAD

        def col0_of(kbp):
            for i in range(QB):
                if qk[i] >= kbp:
                    return i
            raise AssertionError

        for i in range(QB):
            emit_rms_hT(p, c, x, t0 + i * 128, gcol, hT, i * 128)
        p.dma("sp", cosS.t[:], cos_d.t[:, t0:t0 + TW], reads=[cos_d], writes=[cosS])
        p.dma("sp", sinS.t[:], sin_d.t[:, t0:t0 + TW], reads=[sin_d], writes=[sinS])

        for b in range(8):
            fproj(b, lambda bank, b=b: p.op("act", lambda e: e.activation(out=qaT.t[:, b, :], in_=bank.t[:, 0:TW], func=AF.Copy),
                                            reads=[bank], writes=[qaT]))
        for b in range(8):
            def ev(bank, b=b):
                s = sq[b % 2]
                p.op("dve", lambda e: e.tensor_copy(out=cqT.t[:, b, :], in_=bank.t[:, 0:TW]), reads=[bank], writes=[cqT])
                p.op("act", lambda e: e.activation(out=s.t[:], in_=cqT.t[:, b, :], func=AF.Square), reads=[cqT], writes=[s])
                p.op("pe", lambda e: e.matmul(ps[2].t[:, 0:TW], ones.t[:], s.t[:], start=(b == 0), stop=(b == 7)),
                     reads=[ones, s], writes=[ps[2]], inc=True)
            fproj(8 + b, ev)
        p.op("act", lambda e: e.activation(out=rstdq.t[:], in_=ps[2].t[:, 0:TW], func=AF.Sqrt, scale=1.0 / 1024, bias=c.epsc.t[:, 0:1]),
             reads=[ps[2], c.epsc], writes=[rstdq])
        p.op("dve", lambda e: e.reciprocal(out=rstdq.t[:], in_=rstdq.t[:]), reads=[rstdq], writes=[rstdq])
        for b in range(8):
            p.op("dve", lambda e, b=b: e.scalar_tensor_tensor(out=cqn.t[:, b, :], in0=cqT.t[:, b, :], scalar=gq.t[:, b:b + 1],
                                                              in1=rstdq.t[:], op0=ALU.mult, op1=ALU.mult),
                 reads=[cqT, gq, rstdq], writes=[cqn])
        for b in range(4):
            fproj(16 + b, lambda bank, b=b: p.op("act", lambda e: e.activation(out=qcT.t[:, b, :], in_=bank.t[:, 0:TW], func=AF.Copy,
                                                                               scale=128.0 ** -0.5), reads=[bank], writes=[qcT]))
        for h in range(4):
            w = wqs[0]
            wload(w, ("p (c n) -> p c n", dict(c=8)), wq_d, wq_d.t[h].rearrange("p (c n) -> p c n", c=8), wq_s, wq_sb, h)
            mm_group(p, ps[3], [(w.t[:, ch, 0:128], cqn.t[:, ch, :]) for ch in range(8)], n1=TW, reads=[w, cqn])
            mm_group(p, ps[4], [(w.t[:, ch, 128:192], cqn.t[:, ch, :]) for ch in range(8)], n1=TW, m=64, reads=[w, cqn])
            mm_group(p, ps[5], [(w.t[:, ch, 192:256], cqn.t[:, ch, :]) for ch in range(8)], n1=TW, m=64, reads=[w, cqn])
            p.op("act", lambda e, h=h: e.activation(out=qnT.t[:, h, :], in_=ps[3].t[:, 0:TW], func=AF.Copy), reads=[ps[3]], writes=[qnT])
            p.op("dve", lambda e: e.tensor_tensor(out=t1.t[:], in0=ps[4].t[0:64, 0:TW], in1=cosS.t[:], op=ALU.mult),
                 reads=[ps[4], cosS], writes=[t1])
            p.op("dve", lambda e: e.tensor_tensor(out=t2.t[:], in0=ps[5].t[0:64, 0:TW], in1=sinS.t[:], op=ALU.mult),
                 reads=[ps[5], sinS], writes=[t2])
            p.op("pool", lambda e, h=h: e.tensor_tensor(out=qrT.t[0:64, h, :], in0=t1.t[:], in1=t2.t[:], op=ALU.add),
                 reads=[t1, t2], writes=[qrT])

        for i in range(QB):
            for g in range(2):
                k = cnt["ch"] % 8
                cnt["ch"] += 1
                kt, vt = kch[k], vch[k]
                k0 = (qk[i] + PAD - 1) * 128
                p.dma("sp", kt.t[:, 0:256], kaT_d.t[g * 128:(g + 1) * 128, k0:k0 + 256], reads=[kaT_d], writes=[kt])
                p.dma("sp", vt.t[:, 0:2, :], va_d.t[k0:k0 + 256, g * 128:(g + 1) * 128].rearrange("(b p) d -> p b d", p=128),
                      reads=[va_d], writes=[vt])
                q4 = qaT.t[:, 4 * g:4 * g + 4, i * 128:(i + 1) * 128]
                for t_ in range(2):
                    sb_ = ps[t_]
                    s3 = sb_.t[:, 0:512].rearrange("p (h q) -> p h q", h=4)
                    p.op("pe", lambda e, t_=t_, s3=s3: e.matmul(s3, kt.t[:, t_ * 128:(t_ + 1) * 128], q4, start=True, stop=True),
                         reads=[kt, qaT], writes=[sb_])
                    ee = ef[cnt["e"] % 2]
                    cnt["e"] += 1
                    e3 = ee.t[:, 0:512].rearrange("p (h q) -> p h q", h=4)
                    p.op("dve", lambda e, s3=s3, e3=e3, t_=t_: e.scalar_tensor_tensor(
                        out=e3, in0=s3, scalar=128.0 ** -0.5, in1=biasT.t[:, 4 * g:4 * g + 4, t_, :], op0=ALU.mult, op1=ALU.add),
                         reads=[sb_, biasT], writes=[ee])
                    P = Pt[cnt["P"] % 2]
                    cnt["P"] += 1
                    bap = bias_ap(qk[i] - 1) if t_ == 0 else bcol.t[:, 7:8]
                    p.op("act", lambda e, P=P, ee=ee, bap=bap: e.activation(out=P.t[:, 0:512], in_=ee.t[:, 0:512], func=AF.Exp, bias=bap),
                         reads=[ee, bcol], writes=[P])
                    P3 = P.t[:, 0:512].rearrange("p (h q) -> p h q", h=4)
                    p.op("dve", lambda e, P3=P3, t_=t_: e.tensor_tensor(out=P3, in0=P3, in1=maskA.t[:, t_, :, :], op=ALU.mult),
                         reads=[P, maskA], writes=[P])
                    p.op("pe", lambda e, P=P, t_=t_: e.matmul(ps[2].t[:, 0:512], vt.t[:, t_, :], P.t[:, 0:512], start=(t_ == 0), stop=(t_ == 1)),
                         reads=[vt, P], writes=[ps[2]], inc=(t_ == 1))
                    p.op("pe", lambda e, P=P, t_=t_: e.matmul(ps[3].t[:, 0:512], onesb.t[:], P.t[:, 0:512], start=(t_ == 0), stop=(t_ == 1)),
                         reads=[onesb, P], writes=[ps[3]], inc=(t_ == 1))
                p.op("dve", lambda e, g=g: e.tensor_tensor(out=rden.t[:, 0:512], in0=ps[3].t[:, 0:512],
                                                           in1=esink.t[:, g, :, :].rearrange("p h q -> p (h q)"), op=ALU.add),
                     reads=[ps[3], esink], writes=[rden])
                p.op("dve", lambda e: e.reciprocal(out=rden.t[:, 0:512], in_=rden.t[:, 0:512]), reads=[rden], writes=[rden])
                p.op("dve", lambda e, g=g, i=i: e.tensor_tensor(
                    out=yT.t[:, 4 * g:4 * g + 4, i * 128:(i + 1) * 128], in0=ps[2].t[:, 0:512].rearrange("p (h q) -> p h q", h=4),
                    in1=rden.t[:, 0:512].rearrange("p (h q) -> p h q", h=4), op=ALU.mult),
                     reads=[ps[2], rden], writes=[yT])

        sclB = 192.0 ** -0.5
        for bk in (ps[4], ps[5], ps[6], ps[7]):
            p.op("dve", lambda e: e.memset(bk.t[:, :], 0.0), writes=[bk])
        cacheB = {}

        def chunksB(ck):
            if ck not in cacheB:
                k0 = ck * KCH * 128
                krt = krch[cnt["kr"] % 2]
                cnt["kr"] += 1
                p.dma("sp", krt.t[0:64, :], krT_d.t[:, k0:k0 + KCH * 128], reads=[krT_d], writes=[krt])
                cacheB[ck] = (krt, [load_chunk(knT_d, h * 128, vb_d, h * 128, ck) for h in range(4)])
            return cacheB[ck]

        def binfo(idx):
            kbp = idx - PAD
            i0 = col0_of(kbp)
            return kbp, i0 * 128, (kbp == qk[i0])

        def v3(t, n0):
            return t.t[:, 0:2 * TW].rearrange("p (h q) -> p h q", h=2)[:, :, n0:TW]

        def scoreB(idx, pr):
            kbp, n0, diag = binfo(idx)
            krt, chunks = chunksB(idx // KCH)
            bl = idx % KCH
            sb_ = ps[2 * pr + idx % 2]
            for hh in range(2):
                h = 2 * pr + hh
                kt, vt = chunks[h]
                oc = hh * TW
                p.op("pe", lambda e: e.matmul(sb_.t[:, oc + n0:oc + TW], kt.t[:, bl * 128:(bl + 1) * 128], qnT.t[:, h, n0:TW],
                                              start=True, stop=False), reads=[kt, qnT], writes=[sb_], inc=False)
                p.op("pe", lambda e: e.matmul(sb_.t[:, oc + n0:oc + TW], krt.t[:, bl * 128:(bl + 1) * 128], qrT.t[:, h, n0:TW],
                                              start=False, stop=True), reads=[krt, qrT], writes=[sb_], inc=(hh == 1))

        for pr in range(2):
            scoreB(0, pr)
        for idx in range(idx_max + 1):
            kbp, n0, diag = binfo(idx)
            lastb = (idx == idx_max)
            krt, chunks = chunksB(idx // KCH)
            bl = idx % KCH
            for pr in range(2):
                if not lastb:
                    scoreB(idx + 1, pr)
                sb_ = ps[2 * pr + idx % 2]
                P = PtP[2 * pr + idx % 2]
                if kbp < 0:
                    bap = bias_ap(kbp)
                    p.op("act", lambda e: e.activation(out=v3(P, n0), in_=v3(sb_, n0), func=AF.Exp, scale=sclB, bias=bap),
                         reads=[sb_, bcol], writes=[P])
                else:
                    p.op("act", lambda e: e.activation(out=v3(P, n0), in_=v3(sb_, n0), func=AF.Exp, scale=sclB),
                         reads=[sb_], writes=[P])
                if diag:
                    pd = P.t[:, 0:2 * TW].rearrange("p (h q) -> p h q", h=2)[:, :, n0:n0 + 128]
                    p.op("dve", lambda e: e.tensor_tensor(out=pd, in0=pd, in1=mskb2.t[:, 0, :, :], op=ALU.mult), reads=[P, mskb2], writes=[P])
                ob, db = ps[4 + pr], ps[6 + pr]
                for hh in range(2):
                    h = 2 * pr + hh
                    kt, vt = chunks[h]
                    oc = hh * TW
                    p.op("pe", lambda e: e.matmul(ob.t[:, oc + n0:oc + TW], vt.t[:, bl, :], P.t[:, oc + n0:oc + TW], start=False, stop=lastb,
                                                  skip_group_check=True), reads=[vt, P], writes=[ob], inc=False)
                    p.op("pe", lambda e: e.matmul(db.t[:, oc + n0:oc + TW], onesb.t[:], P.t[:, oc + n0:oc + TW], start=False, stop=lastb,
                                                  skip_group_check=True), reads=[onesb, P], writes=[db], inc=(hh == 1))
        for pr in range(2):
            ob, db = ps[4 + pr], ps[6 + pr]
            rd = rdp[pr]
            p.op("dve", lambda e: e.reciprocal(out=rd.t[:, :], in_=db.t[:, 0:2 * TW]), reads=[db], writes=[rd])
            p.op("dve", lambda e: e.tensor_tensor(out=yT.t[:, 8 + 2 * pr:10 + 2 * pr, :], in0=ob.t[:, 0:2 * TW].rearrange("p (h q) -> p h q", h=2),
                                                  in1=rd.t[:, :].rearrange("p (h q) -> p h q", h=2), op=ALU.mult),
                 reads=[ob, rd], writes=[yT])

        for pr in range(2):
            p.op("dve", lambda e: e.memset(LaP[pr].t[:], 0.0), writes=[LaP[pr]])
            p.op("dve", lambda e: e.memset(LbP[pr].t[:], 0.0), writes=[LbP[pr]])
        for bk in (ps[4], ps[5]):
            p.op("dve", lambda e: e.memset(bk.t[:, :], 0.0), writes=[bk])
        cacheC = {}

        def chunksC(ck):
            if ck not in cacheC:
                cacheC[ck] = [load_chunk(kcT_d, h * 128, vc_d, h * 128, ck) for h in range(4)]
            return cacheC[ck]

        def zC(idx):
            kbp, n0, diag = binfo(idx)
            chunks = chunksC(idx // KCH)
            bl = idx % KCH
            for h in range(4):
                kt, vt = chunks[h]
                zb = ps[h // 2]
                oc = (h % 2) * TW
                p.op("pe", lambda e: e.matmul(zb.t[:, oc + n0:oc + TW], kt.t[:, bl * 128:(bl + 1) * 128], qcT.t[:, h, n0:TW], start=True, stop=True),
                     reads=[kt, qcT], writes=[zb], inc=(h % 2 == 1))

        def spC(idx):
            kbp, n0, diag = binfo(idx)
            bap = bias_ap(kbp) if kbp < 0 else 0.0
            for pr in range(2):
                zb, ee, sp_ = ps[pr], eeP[pr], spP[pr]
                p.op("act", lambda e: e.activation(out=v3(ee, n0), in_=v3(zb, n0), func=AF.Exp, bias=bap), reads=[zb, bcol], writes=[ee])
                p.op("act", lambda e: e.activation(out=v3(sp_, n0), in_=v3(ee, n0), func=AF.Ln, bias=1.0), reads=[ee], writes=[sp_])
                if diag:
                    sd = sp_.t[:, 0:2 * TW].rearrange("p (h q) -> p h q", h=2)[:, :, n0:n0 + 128]
                    p.op("dve", lambda e: e.tensor_tensor(out=sd, in0=sd, in1=mskb2.t[:, 1, :, :], op=ALU.mult), reads=[sp_, mskb2], writes=[sp_])

        zC(idx_max)
        spC(idx_max)
        for idx in range(idx_max, -1, -1):
            kbp, n0, diag = binfo(idx)
            lastb = (idx == 0)
            chunks = chunksC(idx // KCH)
            bl = idx % KCH
            bap = bias_ap(kbp) if kbp < 0 else 0.0
            for h in range(4):
                kt, vt = chunks[h]
                ab, sp_, Lb = ps[2 + h // 2], spP[h // 2], LbP[h // 2]
                oc = (h % 2) * TW
                kap = kt.t[:, bl * 128:(bl + 1) * 128]
                p.op("pe", lambda e: e.matmul(ab.t[:, oc + n0:oc + TW], kap, qcT.t[:, h, n0:TW], start=True, stop=False),
                     reads=[kt, qcT], writes=[ab], inc=False)
                p.op("pe", lambda e: e.matmul(ab.t[:, oc + n0:oc + TW], negIUb.t[:], sp_.t[:, oc + n0:oc + TW], start=False, stop=False),
                     reads=[negIUb, sp_], writes=[ab], inc=False)
                p.op("pe", lambda e: e.matmul(ab.t[:, oc + n0:oc + TW], negonesb.t[:], Lb.t[:, oc + n0:oc + TW], start=False, stop=True),
                     reads=[negonesb, Lb], writes=[ab], inc=(h % 2 == 1))
            if not lastb:
                zC(idx - 1)
            for pr in range(2):
                ab, sp_, La, Lb = ps[2 + pr], spP[pr], LaP[pr], LbP[pr]
                P = PtP[2 * pr + idx % 2]
                p.op("act", lambda e: e.activation(out=v3(P, n0), in_=v3(ab, n0), func=AF.Exp, bias=bap), reads=[ab, bcol], writes=[P])
                if diag:
                    pd = P.t[:, 0:2 * TW].rearrange("p (h q) -> p h q", h=2)[:, :, n0:n0 + 128]
                    p.op("dve", lambda e: e.tensor_tensor(out=pd, in0=pd, in1=mskb2.t[:, 1, :, :], op=ALU.mult), reads=[P, mskb2], writes=[P])
                p.op("dve", lambda e: e.tensor_tensor(out=v3(La, n0), in0=v3(La, n0), in1=v3(sp_, n0), op=ALU.add), reads=[La, sp_], writes=[La])
                p.op("dve", lambda e: e.tensor_copy(out=v3(Lb, n0), in_=v3(La, n0)), reads=[La], writes=[Lb])
            if not lastb:
                spC(idx - 1)
            for h in range(4):
                kt, vt = chunks[h]
                ob = ps[4 + h // 2]
                oc = (h % 2) * TW
                P = PtP[2 * (h // 2) + idx % 2]
                p.op("pe", lambda e: e.matmul(ob.t[:, oc + n0:oc + TW], vt.t[:, bl, :], P.t[:, oc + n0:oc + TW], start=False, stop=lastb,
                                              skip_group_check=True), reads=[vt, P], writes=[ob], inc=(h % 2 == 1))
        for pr in range(2):
            ob = ps[4 + pr]
            p.op("act", lambda e: e.activation(out=yT.t[:, 12 + 2 * pr:14 + 2 * pr, :], in_=ob.t[:, 0:2 * TW].rearrange("p (h q) -> p h q", h=2),
                                               func=AF.Copy), reads=[ob], writes=[yT])

        for k in range(16):
            def ev(bank, k=k):
                s = sg[k % 2]
                p.op("act", lambda e: e.activation(out=s.t[:], in_=bank.t[:, 0:TW], func=AF.Silu), reads=[bank], writes=[s])
                p.op("dve", lambda e: e.tensor_tensor(out=yT.t[:, k, :], in0=yT.t[:, k, :], in1=s.t[:], op=ALU.mult),
                     reads=[yT, s], writes=[yT])
            fproj(20 + k, ev)

        for jb in range(KC):
            wp = wps[jb % 2]
            if j == 0:
                p.dma("pool", wp.t[:, 0:8, :], wpa_d.t[jb].rearrange("p (c n) -> p c n", c=8), reads=[wpa_d], writes=[wp])
                p.dma("pool", wp.t[:, 8:12, :], wpb_d.t[jb].rearrange("p (c n) -> p c n", c=4), reads=[wpb_d], writes=[wp])
                p.dma("pool", wp.t[:, 12:16, :], wpc_d.t[jb].rearrange("p (c n) -> p c n", c=4), reads=[wpc_d], writes=[wp])
                p.dma("pool", wp_s.t[jb].rearrange("p (c n) -> p c n", c=16), wp.t[:], reads=[wp], writes=[wp_sb[jb]])
            else:
                p.dma("sp", wp.t[:], wp_s.t[jb].rearrange("p (c n) -> p c n", c=16), reads=[wp_sb[jb]], writes=[wp])
            a_ = acc[jb % 2]
            for br, (k0, k1) in enumerate(((0, 8), (8, 12), (12, 16))):
                pb = ps[2 + br]
                mm_group(p, pb, [(wp.t[:, k, :], yT.t[:, k, :]) for k in range(k0, k1)], n1=TW, reads=[wp, yT])

                def ev(bank, br=br, pb=pb, a_=a_, jb=jb):
                    s = sg[cnt["sg"] % 2]
                    cnt["sg"] += 1
                    p.op("act", lambda e: e.activation(out=s.t[:], in_=bank.t[:, 0:TW], func=AF.Sigmoid), reads=[bank], writes=[s])
                    if br == 0:
                        p.op("dve", lambda e: e.tensor_tensor(out=a_.t[:], in0=pb.t[:, 0:TW], in1=s.t[:], op=ALU.mult),
                             reads=[pb, s], writes=[a_])
                    else:
                        p.op("dve", lambda e: e.tensor_tensor(out=s.t[:], in0=pb.t[:, 0:TW], in1=s.t[:], op=ALU.mult),
                             reads=[pb, s], writes=[s])
                        if br == 1:
                            p.op("pool", lambda e: e.tensor_tensor(out=a_.t[:], in0=a_.t[:], in1=s.t[:], op=ALU.add),
                                 reads=[a_, s], writes=[a_])
                        else:
                            p.op("pool", lambda e: e.tensor_tensor(out=mT.t[:, jb, :], in0=a_.t[:], in1=s.t[:], op=ALU.add),
                                 reads=[a_, s], writes=[mT])
                fproj(36 + br * KC + jb, ev)

        xn = [c.xin, c.xs]
        for i in range(QB):
            p.dma("sp", xn[i].t[:], x.t[t0 + i * 128:t0 + (i + 1) * 128, :], reads=[x], writes=[xn[i]])
        for n in range(KC):
            w = wFs[cnt["wf"] % 3]
            cnt["wf"] += 1
            wload(w, ("p (c n) -> p c n", dict(c=KC)), wout_d, wout_d.t[n].rearrange("p (c n) -> p c n", c=KC), wo_s, wo_sb, n)
            for i in range(QB):
                bank = ps[(n * QB + i) % 2]
                mm_group(p, bank, [(mT.t[:, ch, i * 128:(i + 1) * 128], w.t[:, ch, :]) for ch in range(KC)], n1=128, reads=[w, mT])
                p.op("dve", lambda e, bank=bank, i=i, n=n: e.tensor_tensor(out=xn[i].t[:, n * 128:(n + 1) * 128], in0=bank.t[:, 0:128],
                                                                          in1=xn[i].t[:, n * 128:(n + 1) * 128], op=ALU.add),
                     reads=[bank, xn[i]], writes=[xn[i]])
        for i in range(QB):
            if last:
                junk = hT.t[:, :, :].rearrange("p c t -> p (c t)")[:, 0:D]
                p.op("act", lambda e, i=i, junk=junk: e.activation(out=junk, in_=xn[i].t[:], func=AF.Square, accum_out=c.ss.t[:, 0:1]),
                     reads=[xn[i]], writes=[hT, c.ss])
                p.op("act", lambda e: e.activation(out=c.rs.t[:, 0:1], in_=c.ss.t[:, 0:1], func=AF.Sqrt, scale=1.0 / D, bias=c.epsc.t[:, 0:1]),
                     reads=[c.ss, c.epsc], writes=[c.rs])
                p.op("dve", lambda e: e.reciprocal(out=c.rs.t[:, 0:1], in_=c.rs.t[:, 0:1]), reads=[c.rs], writes=[c.rs])
                for k in range(0, D, 256):
                    f = fgp[(k // 256) % 2]
                    p.dma("sp", f.t[:], fg_d.t[:, k:k + 256], reads=[fg_d], writes=[f])
                    p.op("dve", lambda e, i=i, k=k, f=f: e.scalar_tensor_tensor(out=xn[i].t[:, k:k + 256], in0=xn[i].t[:, k:k + 256],
                                                                               scalar=c.rs.t[:, 0:1], in1=f.t[:], op0=ALU.mult, op1=ALU.mult),
                         reads=[xn[i], c.rs, f], writes=[xn[i]])
            p.dma("sp", out_o.t[t0 + i * 128:t0 + (i + 1) * 128, :], xn[i].t[:], reads=[xn[i]], writes=[])
    p.finish()
    es.close()
    nc._nops = p.nops
    return nc


def t5_bucket_np(rel):
    n = np.maximum(rel, 0)
    logn = np.log(np.maximum(n, 1).astype(np.float32) / np.float32(16))
    large = 16 + (logn / np.float32(np.log(128 / 16)) * np.float32(16)).astype(np.int32)
    large = np.minimum(large, 31)
    return np.where(n < 16, n, large)


def make_masks():
    r = np.arange(128)[:, None]
    cc = np.arange(128)[None, :]
    m = np.zeros((128, 6, 128), np.float32)
    m[:, 0] = (r <= cc)
    m[:, 1] = (r < cc)
    m[:, 2] = (r > cc)
    m[:, 3] = (r <= cc)
    m[:, 4] = -(r >= cc).astype(np.float32)
    return np.ascontiguousarray(m.reshape(128, 6 * 128))


def shifted(G, core, NB, axis):
    NKB = ((NB + PAD + KCH - 1) // KCH) * KCH
    tot = NKB + 8
    shp = list(G.shape)
    shp[axis] = tot * 128
    Gp = np.zeros(shp, G.dtype)
    sl = [slice(None)] * G.ndim
    sl[axis] = slice(PAD * 128, (PAD + NB) * 128)
    Gp[tuple(sl)] = G
    sl[axis] = slice(core * 128, (core + NKB) * 128)
    return np.ascontiguousarray(Gp[tuple(sl)])


def prep_B_shared(D, lw, rel_bias, final_g):
    off = in_offsets(D)
    W = lw["w_in"]
    sl = lambda n: W[:, off[n][0]:off[n][1]]
    wF = f_layout(np.concatenate([sl(n) for n in ("qa", "cq", "qc", "za", "zb", "zc", "ga", "gb", "gc")], axis=1))
    Wq = lw["w_q_up"].reshape(1024, 4, 192)
    wq = np.stack([t_layout(np.concatenate([Wq[:, h, :128], Wq[:, h, 128:], Wq[:, h, 128:][:, SWAP64]], axis=1)) for h in range(4)])
    s_ = np.arange(128)[:, None]
    q_ = np.arange(128)[None, :]
    sw = np.zeros((128, 8, 2, 128), np.float32)
    for t in range(2):
        rel = 128 + q_ - (t * 128 + s_)
        sw[:, :, t, :] = rel_bias[t5_bucket_np(rel)].transpose(0, 2, 1)
    return {
        "gcol": col_layout(lw["norm_g"]), "ident": np.eye(128, dtype=np.float32),
        "swabias": np.ascontiguousarray(sw.reshape(128, 8 * 2 * 128)),
        "sinks": np.ascontiguousarray(np.broadcast_to(lw["attn_sinks"][None, :], (128, 8))),
        "masks": make_masks(), "wF": wF, "gq": col_layout(lw["g_q_lora"]), "wq": wq,
        "wpa": f_layout(lw["w_proj_a"]), "wpb": f_layout(lw["w_proj_b"]), "wpc": f_layout(lw["w_proj_c"]),
        "wout": f_layout(lw["w_out"]), "fgb": np.ascontiguousarray(np.broadcast_to(final_g[None, :], (128, D))),
    }


def prep_B_core(D, NB, x_rows, glob, core, shared):
    TPC = NB // 32
    C, S = rope_tables(own_rows(core, TPC))
    bc = np.zeros((128, 8), np.float32)
    for t in range(7):
        if (t - 7) + core < 0:
            bc[:, t] = NEG
    m = dict(shared)
    m.update({
        "x": x_rows, "cosT": C, "sinT": S, "biascols": bc,
        "kaTp": shifted(glob["kaT"], core, NB, 1), "vap": shifted(glob["va"], core, NB, 0),
        "knTp": shifted(glob["knT"], core, NB, 1), "krTp": shifted(glob["krT"], core, NB, 1),
        "vbp": shifted(glob["vb"], core, NB, 0), "kcTp": shifted(glob["kcT"], core, NB, 1),
        "vcp": shifted(glob["vc"], core, NB, 0),
    })
    return m


def gather_kv(resA, NB):
    TPC = NB // 32
    glob = {}
    for k in ("kaT", "kcT", "knT", "krT", "va", "vc", "vb"):
        tm = not k.endswith("T")
        a0 = np.asarray(resA[0][k])
        shp = (NB * 128, a0.shape[1]) if tm else (a0.shape[0], NB * 128)
        G = np.zeros(shp, a0.dtype)
        for core in range(NCORES):
            a = np.asarray(resA[core][k])
            for m_, gb in enumerate(own_blocks(core, TPC)):
                if tm:
                    G[gb * 128:(gb + 1) * 128] = a[m_ * 128:(m_ + 1) * 128]
                else:
                    G[:, gb * 128:(gb + 1) * 128] = a[:, m_ * 128:(m_ + 1) * 128]
        glob[k] = G
    return glob


_CACHE = {}


def run_model(x2d, layers, rel_bias, final_g, D, SEQ):
    NB = SEQ // 128
    TPC = NB // 32
    rows = [own_rows(c, TPC) for c in range(NCORES)]
    xs = [np.ascontiguousarray(x2d[r]) for r in rows]
    cores = list(range(NCORES))
    for li, lw in enumerate(layers):
        last = (li == len(layers) - 1)
        if ("A", D, TPC) not in _CACHE:
            _CACHE[("A", D, TPC)] = build_A(D, TPC)
        if ("B", D, NB, last) not in _CACHE:
            _CACHE[("B", D, NB, last)] = build_B(D, NB, last)
        resA = run_bass_kernel_spmd(_CACHE[("A", D, TPC)], [prep_A(D, TPC, xs[c], lw, c) for c in cores], core_ids=cores).results
        glob = gather_kv(resA, NB)
        shared = prep_B_shared(D, lw, rel_bias, final_g)
        resB = run_bass_kernel_spmd(_CACHE[("B", D, NB, last)], [prep_B_core(D, NB, xs[c], glob, c, shared) for c in cores],
                                    core_ids=cores).results
        xs = [np.asarray(resB[c]["xout"]) for c in cores]
    out = np.zeros((SEQ, D), np.float32)
    for c in cores:
        out[rows[c]] = xs[c]
    return out


def kernel(x, norm_g, w_in, attn_sinks, rel_bias, g_q_lora, w_q_up, g_kv_lora, w_kv_up,
           w_proj_a, w_proj_b, w_proj_c, w_out, final_g):
    x = np.asarray(x, np.float32)
    B_, S_, D_ = x.shape
    names = ("norm_g", "w_in", "attn_sinks", "g_q_lora", "w_q_up", "g_kv_lora", "w_kv_up", "w_proj_a", "w_proj_b", "w_proj_c", "w_out")
    vals = (norm_g, w_in, attn_sinks, g_q_lora, w_q_up, g_kv_lora, w_kv_up, w_proj_a, w_proj_b, w_proj_c, w_out)
    depth = np.asarray(norm_g).shape[0]
    layers = [{n: np.asarray(v, np.float32)[l] for n, v in zip(names, vals)} for l in range(depth)]
    out = run_model(x[0], layers, np.asarray(rel_bias, np.float32), np.asarray(final_g, np.float32), D_, S_)
    return out[None].astype(np.float32)
```
